# Optimizing a Trainium2 kernel written in Bass

```python
import jax, jax.numpy as jnp
from jax import lax
import numpy as np

D_MODEL = 2048
BATCH = 2
SEQ = 8192
DEPTH = 2

N_MIXERS = 2
HEAD_DIM = 128
EPS = 1e-6
FOX_HEADS = D_MODEL // HEAD_DIM
FOX_BLOCK = 128
DIL_PATTERNS = ((128, 1), (512, 4), (2048, 16))
N_GROUPS = len(DIL_PATTERNS)
DIL_SPAN = 128
DIL_HEADS = D_MODEL // (2 * HEAD_DIM)
DIL_V_DIM = D_MODEL // DIL_HEADS
ALIBI_MAX_EXP = 8.0
D_FF = 4 * D_MODEL
N_FOX_LAYERS = (DEPTH + 1) // 2
N_DIL_LAYERS = DEPTH // 2

kernel_name = "fox_dilated_hybrid_trunk"


def rms_norm(x, g):
    xf = x.astype(jnp.float32)
    y = xf * lax.rsqrt(jnp.mean(xf * xf, axis=-1, keepdims=True) + EPS)
    return (y * g.astype(jnp.float32)).astype(x.dtype)


def sq_relu_mlp(h, w_up, w_down):
    a = jax.nn.relu(h @ w_up)
    return (a * a) @ w_down


def fox_attention(h, w_in, b_f, q_gain, k_gain, w_out):
    B, S, _ = h.shape
    H, dh = FOX_HEADS, HEAD_DIM
    proj = h @ w_in
    q, k, v, f = jnp.split(proj, [H * dh, 2 * H * dh, 3 * H * dh], axis=-1)
    q = rms_norm(q.reshape(B, S, H, dh), q_gain)
    k = rms_norm(k.reshape(B, S, H, dh), k_gain)
    v = v.reshape(B, S, H, dh)
    log_f = jax.nn.log_sigmoid((f + b_f).astype(jnp.float32))
    c = jnp.cumsum(log_f, axis=1)
    c_keys = jnp.transpose(c, (0, 2, 1))
    nb = S // FOX_BLOCK
    qb = jnp.moveaxis(q.reshape(B, nb, FOX_BLOCK, H, dh), 1, 0)
    cb = jnp.moveaxis(c.reshape(B, nb, FOX_BLOCK, H), 1, 0)
    kpos = jnp.arange(S)
    scale = dh ** -0.5

    def one_block(args):
        i, q_i, c_i = args
        s = jnp.einsum('bqhd,bkhd->bhqk', q_i, k, preferred_element_type=jnp.float32) * scale
        s = s + jnp.transpose(c_i, (0, 2, 1))[..., None] - c_keys[:, :, None, :]
        qpos = i * FOX_BLOCK + jnp.arange(FOX_BLOCK)
        causal = kpos[None, :] <= qpos[:, None]
        s = jnp.where(causal, s, -jnp.inf)
        p = jax.nn.softmax(s, axis=-1)
        return jnp.einsum('bhqk,bkhd->bqhd', p.astype(v.dtype), v)

    o = lax.map(one_block, (jnp.arange(nb), qb, cb))
    o = jnp.moveaxis(o, 0, 1).reshape(B, S, H * dh)
    return o @ w_out


def dilated_group(q, k, v, slopes, r):
    B, S, H, dh = q.shape
    L = S // r
    nb = -(-L // DIL_SPAN)
    Lp = nb * DIL_SPAN

    def to_blocks(t):
        t = t.reshape((B, L, r) + t.shape[2:])
        t = jnp.moveaxis(t, 2, 1)
        t = jnp.pad(t, [(0, 0), (0, 0), (0, Lp - L)] + [(0, 0)] * (t.ndim - 3))
        return t.reshape((B, r, nb, DIL_SPAN) + t.shape[3:])

    def with_prev(t):
        prev = jnp.pad(t, [(0, 0), (0, 0), (1, 0)] + [(0, 0)] * (t.ndim - 3))[:, :, :-1]
        return jnp.concatenate([prev, t], axis=3)

    def from_blocks(t):
        t = t.reshape((B, r, Lp) + t.shape[4:])[:, :, :L]
        t = jnp.moveaxis(t, 1, 2)
        return t.reshape((B, S) + t.shape[3:])

    qb = to_blocks(q)
    kw = with_prev(to_blocks(k))
    vw = with_prev(to_blocks(v))
    s = jnp.einsum('brnqhd,brnkhd->brnhqk', qb, kw, preferred_element_type=jnp.float32) * dh ** -0.5
    qi = jnp.arange(DIL_SPAN)[:, None]
    kj = jnp.arange(2 * DIL_SPAN)[None, :]
    delta = qi + DIL_SPAN - kj
    blk = jnp.arange(nb)[:, None, None]
    valid = (delta >= 0) & (delta <= DIL_SPAN) & ((blk > 0) | (kj >= DIL_SPAN))
    alibi = -slopes.astype(jnp.float32)[:, None, None] * (delta * r).astype(jnp.float32)
    s = jnp.where(valid[None, None, :, None], s + alibi, -jnp.inf)
    m = jnp.max(s, axis=-1)
    p = jnp.exp(s - m[..., None])
    den = jnp.sum(p, axis=-1)
    num = jnp.einsum('brnhqk,brnkhd->brnqhd', p, vw.astype(jnp.float32))
    return (from_blocks(jnp.swapaxes(m, 3, 4)),
            from_blocks(jnp.swapaxes(den, 3, 4)),
            from_blocks(num))


def dilated_attention(h, w_in, q_gain, k_gain, w_out):
    B, S, _ = h.shape
    G, H, dh, dv = N_GROUPS, DIL_HEADS, HEAD_DIM, DIL_V_DIM
    proj = h @ w_in
    q, k, v = jnp.split(proj, [G * H * dh, 2 * G * H * dh], axis=-1)
    q = rms_norm(q.reshape(B, S, G, H, dh), q_gain[:, None, :])
    k = rms_norm(k.reshape(B, S, G, H, dh), k_gain[:, None, :])
    v = v.reshape(B, S, H, dv)
    slopes = jnp.exp2(-ALIBI_MAX_EXP * jnp.arange(1, G * H + 1, dtype=jnp.float32) / (G * H)).reshape(G, H)
    ms, dens, nums = [], [], []
    for g, (window, r) in enumerate(DIL_PATTERNS):
        m_g, den_g, num_g = dilated_group(q[:, :, g], k[:, :, g], v, slopes[g], r)
        ms.append(m_g); dens.append(den_g); nums.append(num_g)
    m_all = jnp.stack(ms, 0)
    w = jnp.exp(m_all - jnp.max(m_all, axis=0, keepdims=True))
    den = jnp.sum(w * jnp.stack(dens, 0), axis=0)
    num = jnp.sum(w[..., None] * jnp.stack(nums, 0), axis=0)
    o = (num / den[..., None]).reshape(B, S, H * dv).astype(h.dtype)
    return o @ w_out


def setup_inputs(seed: int = 0) -> dict:
    key = jax.random.key(seed)
    ks = jax.random.split(key, 16)
    D = D_MODEL
    nrm = lambda k, shape, fan_in: jax.random.normal(k, shape, jnp.float32) * fan_in ** -0.5
    x = jax.random.normal(ks[0], (BATCH, SEQ, D), jnp.float32)
    fox_qkv = nrm(ks[1], (N_FOX_LAYERS, D, 3 * FOX_HEADS * HEAD_DIM), D)
    fox_fg = 0.1 * nrm(ks[2], (N_FOX_LAYERS, D, FOX_HEADS), D)
    fox_w_in = jnp.concatenate([fox_qkv, fox_fg], axis=-1)
    fox_b_f = 3.0 + 0.1 * jax.random.normal(ks[3], (N_FOX_LAYERS, FOX_HEADS), jnp.float32)
    fox_q_gain = 1.0 + 0.02 * jax.random.normal(ks[4], (N_FOX_LAYERS, HEAD_DIM), jnp.float32)
    fox_k_gain = 1.0 + 0.02 * jax.random.normal(ks[5], (N_FOX_LAYERS, HEAD_DIM), jnp.float32)
    fox_w_out = nrm(ks[6], (N_FOX_LAYERS, FOX_HEADS * HEAD_DIM, D), FOX_HEADS * HEAD_DIM)
    dil_cols = 2 * N_GROUPS * DIL_HEADS * HEAD_DIM + DIL_HEADS * DIL_V_DIM
    dil_w_in = nrm(ks[7], (N_DIL_LAYERS, D, dil_cols), D)
    dil_q_gain = 1.0 + 0.02 * jax.random.normal(ks[8], (N_DIL_LAYERS, N_GROUPS, HEAD_DIM), jnp.float32)
    dil_k_gain = 1.0 + 0.02 * jax.random.normal(ks[9], (N_DIL_LAYERS, N_GROUPS, HEAD_DIM), jnp.float32)
    dil_w_out = nrm(ks[10], (N_DIL_LAYERS, DIL_HEADS * DIL_V_DIM, D), DIL_HEADS * DIL_V_DIM)
    mix_norm_g = 1.0 + 0.02 * jax.random.normal(ks[11], (DEPTH, D), jnp.float32)
    mlp_norm_g = 1.0 + 0.02 * jax.random.normal(ks[12], (DEPTH, D), jnp.float32)
    mlp_w_up = nrm(ks[13], (DEPTH, D, D_FF), D)
    mlp_w_down = nrm(ks[14], (DEPTH, D_FF, D), D_FF)
    return {"x": x, "fox_w_in": fox_w_in, "fox_b_f": fox_b_f, "fox_q_gain": fox_q_gain,
            "fox_k_gain": fox_k_gain, "fox_w_out": fox_w_out, "dil_w_in": dil_w_in,
            "dil_q_gain": dil_q_gain, "dil_k_gain": dil_k_gain, "dil_w_out": dil_w_out,
            "mix_norm_g": mix_norm_g, "mlp_norm_g": mlp_norm_g,
            "mlp_w_up": mlp_w_up, "mlp_w_down": mlp_w_down}


def reference(x, fox_w_in, fox_b_f, fox_q_gain, fox_k_gain, fox_w_out, dil_w_in,
              dil_q_gain, dil_k_gain, dil_w_out, mix_norm_g, mlp_norm_g, mlp_w_up, mlp_w_down):
    for i in range(DEPTH):
        j = i // N_MIXERS
        h = rms_norm(x, mix_norm_g[i])
        if i % N_MIXERS == 0:
            mix = fox_attention(h, fox_w_in[j], fox_b_f[j], fox_q_gain[j], fox_k_gain[j], fox_w_out[j])
        else:
            mix = dilated_attention(h, dil_w_in[j], dil_q_gain[j], dil_k_gain[j], dil_w_out[j])
        x = x + mix.astype(x.dtype)
        h = rms_norm(x, mlp_norm_g[i])
        x = x + sq_relu_mlp(h, mlp_w_up[i], mlp_w_down[i]).astype(x.dtype)
    return x
```

```python
import numpy as np
from contextlib import ExitStack
import ml_dtypes
import concourse.bass as bass
import concourse.mybir as mybir
from concourse.bass_utils import run_bass_kernel_spmd

F32 = mybir.dt.float32
BF16 = mybir.dt.bfloat16
AF = mybir.ActivationFunctionType
ALU = mybir.AluOpType

NCORES = 8
D = 2048
S_LEN = 8192
NTOK = 16384
TPC = NTOK // NCORES
EPS = 1e-6
NEG = -30000.0

ENGS = ("pe", "act", "dve", "pool", "sp")


class Tile:
    __slots__ = ("name", "last_w", "readers", "dsem", "dcnt")

    def __init__(self, name):
        self.name = name
        self.last_w = None
        self.readers = []
        self.dsem = None
        self.dcnt = 0


class Op:
    __slots__ = ("idx", "eng", "fn", "deps", "dma", "dsem", "dval", "has_dep", "inc", "waits")

    def __init__(self, idx, eng, fn, dma):
        self.idx = idx
        self.eng = eng
        self.fn = fn
        self.deps = set()
        self.dma = dma
        self.dsem = None
        self.dval = 0
        self.has_dep = False
        self.inc = 0
        self.waits = []


class Sched:
    def __init__(self, nc, es):
        self.nc = nc
        self.es = es
        self.ops = []
        self.sem = {e: es.enter_context(nc.semaphore("sem_" + e)) for e in ENGS}
        self.final_dma = []
        self.ntile = 0

    def tile(self, name="t"):
        self.ntile += 1
        return Tile(f"{name}_{self.ntile}")

    def tiles(self, name, n):
        return [self.tile(name) for _ in range(n)]

    def _dma_sem(self, t):
        if t.dsem is None:
            t.dsem = self.es.enter_context(self.nc.semaphore("d_" + t.name))
        return t.dsem

    def op(self, eng, fn, reads=(), writes=(), dma=False, dma_tile=None, final=False):
        o = Op(len(self.ops), eng, fn, dma)
        for t in reads:
            if t.last_w is not None:
                o.deps.add(t.last_w)
        for t in writes:
            if t.last_w is not None:
                o.deps.add(t.last_w)
            o.deps.update(t.readers)
        for t in reads:
            t.readers.append(o.idx)
        for t in writes:
            t.last_w = o.idx
            t.readers = []
        if dma:
            t = dma_tile if dma_tile is not None else (writes[0] if writes else reads[0])
            o.dsem = self._dma_sem(t)
            t.dcnt += 16
            o.dval = t.dcnt
            if final:
                self.final_dma.append(o)
        self.ops.append(o)
        return o

    def finalize(self):
        ops = self.ops
        for o in ops:
            best = {}
            keep = []
            for d in o.deps:
                p = ops[d]
                if p.dma:
                    keep.append(d)
                    continue
                if p.eng == "pe" and o.eng == "pe":
                    continue
                if p.eng not in best or best[p.eng] < d:
                    best[p.eng] = d
            o.deps = keep + list(best.values())
            for d in o.deps:
                ops[d].has_dep = True
        cnt = {e: 0 for e in ENGS}
        for o in ops:
            if not o.dma and o.has_dep:
                cnt[o.eng] += 1
                o.inc = cnt[o.eng]
        known = {e: {} for e in ENGS}
        for o in ops:
            w = {}
            for d in o.deps:
                p = ops[d]
                if p.dma:
                    key = ("d", id(p.dsem))
                    sem, val = p.dsem, p.dval
                else:
                    key = ("e", p.eng)
                    sem, val = self.sem[p.eng], p.inc
                if known[o.eng].get(key, 0) >= val:
                    continue
                if key not in w or w[key][1] < val:
                    w[key] = (sem, val)
            for key, (sem, val) in w.items():
                known[o.eng][key] = val
            o.waits = list(w.values())

    def emit(self, block):
        per = {e: [o for o in self.ops if o.eng == e] for e in ENGS}
        finals = self.final_dma
        sems = self.sem

        def run(h, lst):
            for o in lst:
                for sem, val in o.waits:
                    h.wait_ge(sem, val)
                ins = o.fn(h)
                if o.dma:
                    ins.then_inc(o.dsem, 16)
                elif o.inc:
                    ins.then_inc(sems[o.eng], 1)

        @block.tensor
        def _(e):
            run(e, per["pe"])

        @block.scalar
        def _(e):
            run(e, per["act"])

        @block.vector
        def _(e):
            run(e, per["dve"])

        @block.gpsimd
        def _(e):
            run(e, per["pool"])

        @block.sync
        def _(e):
            run(e, per["sp"])
            for o in finals:
                e.wait_ge(o.dsem, o.dval)


class Ctx:
    def __init__(self, nc, es):
        self.nc = nc
        self.es = es
        self.S = Sched(nc, es)
        self.nalloc = 0

    def sb(self, shape, dt, name="sb"):
        self.nalloc += 1
        return self.es.enter_context(self.nc.sbuf_tensor(f"{name}{self.nalloc}", list(shape), dt))

    def ps(self, shape, dt=F32, name="ps"):
        self.nalloc += 1
        return self.es.enter_context(self.nc.psum_tensor(f"{name}{self.nalloc}", list(shape), dt))

    def dram_in(self, name, shape, dt):
        return self.nc.dram_tensor(name, list(shape), dt, kind="ExternalInput").ap()

    def dram_out(self, name, shape, dt):
        return self.nc.dram_tensor(name, list(shape), dt, kind="ExternalOutput").ap()

    def consts(self):
        S = self.S
        self.ones_bf = self.sb([128, 128], BF16, "ones_bf")
        self.ones_f = self.sb([128, 128], F32, "ones_f")
        self.t_ones_bf = S.tile("ones_bf")
        self.t_ones_f = S.tile("ones_f")
        ob, of = self.ones_bf, self.ones_f
        S.op("dve", lambda e: e.memset(ob[:], 1.0), writes=[self.t_ones_bf])
        S.op("dve", lambda e: e.memset(of[:], 1.0), writes=[self.t_ones_f])


def load_small(cx, dram_ap, shape, dt=F32, name="c", q="sp"):
    t = cx.sb(shape, dt, name)
    tl = cx.S.tile(name)
    cx.S.op(q, lambda e: e.dma_start(out=t[:], in_=dram_ap), writes=[tl], dma=True)
    return t, tl


def load_cast_weight(cx, dram_ap, shape, name):
    t = cx.sb(shape, BF16, name)
    tl = cx.S.tile(name)
    cx.S.op("pool", lambda e: e.dma_start(out=t[:], in_=dram_ap), writes=[tl], dma=True)
    return t, tl


def rstd_from_sumsq(cx, ss_ps, t_ss, n, inv_count, lnb, t_ln, rstd, t_rstd):
    S = cx.S
    eps_t = cx.eps_t
    S.op("act", lambda e: e.activation(out=lnb[:, :n], in_=ss_ps[:, :n], func=AF.Ln, bias=eps_t[:, 0:1], scale=inv_count),
         reads=[t_ss, cx.t_eps], writes=[t_ln])
    S.op("act", lambda e: e.activation(out=rstd[:, :n], in_=lnb[:, :n], func=AF.Exp, scale=-0.5),
         reads=[t_ln], writes=[t_rstd])


def make_eps(cx):
    cx.eps_t = cx.sb([128, 1], F32, "eps")
    cx.t_eps = cx.S.tile("eps")
    et = cx.eps_t
    cx.S.op("dve", lambda e: e.memset(et[:], EPS), writes=[cx.t_eps])


def norm_tile(cx, xt, t_xt, g_sb, t_g, hT, t_hT, n, sqb, t_sq, ss_ps, t_ss, lnb, t_ln, rstd, t_rstd):
    S = cx.S
    S.op("act", lambda e: e.activation(out=sqb[:, :, :n], in_=xt[:, :, :n], func=AF.Square),
         reads=t_xt, writes=[t_sq])
    ob = cx.ones_bf
    for dc in range(16):
        S.op("pe", lambda e, dc=dc: e.matmul(ss_ps[:, :n], ob[:], sqb[:, dc, :n], start=(dc == 0), stop=(dc == 15)),
             reads=[t_sq, cx.t_ones_bf], writes=[t_ss])
    rstd_from_sumsq(cx, ss_ps, t_ss, n, 1.0 / D, lnb, t_ln, rstd, t_rstd)
    for dc in range(16):
        S.op("dve", lambda e, dc=dc: e.scalar_tensor_tensor(out=hT[:, dc, :n], in0=xt[:, dc, :n], scalar=g_sb[:, dc:dc + 1],
                                                             in1=rstd[:, :n], op0=ALU.mult, op1=ALU.mult),
             reads=[t_xt[dc], t_g, t_rstd], writes=[t_hT])


def build_mlp(emit_h):
    nc = bass.Bass("TRN2", target_bir_lowering=False)
    with ExitStack() as es:
        cx = Ctx(nc, es)
        S = cx.S
        TT = 512
        NT = TPC // TT
        xT = cx.dram_in("xT", [16, 128, TPC], F32)
        oT = cx.dram_in("oT", [16, 128, TPC], BF16)
        wout = cx.dram_in("wout", [4, 128, 4, 2048], F32)
        wup = cx.dram_in("wup", [16, 128, 4, 2048], F32)
        wdn = cx.dram_in("wdn", [16, 128, 4, 2048], F32)
        g_mlp = cx.dram_in("g_mlp", [128, 16], F32)
        x1T = cx.dram_out("x1T", [16, 128, TPC], F32)
        if emit_h:
            g_nxt = cx.dram_in("g_nxt", [128, 16], F32)
            h1T = cx.dram_out("h1T", [16, 128, TPC], BF16)
        cx.consts()
        make_eps(cx)
        g_sb, t_g = load_small(cx, g_mlp, [128, 16], F32, "g_mlp")
        if emit_h:
            gn_sb, t_gn = load_small(cx, g_nxt, [128, 16], F32, "g_nxt")

        xt = cx.sb([128, 16, TT], F32, "xt")
        t_xt = S.tiles("xt", 16)
        t_xt_ld = S.tile("xt_ld")
        ot = cx.sb([128, 16, TT], BF16, "ot")
        t_ot = S.tile("ot")
        hT = cx.sb([128, 16, TT], BF16, "hT")
        t_hT = S.tile("hT")
        aT = cx.sb([128, 64, TT], BF16, "aT")
        t_aT = S.tiles("aT", 64)
        sqb = cx.sb([128, 16, TT], BF16, "sqb")
        t_sq = S.tile("sq")
        lnb = cx.sb([128, TT], F32, "lnb")
        t_ln = S.tile("ln")
        rstd = cx.sb([128, TT], F32, "rstd")
        t_rstd = S.tile("rstd")
        r32 = [cx.sb([128, TT], F32, "r32") for _ in range(2)]
        t_r32 = S.tiles("r32", 2)
        NW = 3
        wslot = [cx.sb([128, 4, 2048], BF16, "wslot") for _ in range(NW)]
        t_w = S.tiles("w", NW)
        pbank = [cx.ps([128, 512], F32, "pb") for _ in range(5)]
        t_pb = S.tiles("pb", 5)
        ss_ps = cx.ps([128, 512], F32, "ss")
        t_ss = S.tile("ss")
        wctr = [0]
        pctr = [0]

        def wload(src):
            i = wctr[0] % NW
            wctr[0] += 1
            S.op("pool", lambda e: e.dma_start(out=wslot[i][:], in_=src), writes=[t_w[i]], dma=True)
            return wslot[i], t_w[i]

        def nextbank():
            i = pctr[0] % 5
            pctr[0] += 1
            return pbank[i], t_pb[i]

        for tt in range(NT):
            t0 = tt * TT
            S.op("sp", lambda e, t0=t0: e.dma_start(out=ot[:], in_=oT[:, :, t0:t0 + TT].rearrange("c p t -> p c t")),
                 writes=[t_ot], dma=True)
            S.op("sp", lambda e, t0=t0: e.dma_start(out=xt[:], in_=xT[:, :, t0:t0 + TT].rearrange("c p t -> p c t")),
                 writes=t_xt, dma=True, dma_tile=t_xt_ld)
            for grp in range(4):
                ws, tw = wload(wout[grp])
                for dc4 in range(4):
                    dc = grp * 4 + dc4
                    pb, tpb = nextbank()
                    for kc in range(16):
                        S.op("pe", lambda e, ws=ws, pb=pb, dc4=dc4, kc=kc: e.matmul(
                            pb[:], ws[:, dc4, kc * 128:(kc + 1) * 128], ot[:, kc, :], start=(kc == 0), stop=(kc == 15)),
                            reads=[tw, t_ot], writes=[tpb])
                    S.op("dve", lambda e, pb=pb, dc=dc: e.tensor_tensor(out=xt[:, dc, :], in0=pb[:], in1=xt[:, dc, :], op=ALU.add),
                         reads=[tpb, t_xt[dc]], writes=[t_xt[dc]])
            norm_tile(cx, xt, t_xt, g_sb, t_g, hT, t_hT, TT, sqb, t_sq, ss_ps, t_ss, lnb, t_ln, rstd, t_rstd)
            for grp in range(16):
                ws, tw = wload(wup[grp])
                for fc4 in range(4):
                    fc = grp * 4 + fc4
                    pb, tpb = nextbank()
                    for dc in range(16):
                        S.op("pe", lambda e, ws=ws, pb=pb, fc4=fc4, dc=dc: e.matmul(
                            pb[:], ws[:, fc4, dc * 128:(dc + 1) * 128], hT[:, dc, :], start=(dc == 0), stop=(dc == 15)),
                            reads=[tw, t_hT], writes=[tpb])
                    rb = r32[fc % 2]
                    trb = t_r32[fc % 2]
                    S.op("act", lambda e, pb=pb, rb=rb: e.activation(out=rb[:], in_=pb[:], func=AF.Relu),
                         reads=[tpb], writes=[trb])
                    S.op("dve", lambda e, pb=pb, rb=rb, fc=fc: e.scalar_tensor_tensor(
                        out=aT[:, fc, :], in0=pb[:], scalar=0.0, in1=rb[:], op0=ALU.max, op1=ALU.mult),
                        reads=[tpb, trb], writes=[t_aT[fc]])
            for dc in range(16):
                ws, tw = wload(wdn[dc])
                pb, tpb = nextbank()
                for fc in range(64):
                    S.op("pe", lambda e, ws=ws, pb=pb, fc=fc: e.matmul(
                        pb[:], ws[:, fc // 16, (fc % 16) * 128:(fc % 16 + 1) * 128], aT[:, fc, :], start=(fc == 0), stop=(fc == 63)),
                        reads=[tw, t_aT[fc]], writes=[tpb])
                S.op("dve", lambda e, pb=pb, dc=dc: e.tensor_tensor(out=xt[:, dc, :], in0=pb[:], in1=xt[:, dc, :], op=ALU.add),
                     reads=[tpb, t_xt[dc]], writes=[t_xt[dc]])
            S.op("sp", lambda e, t0=t0: e.dma_start(out=x1T[:, :, t0:t0 + TT].rearrange("c p t -> p c t"), in_=xt[:]),
                 reads=t_xt, dma=True, dma_tile=S.tile("x1st"), final=True)
            if emit_h:
                norm_tile(cx, xt, t_xt, gn_sb, t_gn, hT, t_hT, TT, sqb, t_sq, ss_ps, t_ss, lnb, t_ln, rstd, t_rstd)
                S.op("sp", lambda e, t0=t0: e.dma_start(out=h1T[:, :, t0:t0 + TT].rearrange("c p t -> p c t"), in_=hT[:]),
                     reads=[t_hT], dma=True, dma_tile=S.tile("h1st"), final=True)
        S.finalize()
        with nc.Block() as block:
            S.emit(block)
    return nc


def qk_proj(cx, W, t_W, col0, hT, t_hT, n, pb, tpb, ss2, t_ss2, q32, t_q32, sq32, t_sq32, lnb, t_ln, r2, t_r2,
            gain, t_gain, dst_ap, t_dst):
    S = cx.S
    for dc in range(16):
        S.op("pe", lambda e, dc=dc: e.matmul(pb[:, :n], W[:, dc, col0:col0 + 128], hT[:, dc, :n], start=(dc == 0), stop=(dc == 15)),
             reads=[t_W, t_hT], writes=[tpb])
    S.op("act", lambda e: e.activation(out=q32[:, :n], in_=pb[:, :n], func=AF.Copy), reads=[tpb], writes=[t_q32])
    S.op("dve", lambda e: e.tensor_tensor(out=sq32[:, :n], in0=q32[:, :n], in1=q32[:, :n], op=ALU.mult),
         reads=[t_q32], writes=[t_sq32])
    ofb = cx.ones_bf
    S.op("pe", lambda e: e.matmul(ss2[:, :n], ofb[:], sq32[:, :n], start=True, stop=True),
         reads=[t_sq32, cx.t_ones_bf], writes=[t_ss2])
    rstd_from_sumsq(cx, ss2, t_ss2, n, 1.0 / 128, lnb, t_ln, r2, t_r2)
    S.op("dve", lambda e: e.scalar_tensor_tensor(out=dst_ap, in0=q32[:, :n], scalar=gain, in1=r2[:, :n],
                                                 op0=ALU.mult, op1=ALU.mult),
         reads=[t_q32, t_r2, t_gain], writes=[t_dst])


def build_fox(SL=S_LEN, NB=2, stop=9):
    nc = bass.Bass("TRN2", target_bir_lowering=False)
    with ExitStack() as es:
        cx = Ctx(nc, es)
        S = cx.S
        TA = 256
        NTK = SL * NB
        NJ = SL // 128
        NQB = SL // 512
        xT = cx.dram_in("xT", [NTK // TA, 128, 16 * TA], F32)
        wq_d = cx.dram_in("wq", [128, 2, 2048], F32)
        wk_d = cx.dram_in("wk", [128, 2, 2048], F32)
        wv_d = cx.dram_in("wv", [128, 16, 256], F32)
        wf_d = cx.dram_in("wf", [128, 16, 4], F32)
        gmix_d = cx.dram_in("gmix", [128, 16], F32)
        qg_d = cx.dram_in("qg", [128, 1], F32)
        kg_d = cx.dram_in("kg", [128, 1], F32)
        bf_d = cx.dram_in("bfb", [128, 4], F32)
        oT = cx.dram_out("oT", [2, 128, NTK], BF16)
        cx.consts()
        make_eps(cx)
        g_sb, t_g = load_small(cx, gmix_d, [128, 16], F32, "gmix")
        qg, t_qg0 = load_small(cx, qg_d, [128, 1], F32, "qg")
        kg, t_kg = load_small(cx, kg_d, [128, 1], F32, "kg")
        bfb, t_bfb = load_small(cx, bf_d, [128, 4], F32, "bfb")
        qgs = cx.sb([128, 1], F32, "qgs")
        t_qg = S.tile("qgs")
        S.op("dve", lambda e: e.tensor_scalar(out=qgs[:], in0=qg[:], scalar1=float(128 ** -0.5), scalar2=None, op0=ALU.mult),
             reads=[t_qg0], writes=[t_qg])
        wq_f, t_wq = load_cast_weight(cx, wq_d, [128, 2, 2048], "wq")
        wk_f, t_wk = load_cast_weight(cx, wk_d, [128, 2, 2048], "wk")
        wvf_f = cx.sb([128, 16, 260], BF16, "wvf")
        t_wvf = S.tile("wvf")
        wf32 = cx.sb([128, 16, 4], F32, "wf32")
        t_wf32 = S.tile("wf32")
        S.op("pool", lambda e: e.dma_start(out=wvf_f[:, :, 0:256], in_=wv_d), writes=[t_wvf], dma=True)
        S.op("sp", lambda e: e.dma_start(out=wf32[:], in_=wf_d), writes=[t_wf32], dma=True)
        S.op("dve", lambda e: e.tensor_copy(out=wvf_f[:, :, 256:260], in_=wf32[:]), reads=[t_wf32, t_wvf], writes=[t_wvf])
        wq = wq_f[:].rearrange("p a (c n) -> p (a c) n", n=256)
        wk = wk_f[:].rearrange("p a (c n) -> p (a c) n", n=256)
        wvf = wvf_f[:]
        tri = cx.sb([128, 128], BF16, "tri")
        t_tri = S.tile("tri")
        of = cx.ones_f
        obf_ = cx.ones_bf
        S.op("pool", lambda e: e.affine_select(out=tri[:], in_=obf_[:], pattern=[[1, 128]], compare_op=ALU.is_ge, fill=0.0,
                                               base=0, channel_multiplier=-1),
             reads=[cx.t_ones_bf], writes=[t_tri])

        if stop == 0:
            S.finalize()
            with nc.Block() as block:
                S.emit(block)
            return nc
        xt = [cx.sb([128, 16, TA], F32, "xt") for _ in range(2)]
        t_xt = [S.tiles("xt", 1) * 16 for _ in range(2)]
        hT = [cx.sb([128, 16, TA], BF16, "hT") for _ in range(2)]
        t_hT = S.tiles("hT", 2)
        sqb = cx.sb([128, 16, TA], BF16, "sqb")
        t_sq = S.tile("sq")
        lnb = [cx.sb([128, TA], F32, "lnb") for _ in range(2)]
        t_ln = S.tiles("ln", 2)
        rstd = cx.sb([128, TA], F32, "rstd")
        t_rstd = S.tile("rstd")
        NQ = 3
        q32 = [cx.sb([128, TA], F32, "q32") for _ in range(NQ)]
        t_q32 = S.tiles("q32", NQ)
        sq32 = [cx.sb([128, TA], BF16, "sq32") for _ in range(NQ)]
        t_sq32 = S.tiles("sq32", NQ)
        r2 = [cx.sb([128, TA], F32, "r2") for _ in range(NQ)]
        t_r2 = S.tiles("r2", NQ)
        QT = [cx.sb([128, SL], BF16, "QT") for _ in range(2)]
        KT = [cx.sb([128, SL], BF16, "KT") for _ in range(2)]
        V = cx.sb([128, 2, NJ, 128], BF16, "V")
        t_QT = S.tiles("QT", 2)
        t_KT = S.tiles("KT", 2)
        t_V = S.tile("V")
        Z = cx.sb([128, NJ, 4], F32, "Z")
        t_Z = S.tile("Z")
        E = cx.sb([128, NJ, 2], F32, "E")
        t_E = S.tile("E")
        SP = cx.sb([128, 2, NJ], F32, "SP")
        t_SP = S.tile("SP")
        SPp = [cx.sb([128, 2, NJ], BF16, "SPp") for _ in range(3)]
        t_SPp = S.tile("SPp")
        SPr = cx.sb([128, 2, NJ], F32, "SPr")
        t_SPr = S.tile("SPr")
        tot = cx.sb([128, NJ], F32, "tot")
        t_tot = S.tile("tot")
        cum = cx.sb([128, NJ], F32, "cum")
        t_cum = S.tile("cum")
        excl = cx.sb([128, NJ], F32, "excl")
        t_excl = S.tile("excl")
        negc = cx.sb([128, NJ], F32, "negc")
        t_negc = S.tile("negc")
        bias = cx.sb([128, NQB, NJ], F32, "bias")
        t_bias = S.tile("bias")
        NP = 4
        pT = [cx.sb([128, 512], BF16, "pT") for _ in range(NP)]
        t_pT = S.tiles("pT", NP)
        rden = cx.sb([128, 512], F32, "rden")
        t_rden = S.tile("rden")
        ob = [cx.sb([128, 512], BF16, "ob") for _ in range(2)]
        t_ob = S.tiles("ob", 2)
        banks = [cx.ps([128, 512], F32, "bk") for _ in range(8)]
        t_bk = S.tiles("bk", 8)
        obf = cx.ones_bf
        NTI = SL // TA

        for b in range(NB):
            pa = [0]
            qc = [0]

            def load_x(ti):
                tok0 = b * SL + ti * TA
                xtb = xt[ti % 2]
                gt = tok0 // TA
                S.op("sp", lambda e, xtb=xtb, gt=gt: e.dma_start(out=xtb[:].rearrange("p c t -> p (c t)"), in_=xT[gt]),
                     writes=[t_xt[ti % 2][0]], dma=True)

            def norm_p1(ti):
                xtb = xt[ti % 2]
                S.op("act", lambda e: e.activation(out=sqb[:], in_=xtb[:], func=AF.Square), reads=t_xt[ti % 2][:1], writes=[t_sq])

            def norm_p2(ti):
                for dc in range(16):
                    S.op("pe", lambda e, dc=dc: e.matmul(banks[7][:, :TA], obf[:], sqb[:, dc, :], start=(dc == 0), stop=(dc == 15)),
                         reads=[t_sq, cx.t_ones_bf], writes=[t_bk[7]])
                rstd_from_sumsq(cx, banks[7], t_bk[7], TA, 1.0 / D, lnb[0], t_ln[0], rstd, t_rstd)

            def norm_p3(ti):
                xtb, hTb = xt[ti % 2], hT[ti % 2]
                for dc in range(16):
                    S.op("dve", lambda e, dc=dc: e.scalar_tensor_tensor(out=hTb[:, dc, :], in0=xtb[:, dc, :], scalar=g_sb[:, dc:dc + 1],
                                                                         in1=rstd[:], op0=ALU.mult, op1=ALU.mult),
                         reads=[t_xt[ti % 2][0], t_g, t_rstd], writes=[t_hT[ti % 2]])

            combos = [(wq, t_wq, QT, t_QT, qgs, t_qg, 0), (wq, t_wq, QT, t_QT, qgs, t_qg, 1),
                      (wk, t_wk, KT, t_KT, kg, t_kg, 0), (wk, t_wk, KT, t_KT, kg, t_kg, 1)]

            def part1(ti, k):
                W, t_W, dstl, t_dstl, gain, t_gain, hd = combos[k]
                hTb = hT[ti % 2]
                pbi = pa[0] % 5
                pa[0] += 1
                qi_ = qc[0] % NQ
                qc[0] += 1
                pb, tpb = banks[pbi], t_bk[pbi]
                for dc in range(16):
                    S.op("pe", lambda e, dc=dc: e.matmul(pb[:, :TA], W[:, dc, hd * 128:hd * 128 + 128], hTb[:, dc, :], start=(dc == 0), stop=(dc == 15)),
                         reads=[t_W, t_hT[ti % 2]], writes=[tpb])
                S.op("act", lambda e: e.activation(out=q32[qi_][:], in_=pb[:, :TA], func=AF.Copy), reads=[tpb], writes=[t_q32[qi_]])
                S.op("dve", lambda e: e.tensor_tensor(out=sq32[qi_][:], in0=q32[qi_][:], in1=q32[qi_][:], op=ALU.mult),
                     reads=[t_q32[qi_]], writes=[t_sq32[qi_]])
                return qi_

            def part2(ti, k, qi_):
                W, t_W, dstl, t_dstl, gain, t_gain, hd = combos[k]
                sb_ = 5 + (qi_ % 2)
                S.op("pe", lambda e: e.matmul(banks[sb_][:, :TA], obf[:], sq32[qi_][:], start=True, stop=True),
                     reads=[t_sq32[qi_], cx.t_ones_bf], writes=[t_bk[sb_]])
                rstd_from_sumsq(cx, banks[sb_], t_bk[sb_], TA, 1.0 / 128, lnb[1], t_ln[1], r2[qi_], t_r2[qi_])
                S.op("dve", lambda e: e.scalar_tensor_tensor(out=dstl[hd][:, ti * TA:(ti + 1) * TA], in0=q32[qi_][:], scalar=gain[:, 0:1],
                                                             in1=r2[qi_][:], op0=ALU.mult, op1=ALU.mult),
                     reads=[t_q32[qi_], t_r2[qi_], t_gain], writes=[t_dstl[hd]])

            def vproj(ti, sub):
                hTb = hT[ti % 2]
                j = ti * (TA // 128) + sub
                pbi = pa[0] % 5
                pa[0] += 1
                pb, tpb = banks[pbi], t_bk[pbi]
                for dc in range(16):
                    S.op("pe", lambda e, dc=dc: e.matmul(pb[:, :260], hTb[:, dc, sub * 128:(sub + 1) * 128], wvf[:, dc, :], start=(dc == 0), stop=(dc == 15)),
                         reads=[t_hT[ti % 2], t_wvf], writes=[tpb])
                for hd_ in range(2):
                    S.op("dve", lambda e, hd_=hd_: e.tensor_copy(out=V[:, hd_, j, :], in_=pb[:, hd_ * 128:(hd_ + 1) * 128]),
                         reads=[tpb], writes=[t_V])
                S.op("dve", lambda e: e.tensor_tensor(out=Z[:, j, :], in0=pb[:, 256:260], in1=bfb[:], op=ALU.add),
                     reads=[tpb, t_bfb], writes=[t_Z])

            load_x(0)
            if NTI > 1:
                load_x(1)
            norm_p1(0)
            norm_p2(0)
            norm_p3(0)
            for ti in range(NTI):
                nxt = ti + 1 < NTI
                if nxt:
                    norm_p1(ti + 1)
                a0 = part1(ti, 0)
                a1 = part1(ti, 1)
                if nxt:
                    norm_p2(ti + 1)
                part2(ti, 0, a0)
                a2 = part1(ti, 2)
                part2(ti, 1, a1)
                a3 = part1(ti, 3)
                if nxt:
                    norm_p3(ti + 1)
                if ti + 2 < NTI:
                    load_x(ti + 2)
                part2(ti, 2, a2)
                vproj(ti, 0)
                part2(ti, 3, a3)
                vproj(ti, 1)
            if stop == 1:
                break
            S.op("act", lambda e: e.activation(out=E[:], in_=Z[:, :, 0:2], func=AF.Exp, scale=-1.0), reads=[t_Z], writes=[t_E])
            S.op("act", lambda e: e.activation(out=SP[:].rearrange("p h j -> p j h"), in_=E[:], func=AF.Ln, bias=1.0, scale=1.0),
                 reads=[t_E], writes=[t_SP])
            S.op("dve", lambda e: e.tensor_copy(out=SPp[0][:], in_=SP[:]), reads=[t_SP], writes=[t_SPp])
            S.op("dve", lambda e: e.tensor_tensor(out=SPr[:], in0=SP[:], in1=SPp[0][:], op=ALU.subtract), reads=[t_SP, t_SPp], writes=[t_SPr])
            S.op("dve", lambda e: e.tensor_copy(out=SPp[1][:], in_=SPr[:]), reads=[t_SPr], writes=[t_SPp])
            S.op("dve", lambda e: e.tensor_tensor(out=SPr[:], in0=SPr[:], in1=SPp[1][:], op=ALU.subtract), reads=[t_SPr, t_SPp], writes=[t_SPr])
            S.op("dve", lambda e: e.tensor_copy(out=SPp[2][:], in_=SPr[:]), reads=[t_SPr], writes=[t_SPp])
            pc = [0]
            sc = [0]
            for hd in range(2):
                for pc_ in range(3):
                    S.op("pe", lambda e, hd=hd, pc_=pc_: e.matmul(banks[0][:, :NJ], tri[:], SPp[pc_][:, hd, :], start=(pc_ == 0), stop=(pc_ == 2)),
                         reads=[t_tri, t_SPp], writes=[t_bk[0]])
                for pc_ in range(3):
                    S.op("pe", lambda e, hd=hd, pc_=pc_: e.matmul(banks[1][:, :NJ], obf_[:], SPp[pc_][:, hd, :], start=(pc_ == 0), stop=(pc_ == 2)),
                         reads=[cx.t_ones_bf, t_SPp], writes=[t_bk[1]])
                S.op("dve", lambda e: e.tensor_copy(out=tot[:], in_=banks[1][:, :NJ]), reads=[t_bk[1]], writes=[t_tot])
                S.op("dve", lambda e: e.tensor_tensor_scan(out=cum[:], data0=of[:, 0:NJ], data1=tot[:], initial=0.0,
                                                           op0=ALU.mult, op1=ALU.add),
                     reads=[t_tot, cx.t_ones_f], writes=[t_cum])
                S.op("dve", lambda e: e.tensor_tensor(out=excl[:], in0=cum[:], in1=tot[:], op=ALU.subtract),
                     reads=[t_cum, t_tot], writes=[t_excl])
                S.op("dve", lambda e: e.tensor_tensor(out=negc[:], in0=banks[0][:, :NJ], in1=excl[:], op=ALU.add),
                     reads=[t_bk[0], t_excl], writes=[t_negc])
                for qb in range(NQB):
                    nj = 4 * qb + 4
                    S.op("dve", lambda e, qb=qb, nj=nj: e.tensor_scalar(
                        out=bias[:, qb, 0:nj], in0=negc[:, 0:nj], scalar1=excl[:, 4 * qb:4 * qb + 1], scalar2=None, op0=ALU.subtract),
                        reads=[t_negc, t_excl], writes=[t_bias])
                if stop == 2:
                    continue
                pairs = []
                for qb in range(NQB):
                    for j in range(4 * qb + 4):
                        pairs.append((qb, j))
                LA = 2
                slots = {}

                def qk(idx):
                    qb, j = pairs[idx]
                    d = max(0, j - 4 * qb)
                    c0 = 128 * d
                    n = 512 - c0
                    si = sc[0] % 3
                    sc[0] += 1
                    slots[idx] = si
                    q0 = qb * 512
                    S.op("pe", lambda e, hd=hd: e.matmul(banks[si][:, :n], KT[hd][:, j * 128:(j + 1) * 128], QT[hd][:, q0 + c0:q0 + 512], start=True, stop=True),
                         reads=[t_KT[hd], t_QT[hd]], writes=[t_bk[si]])

                for idx in range(min(LA, len(pairs))):
                    qk(idx)
                for idx, (qb, j) in enumerate(pairs):
                    q0 = qb * 512
                    oi = qb % 2
                    ops_, t_ops = banks[3 + oi], t_bk[3 + oi]
                    dps, t_dps = banks[5 + oi], t_bk[5 + oi]
                    nj = 4 * qb + 4
                    d = max(0, j - 4 * qb)
                    c0 = 128 * d
                    n = 512 - c0
                    si = slots.pop(idx)
                    sps, t_sps = banks[si], t_bk[si]
                    pi = pc[0] % NP
                    pc[0] += 1
                    p_, t_p = pT[pi], t_pT[pi]
                    S.op("act", lambda e, sps=sps, p_=p_, qb=qb, j=j, n=n: e.activation(
                        out=p_[:, :n], in_=sps[:, :n], func=AF.Exp, bias=bias[:, qb, j:j + 1], scale=1.0),
                        reads=[t_sps, t_bias], writes=[t_p])
                    if j >= 4 * qb:
                        S.op("pool", lambda e, p_=p_: e.affine_select(
                            out=p_[:, 0:128], in_=p_[:, 0:128], pattern=[[1, 128]], compare_op=ALU.is_ge, fill=0.0,
                            base=0, channel_multiplier=-1), reads=[t_p], writes=[t_p])
                    if idx + LA < len(pairs):
                        qk(idx + LA)
                    S.op("pe", lambda e, ops_=ops_, p_=p_, hd=hd, j=j, c0=c0, n=n, nj=nj: e.matmul(
                        ops_[:, c0:512], V[:, hd, j, :], p_[:, :n], start=(j == 0), stop=(j == nj - 1)),
                        reads=[t_V, t_p], writes=[t_ops])
                    S.op("pe", lambda e, dps=dps, p_=p_, j=j, c0=c0, n=n, nj=nj: e.matmul(
                        dps[:, c0:512], obf[:], p_[:, :n], start=(j == 0), stop=(j == nj - 1)),
                        reads=[cx.t_ones_bf, t_p], writes=[t_dps])
                    if j == nj - 1:
                        S.op("dve", lambda e, dps=dps: e.reciprocal(out=rden[:], in_=dps[:]), reads=[t_dps], writes=[t_rden])
                        obb, t_obb = ob[oi], t_ob[oi]
                        S.op("dve", lambda e, ops_=ops_, obb=obb: e.tensor_tensor(out=obb[:], in0=ops_[:], in1=rden[:], op=ALU.mult),
                             reads=[t_ops, t_rden], writes=[t_obb])
                        tk = b * SL + q0
                        S.op("sp", lambda e, obb=obb, hd=hd, tk=tk: e.dma_start(out=oT[hd, :, tk:tk + 512], in_=obb[:]),
                             reads=[t_obb], dma=True, dma_tile=t_obb, final=True)
        S.finalize()
        with nc.Block() as block:
            S.emit(block)
    return nc


DIL_R = (1, 4, 16)


def build_dil(SL=S_LEN, NB=2):
    nc = bass.Bass("TRN2", target_bir_lowering=False)
    with ExitStack() as es:
        cx = Ctx(nc, es)
        S = cx.S
        TA = 256
        SB = 2048
        NTK = SL * NB
        hT_d = cx.dram_in("hT", [NTK // TA, 128, 16 * TA], BF16)
        wqk_d = cx.dram_in("wqk", [128, 16, 768], F32)
        wv_d = cx.dram_in("wv", [128, 2, 2048], F32)
        qg_d = cx.dram_in("qg", [128, 3], F32)
        kg_d = cx.dram_in("kg", [128, 3], F32)
        bm_d = cx.dram_in("bmat", [128, 3, 256], F32)
        vd = cx.dram_out("vscratch", [NTK, 256], BF16)
        oT = cx.dram_out("oT", [2, 128, NTK], BF16)
        cx.consts()
        make_eps(cx)
        qg, t_qg0 = load_small(cx, qg_d, [128, 3], F32, "qg")
        kg, t_kg = load_small(cx, kg_d, [128, 3], F32, "kg")
        bm, t_bm = load_small(cx, bm_d, [128, 3, 256], F32, "bm")
        qgs = cx.sb([128, 3], F32, "qgs")
        t_qg = S.tile("qgs")
        S.op("dve", lambda e: e.tensor_scalar(out=qgs[:], in0=qg[:], scalar1=float(128 ** -0.5), scalar2=None, op0=ALU.mult),
             reads=[t_qg0], writes=[t_qg])
        wqk_f, t_wqk = load_cast_weight(cx, wqk_d, [128, 16, 768], "wqk")
        wv_f, t_wv = load_cast_weight(cx, wv_d, [128, 2, 2048], "wv")
        wqk = wqk_f[:]
        wv = wv_f[:].rearrange("p a (c n) -> p (a c) n", n=256)

        hT = [cx.sb([128, 16, TA], BF16, "hT") for _ in range(2)]
        t_hT = S.tiles("hT", 2)
        lnb = cx.sb([128, TA], F32, "lnb")
        t_ln = S.tile("ln")
        NQ = 3
        q32 = [cx.sb([128, TA], F32, "q32") for _ in range(NQ)]
        t_q32 = S.tiles("q32", NQ)
        sq32 = [cx.sb([128, TA], BF16, "sq32") for _ in range(NQ)]
        t_sq32 = S.tiles("sq32", NQ)
        r2 = [cx.sb([128, TA], F32, "r2") for _ in range(NQ)]
        t_r2 = S.tiles("r2", NQ)
        Qs = [cx.sb([128, SB], BF16, "Qs") for _ in range(3)]
        t_Qs = S.tiles("Qs", 3)
        Ks = [[cx.sb([128, SB], BF16, "Ks") for _ in range(2)] for _ in range(3)]
        t_Ks = [S.tiles("Ks", 2) for _ in range(3)]
        vst = [cx.sb([128, 256], BF16, "vst") for _ in range(4)]
        t_vst = S.tiles("vst", 4)
        vbuf = [cx.sb([128, 8192], BF16, "vbuf") for _ in range(2)]
        t_vbuf = S.tiles("vbuf", 2)
        acc = cx.sb([128, 3, SB], F32, "acc")
        t_acc = S.tile("acc")
        rden = cx.sb([128, SB], F32, "rden")
        t_rden = S.tile("rden")
        ob = cx.sb([128, 2, SB], BF16, "ob")
        t_ob = S.tile("ob")
        NST = 3
        st = [cx.sb([128, 256], F32, "st") for _ in range(NST)]
        t_st = S.tiles("st", NST)
        pT = [cx.sb([128, 256], BF16, "pT") for _ in range(NST)]
        t_pT = S.tiles("pT", NST)
        banks = [cx.ps([128, 512], F32, "bk") for _ in range(8)]
        t_bk = S.tiles("bk", 8)
        obf = cx.ones_bf
        SPS_B = (0, 1, 2)
        PO_B = (3, 4, 7)

        t_vstore_pool = S.tiles("vstore", 16)
        vctr = [0]
        vbc = [0]
        pa = [0]
        qc = [0]
        NSB = SL // SB
        NTI = SB // TA
        tiles_all = [(b, sb, ti) for b in range(NB) for sb in range(NSB) for ti in range(NTI)]

        def load_h(gi):
            b_, sb_, ti_ = tiles_all[gi]
            tok0 = b_ * SL + sb_ * SB + ti_ * TA
            hTb = hT[gi % 2]
            gt = tok0 // TA
            S.op("sp", lambda e: e.dma_start(out=hTb[:].rearrange("p c t -> p (c t)"), in_=hT_d[gt]),
                 writes=[t_hT[gi % 2]], dma=True)

        load_h(0)
        gi = 0
        for b in range(NB):
            t_vstore = []
            for sb in range(NSB):
                cur = sb % 2
                prv = 1 - cur
                stores_this = []

                def part1(gi, ti, k):
                    which, g = divmod(k, 3)
                    hTb = hT[gi % 2]
                    pbi = pa[0] % 5
                    pa[0] += 1
                    qi_ = qc[0] % NQ
                    qc[0] += 1
                    pb, tpb = banks[pbi], t_bk[pbi]
                    col0 = k * 128
                    for dc in range(16):
                        S.op("pe", lambda e, dc=dc: e.matmul(pb[:, :TA], wqk[:, dc, col0:col0 + 128], hTb[:, dc, :], start=(dc == 0), stop=(dc == 15)),
                             reads=[t_wqk, t_hT[gi % 2]], writes=[tpb])
                    S.op("act", lambda e: e.activation(out=q32[qi_][:], in_=pb[:, :TA], func=AF.Copy), reads=[tpb], writes=[t_q32[qi_]])
                    S.op("dve", lambda e: e.tensor_tensor(out=sq32[qi_][:], in0=q32[qi_][:], in1=q32[qi_][:], op=ALU.mult),
                         reads=[t_q32[qi_]], writes=[t_sq32[qi_]])
                    return qi_

                def part2(gi, ti, k, qi_, cur=cur):
                    which, g = divmod(k, 3)
                    r = DIL_R[g]
                    if which == 0:
                        dst_t, t_dst, gain, t_gain = Qs[g], t_Qs[g], qgs[:, g:g + 1], t_qg
                    else:
                        dst_t, t_dst, gain, t_gain = Ks[g][cur], t_Ks[g][cur], kg[:, g:g + 1], t_kg
                    a0 = ti * TA // r
                    if r == 1:
                        dst_ap = dst_t[:, ti * TA:(ti + 1) * TA]
                        in0 = q32[qi_][:]
                        in1 = r2[qi_][:]
                    else:
                        dst_ap = dst_t[:].rearrange("p (b a) -> p b a", b=r)[:, :, a0:a0 + TA // r]
                        in0 = q32[qi_][:].rearrange("p (a b) -> p b a", b=r)
                        in1 = r2[qi_][:].rearrange("p (a b) -> p b a", b=r)
                    sb_ = 5 + (qi_ % 2)
                    S.op("pe", lambda e: e.matmul(banks[sb_][:, :TA], obf[:], sq32[qi_][:], start=True, stop=True),
                         reads=[t_sq32[qi_], cx.t_ones_bf], writes=[t_bk[sb_]])
                    rstd_from_sumsq(cx, banks[sb_], t_bk[sb_], TA, 1.0 / 128, lnb, t_ln, r2[qi_], t_r2[qi_])
                    S.op("dve", lambda e: e.scalar_tensor_tensor(out=dst_ap, in0=in0, scalar=gain, in1=in1, op0=ALU.mult, op1=ALU.mult),
                         reads=[t_q32[qi_], t_r2[qi_], t_gain], writes=[t_dst])

                def vproj(gi, ti, sub, b=b, sb=sb):
                    hTb = hT[gi % 2]
                    pbi = pa[0] % 5
                    pa[0] += 1
                    pb, tpb = banks[pbi], t_bk[pbi]
                    for dc in range(16):
                        S.op("pe", lambda e, dc=dc: e.matmul(pb[:, :256], hTb[:, dc, sub * 128:(sub + 1) * 128], wv[:, dc, :], start=(dc == 0), stop=(dc == 15)),
                             reads=[t_hT[gi % 2], t_wv], writes=[tpb])
                    vi = vctr[0] % 4
                    vctr[0] += 1
                    S.op("act", lambda e: e.activation(out=vst[vi][:], in_=pb[:, 0:256], func=AF.Copy), reads=[tpb], writes=[t_vst[vi]])
                    tk = b * SL + sb * SB + ti * TA + sub * 128
                    t_store = t_vstore_pool[(ti * (TA // 128) + sub) % 16]
                    S.op("sp", lambda e: e.dma_start(out=vd[tk:tk + 128, :], in_=vst[vi][:]),
                         reads=[t_vst[vi]], writes=[t_store], dma=True, dma_tile=t_store)
                    stores_this.append(t_store)

                for ti in range(NTI):
                    if gi + 1 < len(tiles_all):
                        load_h(gi + 1)
                    a = [None] * 6
                    a[0] = part1(gi, ti, 0)
                    a[1] = part1(gi, ti, 1)
                    for k in range(2, 6):
                        part2(gi, ti, k - 2, a[k - 2])
                        a[k] = part1(gi, ti, k)
                    part2(gi, ti, 4, a[4])
                    vproj(gi, ti, 0)
                    part2(gi, ti, 5, a[5])
                    vproj(gi, ti, 1)
                    gi += 1

                first = True
                has_prev_sb = sb > 0
                lo = -1 if has_prev_sb else 0
                for g in range(3):
                    r = DIL_R[g]
                    nrow = 16 // r
                    Lsb = SB // r
                    vb_i = vbc[0] % 2
                    vbc[0] += 1
                    vb, t_vb = vbuf[vb_i], t_vbuf[vb_i]
                    ntile = nrow - lo
                    base_row = (b * SL + sb * SB) // r
                    vsrc = vd.rearrange("(n x) d -> n (x d)", x=r)
                    r0 = base_row + lo * 128
                    src = vsrc[r0:r0 + ntile * 128, :].rearrange("(n i) x -> i n x", i=128)
                    vview = vb[:, 0:ntile * r * 256].rearrange("p (n x) -> p n x", x=r * 256)
                    deps = list(stores_this) + (list(t_vstore) if has_prev_sb else [])
                    S.op("sp", lambda e, vview=vview, src=src: e.dma_start(out=vview, in_=src),
                         reads=deps, writes=[t_vb], dma=True)
                    accv = acc[:].rearrange("p h (a b) -> p h a b", b=r)
                    blocks = [(rb, n_) for rb in range(r) for n_ in range(nrow)]
                    LA = 2

                    def sps_mm(i, g=g, cur=cur, prv=prv, Lsb=Lsb):
                        rb, n_ = blocks[i]
                        has_prev = has_prev_sb or n_ > 0
                        c_cur = rb * Lsb + n_ * 128
                        qblk = Qs[g][:, c_cur:c_cur + 128]
                        kcur = Ks[g][cur][:, c_cur:c_cur + 128]
                        if n_ > 0:
                            kprev = Ks[g][cur][:, c_cur - 128:c_cur]
                            t_kprev = t_Ks[g][cur]
                        else:
                            kprev = Ks[g][prv][:, rb * Lsb + Lsb - 128:rb * Lsb + Lsb]
                            t_kprev = t_Ks[g][prv]
                        bi_ = SPS_B[i % 3]
                        sps, t_sps = banks[bi_], t_bk[bi_]
                        if has_prev:
                            S.op("pe", lambda e: e.matmul(sps[:, 0:128], kprev, qblk, start=True, stop=True),
                                 reads=[t_kprev, t_Qs[g]], writes=[t_sps])
                        S.op("pe", lambda e: e.matmul(sps[:, 128:256], kcur, qblk, start=True, stop=True),
                             reads=[t_Ks[g][cur], t_Qs[g]], writes=[t_sps])

                    for i in range(min(LA, len(blocks))):
                        sps_mm(i)
                    for i, (rb, n_) in enumerate(blocks):
                        has_prev = has_prev_sb or n_ > 0
                        c_lo = 0 if has_prev else 128
                        bi_ = SPS_B[i % 3]
                        sps, t_sps = banks[bi_], t_bk[bi_]
                        stb, t_stb = st[i % NST], t_st[i % NST]
                        S.op("dve", lambda e, sps=sps, stb=stb, g=g, c_lo=c_lo: e.tensor_tensor(
                            out=stb[:, c_lo:256], in0=sps[:, c_lo:256], in1=bm[:, g, c_lo:256], op=ALU.add),
                            reads=[t_sps, t_bm], writes=[t_stb])
                        p_, t_p = pT[i % NST], t_pT[i % NST]
                        S.op("act", lambda e, stb=stb, p_=p_, c_lo=c_lo: e.activation(out=p_[:, c_lo:256], in_=stb[:, c_lo:256], func=AF.Exp),
                             reads=[t_stb], writes=[t_p])
                        if i + LA < len(blocks):
                            sps_mm(i + LA)
                        pb_ = PO_B[i % 3]
                        po, t_po = banks[pb_], t_bk[pb_]
                        ti_v = n_ - lo
                        for h in range(3):
                            if h < 2:
                                lc = vview[:, ti_v, rb * 256 + h * 128:rb * 256 + (h + 1) * 128]
                                lp = vview[:, ti_v - 1, rb * 256 + h * 128:rb * 256 + (h + 1) * 128] if has_prev else None
                                rd = [t_vb, t_p]
                            else:
                                lc = obf[:]
                                lp = obf[:]
                                rd = [cx.t_ones_bf, t_p]
                            if has_prev:
                                S.op("pe", lambda e, po=po, lp=lp, p_=p_, h=h: e.matmul(po[:, h * 128:(h + 1) * 128], lp, p_[:, 0:128], start=True, stop=False),
                                     reads=rd, writes=[t_po])
                            S.op("pe", lambda e, po=po, lc=lc, p_=p_, h=h, has_prev=has_prev: e.matmul(
                                po[:, h * 128:(h + 1) * 128], lc, p_[:, 128:256], start=(not has_prev), stop=True),
                                reads=rd, writes=[t_po])
                        dst = accv[:, :, n_ * 128:(n_ + 1) * 128, rb]
                        src_po = po[:, 0:384].rearrange("p (h q) -> p h q", h=3)
                        if first:
                            S.op("dve", lambda e, dst=dst, src_po=src_po: e.tensor_copy(out=dst, in_=src_po),
                                 reads=[t_po], writes=[t_acc])
                        else:
                            S.op("dve", lambda e, dst=dst, src_po=src_po: e.tensor_tensor(out=dst, in0=src_po, in1=dst, op=ALU.add),
                                 reads=[t_po, t_acc], writes=[t_acc])
                    first = False
                t_vstore = stores_this
                S.op("dve", lambda e: e.reciprocal(out=rden[:], in_=acc[:, 2, :]), reads=[t_acc], writes=[t_rden])
                for h in range(2):
                    S.op("dve", lambda e, h=h: e.tensor_tensor(out=ob[:, h, :], in0=acc[:, h, :], in1=rden[:], op=ALU.mult),
                         reads=[t_acc, t_rden], writes=[t_ob])
                tk = b * SL + sb * SB
                t_ost = S.tile("ost")
                S.op("sp", lambda e, tk=tk: e.dma_start(out=oT[:, :, tk:tk + SB].rearrange("h p t -> p h t"), in_=ob[:]),
                     reads=[t_ob], writes=[t_ost], dma=True, dma_tile=t_ost, final=True)
        S.finalize()
        with nc.Block() as block:
            S.emit(block)
    return nc


def qk_proj_perm(cx, W, t_W, col0, hT, t_hT, n, pb, tpb, ss2, t_ss2, q32, t_q32, sq32, t_sq32, lnb, t_ln, r2, t_r2,
                 gain, t_gain, dst_ap, t_dst, r):
    S = cx.S
    for dc in range(16):
        S.op("pe", lambda e, dc=dc: e.matmul(pb[:, :n], W[:, dc, col0:col0 + 128], hT[:, dc, :n], start=(dc == 0), stop=(dc == 15)),
             reads=[t_W, t_hT], writes=[tpb])
    S.op("act", lambda e: e.activation(out=q32[:, :n], in_=pb[:, :n], func=AF.Copy), reads=[tpb], writes=[t_q32])
    S.op("dve", lambda e: e.tensor_tensor(out=sq32[:, :n], in0=q32[:, :n], in1=q32[:, :n], op=ALU.mult),
         reads=[t_q32], writes=[t_sq32])
    ofb = cx.ones_bf
    S.op("pe", lambda e: e.matmul(ss2[:, :n], ofb[:], sq32[:, :n], start=True, stop=True),
         reads=[t_sq32, cx.t_ones_bf], writes=[t_ss2])
    rstd_from_sumsq(cx, ss2, t_ss2, n, 1.0 / 128, lnb, t_ln, r2, t_r2)
    if r == 1:
        in0 = q32[:, :n]
        in1 = r2[:, :n]
    else:
        in0 = q32[:, :n].rearrange("p (a b) -> p b a", b=r)
        in1 = r2[:, :n].rearrange("p (a b) -> p b a", b=r)
    S.op("dve", lambda e: e.scalar_tensor_tensor(out=dst_ap, in0=in0, scalar=gain, in1=in1, op0=ALU.mult, op1=ALU.mult),
         reads=[t_q32, t_r2, t_gain], writes=[t_dst])


_CACHE = {}


def _prog(name, fn, *a):
    if name not in _CACHE:
        _CACHE[name] = fn(*a)
    return _CACHE[name]


def _fm(xT2d):
    return np.ascontiguousarray(xT2d.reshape(16, 128, xT2d.shape[1]))


def _tiles(xT2d, ta=256):
    T = xT2d.shape[1]
    a = xT2d.reshape(16, 128, T // ta, ta).transpose(2, 1, 0, 3)
    return np.ascontiguousarray(a.reshape(T // ta, 128, 16 * ta))


def _wchunks(w, ngrp, per):
    K, N = w.shape
    kc = K // 128
    cb = N // 128
    a = w.reshape(kc, 128, cb, 128)
    a = a.transpose(2, 1, 0, 3)
    a = a.reshape(ngrp, per, 128, kc, 128).transpose(0, 2, 1, 3, 4)
    return np.ascontiguousarray(a.reshape(ngrp, 128, 4, (per * kc * 128) // 4))


def _wcols(w, cols):
    a = w[:, cols].reshape(16, 128, len(cols)).transpose(1, 0, 2)
    return np.ascontiguousarray(a)


def _gvec(g):
    return np.ascontiguousarray(g.reshape(16, 128).T)


def _run(nc, in_maps):
    res = run_bass_kernel_spmd(nc, in_maps, core_ids=list(range(NCORES)))
    return res.results


def kernel(x, fox_w_in, fox_b_f, fox_q_gain, fox_k_gain, fox_w_out, dil_w_in, dil_q_gain, dil_k_gain, dil_w_out,
           mix_norm_g, mlp_norm_g, mlp_w_up, mlp_w_down):
    f32 = np.float32
    x = np.asarray(x, f32)
    xT = np.ascontiguousarray(x.reshape(NTOK, D).T)
    xT_tl = _tiles(xT)

    w_in = np.asarray(fox_w_in[0], f32)
    H = 16
    fox = _prog("fox", build_fox)
    maps = []
    for c in range(NCORES):
        hA, hB = 2 * c, 2 * c + 1
        qcols = list(range(hA * 128, hA * 128 + 128)) + list(range(hB * 128, hB * 128 + 128))
        kcols = [H * 128 + q for q in qcols]
        vcols = [2 * H * 128 + q for q in qcols] + [3 * H * 128 + hA, 3 * H * 128 + hB]
        maps.append({
            "xT": xT_tl,
            "wq": _wcols(w_in, qcols).reshape(128, 2, 2048),
            "wk": _wcols(w_in, kcols).reshape(128, 2, 2048),
            "wv": _wcols(w_in, vcols[:256]),
            "wf": np.ascontiguousarray(np.pad(_wcols(w_in, vcols[256:]), ((0, 0), (0, 0), (0, 2)))),
            "gmix": _gvec(np.asarray(mix_norm_g[0], f32)),
            "qg": np.asarray(fox_q_gain[0], f32).reshape(128, 1),
            "kg": np.asarray(fox_k_gain[0], f32).reshape(128, 1),
            "bfb": np.ascontiguousarray(np.pad(np.broadcast_to(np.asarray(fox_b_f[0], f32)[[hA, hB]][None, :], (128, 2)), ((0, 0), (0, 2)))),
        })
    r = _run(fox, maps)
    oT_all = np.concatenate([r[c]["oT"].reshape(256, NTOK) for c in range(NCORES)], axis=0)

    mlp_h = _prog("mlp_h", build_mlp, True)
    wout_l = _wchunks(np.asarray(fox_w_out[0], f32), 4, 4)
    wup_l = _wchunks(np.asarray(mlp_w_up[0], f32), 16, 4)
    wdn = np.asarray(mlp_w_down[0], f32)
    wdn_l = np.ascontiguousarray(wdn.reshape(64, 128, 16, 128).transpose(2, 1, 0, 3).reshape(16, 128, 4, 2048))
    maps = []
    for c in range(NCORES):
        sl = slice(c * TPC, (c + 1) * TPC)
        maps.append({
            "xT": _fm(xT[:, sl]), "oT": _fm(oT_all[:, sl]),
            "wout": wout_l, "wup": wup_l, "wdn": wdn_l,
            "g_mlp": _gvec(np.asarray(mlp_norm_g[0], f32)),
            "g_nxt": _gvec(np.asarray(mix_norm_g[1], f32)),
        })
    r = _run(mlp_h, maps)
    x1T = np.concatenate([r[c]["x1T"].reshape(D, TPC) for c in range(NCORES)], axis=1)
    h1T = np.concatenate([r[c]["h1T"].reshape(D, TPC) for c in range(NCORES)], axis=1)
    h1T_tl = _tiles(h1T)

    dil = _prog("dil", build_dil)
    w_in1 = np.asarray(dil_w_in[0], f32)
    G, HD = 3, 8
    maps = []
    ki = np.arange(128)[:, None]
    qi = np.arange(128)[None, :]
    for c in range(NCORES):
        cols = []
        for which in range(2):
            for g in range(G):
                base = which * G * HD * 128 + (g * HD + c) * 128
                cols += list(range(base, base + 128))
        vcols = list(range(2 * G * HD * 128 + c * 256, 2 * G * HD * 128 + (c + 1) * 256))
        bmat = np.empty((128, 3, 256), f32)
        for g in range(G):
            slope = np.float32(2.0) ** (np.float32(-8.0) * np.float32(g * HD + c + 1) / np.float32(G * HD))
            rr = DIL_R[g]
            dprev = (qi + 128 - ki).astype(f32)
            dcur = (qi - ki).astype(f32)
            bmat[:, g, 0:128] = np.where(qi <= ki, -slope * dprev * rr, NEG)
            bmat[:, g, 128:256] = np.where(qi >= ki, -slope * dcur * rr, NEG)
        maps.append({
            "hT": h1T_tl,
            "wqk": _wcols(w_in1, cols),
            "wv": _wcols(w_in1, vcols).reshape(128, 2, 2048),
            "qg": np.ascontiguousarray(np.asarray(dil_q_gain[0], f32).T),
            "kg": np.ascontiguousarray(np.asarray(dil_k_gain[0], f32).T),
            "bmat": bmat,
        })
    r = _run(dil, maps)
    o1T_all = np.concatenate([r[c]["oT"].reshape(256, NTOK) for c in range(NCORES)], axis=0)

    mlp_l = _prog("mlp_l", build_mlp, False)
    wout_l = _wchunks(np.asarray(dil_w_out[0], f32), 4, 4)
    wup_l = _wchunks(np.asarray(mlp_w_up[1], f32), 16, 4)
    wdn = np.asarray(mlp_w_down[1], f32)
    wdn_l = np.ascontiguousarray(wdn.reshape(64, 128, 16, 128).transpose(2, 1, 0, 3).reshape(16, 128, 4, 2048))
    maps = []
    for c in range(NCORES):
        sl = slice(c * TPC, (c + 1) * TPC)
        maps.append({
            "xT": _fm(x1T[:, sl]), "oT": _fm(o1T_all[:, sl]),
            "wout": wout_l, "wup": wup_l, "wdn": wdn_l,
            "g_mlp": _gvec(np.asarray(mlp_norm_g[1], f32)),
        })
    r = _run(mlp_l, maps)
    x2T = np.concatenate([r[c]["x1T"].reshape(D, TPC) for c in range(NCORES)], axis=1)
    return np.ascontiguousarray(x2T.T).reshape(2, S_LEN, D).astype(f32)
```

```python
import numpy as np
from contextlib import ExitStack
import ml_dtypes
import concourse.bass as bass
import concourse.mybir as mybir
from concourse.bass_utils import run_bass_kernel_spmd

F32 = mybir.dt.float32
BF16 = mybir.dt.bfloat16
AF = mybir.ActivationFunctionType
ALU = mybir.AluOpType

NCORES = 8
D = 2048
S_LEN = 8192
NTOK = 16384
TPC = NTOK // NCORES
EPS = 1e-6
NEG = -30000.0

ENGS = ("pe", "act", "dve", "pool", "sp")


class Tile:
    __slots__ = ("name", "last_w", "readers", "dsem", "dcnt")

    def __init__(self, name):
        self.name = name
        self.last_w = None
        self.readers = []
        self.dsem = None
        self.dcnt = 0


class Op:
    __slots__ = ("idx", "eng", "fn", "deps", "dma", "dsem", "dval", "has_dep", "inc", "waits")

    def __init__(self, idx, eng, fn, dma):
        self.idx = idx
        self.eng = eng
        self.fn = fn
        self.deps = set()
        self.dma = dma
        self.dsem = None
        self.dval = 0
        self.has_dep = False
        self.inc = 0
        self.waits = []


class Sched:
    def __init__(self, nc, es):
        self.nc = nc
        self.es = es
        self.ops = []
        self.sem = {e: es.enter_context(nc.semaphore("sem_" + e)) for e in ENGS}
        self.final_dma = []
        self.ntile = 0

    def tile(self, name="t"):
        self.ntile += 1
        return Tile(f"{name}_{self.ntile}")

    def tiles(self, name, n):
        return [self.tile(name) for _ in range(n)]

    def _dma_sem(self, t):
        if t.dsem is None:
            t.dsem = self.es.enter_context(self.nc.semaphore("d_" + t.name))
        return t.dsem

    def op(self, eng, fn, reads=(), writes=(), dma=False, dma_tile=None, final=False):
        o = Op(len(self.ops), eng, fn, dma)
        for t in reads:
            if t.last_w is not None:
                o.deps.add(t.last_w)
        for t in writes:
            if t.last_w is not None:
                o.deps.add(t.last_w)
            o.deps.update(t.readers)
        for t in reads:
            t.readers.append(o.idx)
        for t in writes:
            t.last_w = o.idx
            t.readers = []
        if dma:
            t = dma_tile if dma_tile is not None else (writes[0] if writes else reads[0])
            o.dsem = self._dma_sem(t)
            t.dcnt += 16
            o.dval = t.dcnt
            if final:
                self.final_dma.append(o)
        self.ops.append(o)
        return o

    def finalize(self):
        ops = self.ops
        for o in ops:
            best = {}
            keep = []
            for d in o.deps:
                p = ops[d]
                if p.dma:
                    keep.append(d)
                    continue
                if p.eng == "pe" and o.eng == "pe":
                    continue
                if p.eng not in best or best[p.eng] < d:
                    best[p.eng] = d
            o.deps = keep + list(best.values())
            for d in o.deps:
                ops[d].has_dep = True
        cnt = {e: 0 for e in ENGS}
        for o in ops:
            if not o.dma and o.has_dep:
                cnt[o.eng] += 1
                o.inc = cnt[o.eng]
        known = {e: {} for e in ENGS}
        for o in ops:
            w = {}
            for d in o.deps:
                p = ops[d]
                if p.dma:
                    key = ("d", id(p.dsem))
                    sem, val = p.dsem, p.dval
                else:
                    key = ("e", p.eng)
                    sem, val = self.sem[p.eng], p.inc
                if known[o.eng].get(key, 0) >= val:
                    continue
                if key not in w or w[key][1] < val:
                    w[key] = (sem, val)
            for key, (sem, val) in w.items():
                known[o.eng][key] = val
            o.waits = list(w.values())

    def emit(self, block):
        per = {e: [o for o in self.ops if o.eng == e] for e in ENGS}
        finals = self.final_dma
        sems = self.sem

        def run(h, lst):
            for o in lst:
                for sem, val in o.waits:
                    h.wait_ge(sem, val)
                ins = o.fn(h)
                if o.dma:
                    ins.then_inc(o.dsem, 16)
                elif o.inc:
                    ins.then_inc(sems[o.eng], 1)

        @block.tensor
        def _(e):
            run(e, per["pe"])

        @block.scalar
        def _(e):
            run(e, per["act"])

        @block.vector
        def _(e):
            run(e, per["dve"])

        @block.gpsimd
        def _(e):
            run(e, per["pool"])

        @block.sync
        def _(e):
            run(e, per["sp"])
            for o in finals:
                e.wait_ge(o.dsem, o.dval)


class Ctx:
    def __init__(self, nc, es):
        self.nc = nc
        self.es = es
        self.S = Sched(nc, es)
        self.nalloc = 0

    def sb(self, shape, dt, name="sb"):
        self.nalloc += 1
        return self.es.enter_context(self.nc.sbuf_tensor(f"{name}{self.nalloc}", list(shape), dt))

    def ps(self, shape, dt=F32, name="ps"):
        self.nalloc += 1
        return self.es.enter_context(self.nc.psum_tensor(f"{name}{self.nalloc}", list(shape), dt))

    def dram_in(self, name, shape, dt):
        return self.nc.dram_tensor(name, list(shape), dt, kind="ExternalInput").ap()

    def dram_out(self, name, shape, dt):
        return self.nc.dram_tensor(name, list(shape), dt, kind="ExternalOutput").ap()

    def consts(self):
        S = self.S
        self.ones_bf = self.sb([128, 128], BF16, "ones_bf")
        self.ones_f = self.sb([128, 128], F32, "ones_f")
        self.t_ones_bf = S.tile("ones_bf")
        self.t_ones_f = S.tile("ones_f")
        ob, of = self.ones_bf, self.ones_f
        S.op("dve", lambda e: e.memset(ob[:], 1.0), writes=[self.t_ones_bf])
        S.op("dve", lambda e: e.memset(of[:], 1.0), writes=[self.t_ones_f])


def load_small(cx, dram_ap, shape, dt=F32, name="c", q="sp"):
    t = cx.sb(shape, dt, name)
    tl = cx.S.tile(name)
    cx.S.op(q, lambda e: e.dma_start(out=t[:], in_=dram_ap), writes=[tl], dma=True)
    return t, tl


def load_cast_weight(cx, dram_ap, shape, name):
    t = cx.sb(shape, BF16, name)
    tl = cx.S.tile(name)
    cx.S.op("pool", lambda e: e.dma_start(out=t[:], in_=dram_ap), writes=[tl], dma=True)
    return t, tl


def rstd_from_sumsq(cx, ss_ps, t_ss, n, inv_count, lnb, t_ln, rstd, t_rstd):
    S = cx.S
    eps_t = cx.eps_t
    S.op("act", lambda e: e.activation(out=lnb[:, :n], in_=ss_ps[:, :n], func=AF.Ln, bias=eps_t[:, 0:1], scale=inv_count),
         reads=[t_ss, cx.t_eps], writes=[t_ln])
    S.op("act", lambda e: e.activation(out=rstd[:, :n], in_=lnb[:, :n], func=AF.Exp, scale=-0.5),
         reads=[t_ln], writes=[t_rstd])


def make_eps(cx):
    cx.eps_t = cx.sb([128, 1], F32, "eps")
    cx.t_eps = cx.S.tile("eps")
    et = cx.eps_t
    cx.S.op("dve", lambda e: e.memset(et[:], EPS), writes=[cx.t_eps])


def norm_tile(cx, xt, t_xt, g_sb, t_g, hT, t_hT, n, sqb, t_sq, ss_ps, t_ss, lnb, t_ln, rstd, t_rstd):
    S = cx.S
    S.op("act", lambda e: e.activation(out=sqb[:, :, :n], in_=xt[:, :, :n], func=AF.Square),
         reads=t_xt, writes=[t_sq])
    ob = cx.ones_bf
    for dc in range(16):
        S.op("pe", lambda e, dc=dc: e.matmul(ss_ps[:, :n], ob[:], sqb[:, dc, :n], start=(dc == 0), stop=(dc == 15)),
             reads=[t_sq, cx.t_ones_bf], writes=[t_ss])
    rstd_from_sumsq(cx, ss_ps, t_ss, n, 1.0 / D, lnb, t_ln, rstd, t_rstd)
    for dc in range(16):
        S.op("dve", lambda e, dc=dc: e.scalar_tensor_tensor(out=hT[:, dc, :n], in0=xt[:, dc, :n], scalar=g_sb[:, dc:dc + 1],
                                                             in1=rstd[:, :n], op0=ALU.mult, op1=ALU.mult),
             reads=[t_xt[dc], t_g, t_rstd], writes=[t_hT])


def build_mlp(emit_h):
    nc = bass.Bass("TRN2", target_bir_lowering=False)
    with ExitStack() as es:
        cx = Ctx(nc, es)
        S = cx.S
        TT = 512
        NT = TPC // TT
        xT = cx.dram_in("xT", [16, 128, TPC], F32)
        oT = cx.dram_in("oT", [16, 128, TPC], BF16)
        wout = cx.dram_in("wout", [4, 128, 4, 2048], F32)
        wup = cx.dram_in("wup", [16, 128, 4, 2048], F32)
        wdn = cx.dram_in("wdn", [16, 128, 4, 2048], F32)
        g_mlp = cx.dram_in("g_mlp", [128, 16], F32)
        x1T = cx.dram_out("x1T", [16, 128, TPC], F32)
        if emit_h:
            g_nxt = cx.dram_in("g_nxt", [128, 16], F32)
            h1T = cx.dram_out("h1T", [16, 128, TPC], BF16)
        cx.consts()
        make_eps(cx)
        g_sb, t_g = load_small(cx, g_mlp, [128, 16], F32, "g_mlp")
        if emit_h:
            gn_sb, t_gn = load_small(cx, g_nxt, [128, 16], F32, "g_nxt")

        xt = cx.sb([128, 16, TT], F32, "xt")
        t_xt = S.tiles("xt", 16)
        t_xt_ld = S.tile("xt_ld")
        ot = cx.sb([128, 16, TT], BF16, "ot")
        t_ot = S.tile("ot")
        hT = cx.sb([128, 16, TT], BF16, "hT")
        t_hT = S.tile("hT")
        aT = cx.sb([128, 64, TT], BF16, "aT")
        t_aT = S.tiles("aT", 64)
        sqb = cx.sb([128, 16, TT], BF16, "sqb")
        t_sq = S.tile("sq")
        lnb = cx.sb([128, TT], F32, "lnb")
        t_ln = S.tile("ln")
        rstd = cx.sb([128, TT], F32, "rstd")
        t_rstd = S.tile("rstd")
        r32 = [cx.sb([128, TT], F32, "r32") for _ in range(2)]
        t_r32 = S.tiles("r32", 2)
        NW = 3
        wslot = [cx.sb([128, 4, 2048], BF16, "wslot") for _ in range(NW)]
        t_w = S.tiles("w", NW)
        pbank = [cx.ps([128, 512], F32, "pb") for _ in range(5)]
        t_pb = S.tiles("pb", 5)
        ss_ps = cx.ps([128, 512], F32, "ss")
        t_ss = S.tile("ss")
        wctr = [0]
        pctr = [0]

        def wload(src):
            i = wctr[0] % NW
            wctr[0] += 1
            S.op("pool", lambda e: e.dma_start(out=wslot[i][:], in_=src), writes=[t_w[i]], dma=True)
            return wslot[i], t_w[i]

        def nextbank():
            i = pctr[0] % 5
            pctr[0] += 1
            return pbank[i], t_pb[i]

        for tt in range(NT):
            t0 = tt * TT
            S.op("sp", lambda e, t0=t0: e.dma_start(out=ot[:], in_=oT[:, :, t0:t0 + TT].rearrange("c p t -> p c t")),
                 writes=[t_ot], dma=True)
            S.op("sp", lambda e, t0=t0: e.dma_start(out=xt[:], in_=xT[:, :, t0:t0 + TT].rearrange("c p t -> p c t")),
                 writes=t_xt, dma=True, dma_tile=t_xt_ld)
            for grp in range(4):
                ws, tw = wload(wout[grp])
                for dc4 in range(4):
                    dc = grp * 4 + dc4
                    pb, tpb = nextbank()
                    for kc in range(16):
                        S.op("pe", lambda e, ws=ws, pb=pb, dc4=dc4, kc=kc: e.matmul(
                            pb[:], ws[:, dc4, kc * 128:(kc + 1) * 128], ot[:, kc, :], start=(kc == 0), stop=(kc == 15)),
                            reads=[tw, t_ot], writes=[tpb])
                    S.op("dve", lambda e, pb=pb, dc=dc: e.tensor_tensor(out=xt[:, dc, :], in0=pb[:], in1=xt[:, dc, :], op=ALU.add),
                         reads=[tpb, t_xt[dc]], writes=[t_xt[dc]])
            norm_tile(cx, xt, t_xt, g_sb, t_g, hT, t_hT, TT, sqb, t_sq, ss_ps, t_ss, lnb, t_ln, rstd, t_rstd)
            for grp in range(16):
                ws, tw = wload(wup[grp])
                for fc4 in range(4):
                    fc = grp * 4 + fc4
                    pb, tpb = nextbank()
                    for dc in range(16):
                        S.op("pe", lambda e, ws=ws, pb=pb, fc4=fc4, dc=dc: e.matmul(
                            pb[:], ws[:, fc4, dc * 128:(dc + 1) * 128], hT[:, dc, :], start=(dc == 0), stop=(dc == 15)),
                            reads=[tw, t_hT], writes=[tpb])
                    rb = r32[fc % 2]
                    trb = t_r32[fc % 2]
                    S.op("act", lambda e, pb=pb, rb=rb: e.activation(out=rb[:], in_=pb[:], func=AF.Relu),
                         reads=[tpb], writes=[trb])
                    S.op("dve", lambda e, pb=pb, rb=rb, fc=fc: e.scalar_tensor_tensor(
                        out=aT[:, fc, :], in0=pb[:], scalar=0.0, in1=rb[:], op0=ALU.max, op1=ALU.mult),
                        reads=[tpb, trb], writes=[t_aT[fc]])
            for dc in range(16):
                ws, tw = wload(wdn[dc])
                pb, tpb = nextbank()
                for fc in range(64):
                    S.op("pe", lambda e, ws=ws, pb=pb, fc=fc: e.matmul(
                        pb[:], ws[:, fc // 16, (fc % 16) * 128:(fc % 16 + 1) * 128], aT[:, fc, :], start=(fc == 0), stop=(fc == 63)),
                        reads=[tw, t_aT[fc]], writes=[tpb])
                S.op("dve", lambda e, pb=pb, dc=dc: e.tensor_tensor(out=xt[:, dc, :], in0=pb[:], in1=xt[:, dc, :], op=ALU.add),
                     reads=[tpb, t_xt[dc]], writes=[t_xt[dc]])
            S.op("sp", lambda e, t0=t0: e.dma_start(out=x1T[:, :, t0:t0 + TT].rearrange("c p t -> p c t"), in_=xt[:]),
                 reads=t_xt, dma=True, dma_tile=S.tile("x1st"), final=True)
            if emit_h:
                norm_tile(cx, xt, t_xt, gn_sb, t_gn, hT, t_hT, TT, sqb, t_sq, ss_ps, t_ss, lnb, t_ln, rstd, t_rstd)
                S.op("sp", lambda e, t0=t0: e.dma_start(out=h1T[:, :, t0:t0 + TT].rearrange("c p t -> p c t"), in_=hT[:]),
                     reads=[t_hT], dma=True, dma_tile=S.tile("h1st"), final=True)
        S.finalize()
        with nc.Block() as block:
            S.emit(block)
    return nc


def qk_proj(cx, W, t_W, col0, hT, t_hT, n, pb, tpb, ss2, t_ss2, q32, t_q32, sq32, t_sq32, lnb, t_ln, r2, t_r2,
            gain, t_gain, dst_ap, t_dst):
    S = cx.S
    for dc in range(16):
        S.op("pe", lambda e, dc=dc: e.matmul(pb[:, :n], W[:, dc, col0:col0 + 128], hT[:, dc, :n], start=(dc == 0), stop=(dc == 15)),
             reads=[t_W, t_hT], writes=[tpb])
    S.op("act", lambda e: e.activation(out=q32[:, :n], in_=pb[:, :n], func=AF.Copy), reads=[tpb], writes=[t_q32])
    S.op("dve", lambda e: e.tensor_tensor(out=sq32[:, :n], in0=q32[:, :n], in1=q32[:, :n], op=ALU.mult),
         reads=[t_q32], writes=[t_sq32])
    ofb = cx.ones_bf
    S.op("pe", lambda e: e.matmul(ss2[:, :n], ofb[:], sq32[:, :n], start=True, stop=True),
         reads=[t_sq32, cx.t_ones_bf], writes=[t_ss2])
    rstd_from_sumsq(cx, ss2, t_ss2, n, 1.0 / 128, lnb, t_ln, r2, t_r2)
    S.op("dve", lambda e: e.scalar_tensor_tensor(out=dst_ap, in0=q32[:, :n], scalar=gain, in1=r2[:, :n],
                                                 op0=ALU.mult, op1=ALU.mult),
         reads=[t_q32, t_r2, t_gain], writes=[t_dst])


def build_fox(SL=S_LEN, NB=2, stop=9):
    nc = bass.Bass("TRN2", target_bir_lowering=False)
    with ExitStack() as es:
        cx = Ctx(nc, es)
        S = cx.S
        TA = 256
        NTK = SL * NB
        NJ = SL // 128
        NQB = SL // 512
        xT = cx.dram_in("xT", [NTK // TA, 128, 16 * TA], F32)
        wq_d = cx.dram_in("wq", [128, 2, 2048], F32)
        wk_d = cx.dram_in("wk", [128, 2, 2048], F32)
        wv_d = cx.dram_in("wv", [128, 16, 256], F32)
        wf_d = cx.dram_in("wf", [128, 16, 4], F32)
        gmix_d = cx.dram_in("gmix", [128, 16], F32)
        qg_d = cx.dram_in("qg", [128, 1], F32)
        kg_d = cx.dram_in("kg", [128, 1], F32)
        bf_d = cx.dram_in("bfb", [128, 4], F32)
        oT = cx.dram_out("oT", [2, 128, NTK], BF16)
        cx.consts()
        make_eps(cx)
        g_sb, t_g = load_small(cx, gmix_d, [128, 16], F32, "gmix")
        qg, t_qg0 = load_small(cx, qg_d, [128, 1], F32, "qg")
        kg, t_kg = load_small(cx, kg_d, [128, 1], F32, "kg")
        bfb, t_bfb = load_small(cx, bf_d, [128, 4], F32, "bfb")
        qgs = cx.sb([128, 1], F32, "qgs")
        t_qg = S.tile("qgs")
        S.op("dve", lambda e: e.tensor_scalar(out=qgs[:], in0=qg[:], scalar1=float(128 ** -0.5), scalar2=None, op0=ALU.mult),
             reads=[t_qg0], writes=[t_qg])
        wq_f, t_wq = load_cast_weight(cx, wq_d, [128, 2, 2048], "wq")
        wk_f, t_wk = load_cast_weight(cx, wk_d, [128, 2, 2048], "wk")
        wvf_f = cx.sb([128, 16, 260], BF16, "wvf")
        t_wvf = S.tile("wvf")
        wf32 = cx.sb([128, 16, 4], F32, "wf32")
        t_wf32 = S.tile("wf32")
        S.op("pool", lambda e: e.dma_start(out=wvf_f[:, :, 0:256], in_=wv_d), writes=[t_wvf], dma=True)
        S.op("sp", lambda e: e.dma_start(out=wf32[:], in_=wf_d), writes=[t_wf32], dma=True)
        S.op("dve", lambda e: e.tensor_copy(out=wvf_f[:, :, 256:260], in_=wf32[:]), reads=[t_wf32, t_wvf], writes=[t_wvf])
        wq = wq_f[:].rearrange("p a (c n) -> p (a c) n", n=256)
        wk = wk_f[:].rearrange("p a (c n) -> p (a c) n", n=256)
        wvf = wvf_f[:]
        tri = cx.sb([128, 128], BF16, "tri")
        t_tri = S.tile("tri")
        of = cx.ones_f
        obf_ = cx.ones_bf
        S.op("pool", lambda e: e.affine_select(out=tri[:], in_=obf_[:], pattern=[[1, 128]], compare_op=ALU.is_ge, fill=0.0,
                                               base=0, channel_multiplier=-1),
             reads=[cx.t_ones_bf], writes=[t_tri])

        if stop == 0:
            S.finalize()
            with nc.Block() as block:
                S.emit(block)
            return nc
        xt = [cx.sb([128, 16, TA], F32, "xt") for _ in range(2)]
        t_xt = [S.tiles("xt", 1) * 16 for _ in range(2)]
        hT = [cx.sb([128, 16, TA], BF16, "hT") for _ in range(2)]
        t_hT = [S.tiles("hT", 16) for _ in range(2)]
        sqb = cx.sb([128, 16, TA], BF16, "sqb")
        t_sq = S.tile("sq")
        lnb = [cx.sb([128, TA], F32, "lnb") for _ in range(2)]
        t_ln = S.tiles("ln", 2)
        rstd = cx.sb([128, TA], F32, "rstd")
        t_rstd = S.tile("rstd")
        NQ = 3
        q32 = [cx.sb([128, TA], F32, "q32") for _ in range(NQ)]
        t_q32 = S.tiles("q32", NQ)
        sq32 = [cx.sb([128, TA], BF16, "sq32") for _ in range(NQ)]
        t_sq32 = S.tiles("sq32", NQ)
        r2 = [cx.sb([128, TA], F32, "r2") for _ in range(NQ)]
        t_r2 = S.tiles("r2", NQ)
        QT = [cx.sb([128, SL], BF16, "QT") for _ in range(2)]
        KT = [cx.sb([128, SL], BF16, "KT") for _ in range(2)]
        V = cx.sb([128, 2, NJ, 128], BF16, "V")
        t_QT = S.tiles("QT", 2)
        t_KT = S.tiles("KT", 2)
        t_V = S.tile("V")
        Z = cx.sb([128, NJ, 4], F32, "Z")
        t_Z = S.tile("Z")
        E = cx.sb([128, NJ, 2], F32, "E")
        t_E = S.tile("E")
        SP = cx.sb([128, 2, NJ], F32, "SP")
        t_SP = S.tile("SP")
        SPp = [cx.sb([128, 2, NJ], BF16, "SPp") for _ in range(3)]
        t_SPp = S.tile("SPp")
        SPr = cx.sb([128, 2, NJ], F32, "SPr")
        t_SPr = S.tile("SPr")
        tot = cx.sb([128, NJ], F32, "tot")
        t_tot = S.tile("tot")
        cum = cx.sb([128, NJ], F32, "cum")
        t_cum = S.tile("cum")
        excl = cx.sb([128, NJ], F32, "excl")
        t_excl = S.tile("excl")
        negc = cx.sb([128, NJ], F32, "negc")
        t_negc = S.tile("negc")
        bias = cx.sb([128, NQB, NJ], F32, "bias")
        t_bias = S.tile("bias")
        NP = 4
        pT = [cx.sb([128, 512], BF16, "pT") for _ in range(NP)]
        t_pT = S.tiles("pT", NP)
        rden = cx.sb([128, 512], F32, "rden")
        t_rden = S.tile("rden")
        ob = [cx.sb([128, 512], BF16, "ob") for _ in range(2)]
        t_ob = S.tiles("ob", 2)
        banks = [cx.ps([128, 512], F32, "bk") for _ in range(8)]
        t_bk = S.tiles("bk", 8)
        obf = cx.ones_bf
        NTI = SL // TA

        for b in range(NB):
            pa = [0]
            qc = [0]

            def load_x(ti):
                tok0 = b * SL + ti * TA
                xtb = xt[ti % 2]
                gt = tok0 // TA
                S.op("sp", lambda e, xtb=xtb, gt=gt: e.dma_start(out=xtb[:].rearrange("p c t -> p (c t)"), in_=xT[gt]),
                     writes=[t_xt[ti % 2][0]], dma=True)

            def norm_p1(ti):
                xtb = xt[ti % 2]
                S.op("act", lambda e: e.activation(out=sqb[:], in_=xtb[:], func=AF.Square), reads=t_xt[ti % 2][:1], writes=[t_sq])

            def norm_p2(ti):
                for dc in range(16):
                    S.op("pe", lambda e, dc=dc: e.matmul(banks[7][:, :TA], obf[:], sqb[:, dc, :], start=(dc == 0), stop=(dc == 15)),
                         reads=[t_sq, cx.t_ones_bf], writes=[t_bk[7]])
                rstd_from_sumsq(cx, banks[7], t_bk[7], TA, 1.0 / D, lnb[0], t_ln[0], rstd, t_rstd)

            def norm_p3(ti, lo_=0, hi_=16):
                xtb, hTb = xt[ti % 2], hT[ti % 2]
                for dc in range(lo_, hi_):
                    S.op("dve", lambda e, dc=dc: e.scalar_tensor_tensor(out=hTb[:, dc, :], in0=xtb[:, dc, :], scalar=g_sb[:, dc:dc + 1],
                                                                         in1=rstd[:], op0=ALU.mult, op1=ALU.mult),
                         reads=[t_xt[ti % 2][0], t_g, t_rstd], writes=[t_hT[ti % 2][dc]])

            combos = [(wq, t_wq, QT, t_QT, qgs, t_qg, 0), (wq, t_wq, QT, t_QT, qgs, t_qg, 1),
                      (wk, t_wk, KT, t_KT, kg, t_kg, 0), (wk, t_wk, KT, t_KT, kg, t_kg, 1)]

            def part1(ti, k):
                W, t_W, dstl, t_dstl, gain, t_gain, hd = combos[k]
                hTb = hT[ti % 2]
                pbi = pa[0] % 5
                pa[0] += 1
                qi_ = qc[0] % NQ
                qc[0] += 1
                pb, tpb = banks[pbi], t_bk[pbi]
                for dc in range(16):
                    S.op("pe", lambda e, dc=dc: e.matmul(pb[:, :TA], W[:, dc, hd * 128:hd * 128 + 128], hTb[:, dc, :], start=(dc == 0), stop=(dc == 15)),
                         reads=[t_W, t_hT[ti % 2][dc]], writes=[tpb])
                S.op("act", lambda e: e.activation(out=q32[qi_][:], in_=pb[:, :TA], func=AF.Copy), reads=[tpb], writes=[t_q32[qi_]])
                S.op("pool", lambda e: e.tensor_tensor(out=sq32[qi_][:], in0=q32[qi_][:], in1=q32[qi_][:], op=ALU.mult),
                     reads=[t_q32[qi_]], writes=[t_sq32[qi_]])
                return qi_

            def part2(ti, k, qi_):
                W, t_W, dstl, t_dstl, gain, t_gain, hd = combos[k]
                sb_ = 5 + (qi_ % 2)
                S.op("pe", lambda e: e.matmul(banks[sb_][:, :TA], obf[:], sq32[qi_][:], start=True, stop=True),
                     reads=[t_sq32[qi_], cx.t_ones_bf], writes=[t_bk[sb_]])
                rstd_from_sumsq(cx, banks[sb_], t_bk[sb_], TA, 1.0 / 128, lnb[1], t_ln[1], r2[qi_], t_r2[qi_])
                S.op("dve", lambda e: e.scalar_tensor_tensor(out=dstl[hd][:, ti * TA:(ti + 1) * TA], in0=q32[qi_][:], scalar=gain[:, 0:1],
                                                             in1=r2[qi_][:], op0=ALU.mult, op1=ALU.mult),
                     reads=[t_q32[qi_], t_r2[qi_], t_gain], writes=[t_dstl[hd]])

            def vproj(ti, sub):
                hTb = hT[ti % 2]
                j = ti * (TA // 128) + sub
                pbi = pa[0] % 5
                pa[0] += 1
                pb, tpb = banks[pbi], t_bk[pbi]
                for dc in range(16):
                    S.op("pe", lambda e, dc=dc: e.matmul(pb[:, :260], hTb[:, dc, sub * 128:(sub + 1) * 128], wvf[:, dc, :], start=(dc == 0), stop=(dc == 15)),
                         reads=[t_hT[ti % 2][dc], t_wvf], writes=[tpb])
                S.op("dve", lambda e: e.tensor_tensor(out=Z[:, j, :], in0=pb[:, 256:260], in1=bfb[:], op=ALU.add),
                     reads=[tpb, t_bfb], writes=[t_Z])
                for hd_ in range(2):
                    S.op("act", lambda e, hd_=hd_: e.activation(out=V[:, hd_, j, :], in_=pb[:, hd_ * 128:(hd_ + 1) * 128], func=AF.Copy),
                         reads=[tpb, t_Z], writes=[t_V])

            load_x(0)
            if NTI > 1:
                load_x(1)
            norm_p1(0)
            norm_p2(0)
            norm_p3(0)
            for ti in range(NTI):
                nxt = ti + 1 < NTI
                if ti + 2 < NTI:
                    load_x(ti + 2)
                if nxt:
                    norm_p1(ti + 1)
                a0 = part1(ti, 0)
                a1 = part1(ti, 1)
                if nxt:
                    norm_p2(ti + 1)
                    norm_p3(ti + 1)
                part2(ti, 0, a0)
                a2 = part1(ti, 2)
                part2(ti, 1, a1)
                a3 = part1(ti, 3)
                part2(ti, 2, a2)
                vproj(ti, 0)
                part2(ti, 3, a3)
                vproj(ti, 1)
            if stop == 1:
                break
            S.op("act", lambda e: e.activation(out=E[:], in_=Z[:, :, 0:2], func=AF.Exp, scale=-1.0), reads=[t_Z], writes=[t_E])
            S.op("act", lambda e: e.activation(out=SP[:].rearrange("p h j -> p j h"), in_=E[:], func=AF.Ln, bias=1.0, scale=1.0),
                 reads=[t_E], writes=[t_SP])
            S.op("dve", lambda e: e.tensor_copy(out=SPp[0][:], in_=SP[:]), reads=[t_SP], writes=[t_SPp])
            S.op("dve", lambda e: e.tensor_tensor(out=SPr[:], in0=SP[:], in1=SPp[0][:], op=ALU.subtract), reads=[t_SP, t_SPp], writes=[t_SPr])
            S.op("dve", lambda e: e.tensor_copy(out=SPp[1][:], in_=SPr[:]), reads=[t_SPr], writes=[t_SPp])
            S.op("dve", lambda e: e.tensor_tensor(out=SPr[:], in0=SPr[:], in1=SPp[1][:], op=ALU.subtract), reads=[t_SPr, t_SPp], writes=[t_SPr])
            S.op("dve", lambda e: e.tensor_copy(out=SPp[2][:], in_=SPr[:]), reads=[t_SPr], writes=[t_SPp])
            pc = [0]
            sc = [0]
            for hd in range(2):
                for pc_ in range(3):
                    S.op("pe", lambda e, hd=hd, pc_=pc_: e.matmul(banks[0][:, :NJ], tri[:], SPp[pc_][:, hd, :], start=(pc_ == 0), stop=(pc_ == 2)),
                         reads=[t_tri, t_SPp], writes=[t_bk[0]])
                for pc_ in range(3):
                    S.op("pe", lambda e, hd=hd, pc_=pc_: e.matmul(banks[1][:, :NJ], obf_[:], SPp[pc_][:, hd, :], start=(pc_ == 0), stop=(pc_ == 2)),
                         reads=[cx.t_ones_bf, t_SPp], writes=[t_bk[1]])
                S.op("dve", lambda e: e.tensor_copy(out=tot[:], in_=banks[1][:, :NJ]), reads=[t_bk[1]], writes=[t_tot])
                S.op("dve", lambda e: e.tensor_tensor_scan(out=cum[:], data0=of[:, 0:NJ], data1=tot[:], initial=0.0,
                                                           op0=ALU.mult, op1=ALU.add),
                     reads=[t_tot, cx.t_ones_f], writes=[t_cum])
                S.op("dve", lambda e: e.tensor_tensor(out=excl[:], in0=cum[:], in1=tot[:], op=ALU.subtract),
                     reads=[t_cum, t_tot], writes=[t_excl])
                S.op("dve", lambda e: e.tensor_tensor(out=negc[:], in0=banks[0][:, :NJ], in1=excl[:], op=ALU.add),
                     reads=[t_bk[0], t_excl], writes=[t_negc])
                for qb in range(NQB):
                    nj = 4 * qb + 4
                    S.op("dve", lambda e, qb=qb, nj=nj: e.tensor_scalar(
                        out=bias[:, qb, 0:nj], in0=negc[:, 0:nj], scalar1=excl[:, 4 * qb:4 * qb + 1], scalar2=None, op0=ALU.subtract),
                        reads=[t_negc, t_excl], writes=[t_bias])
                if stop == 2:
                    continue
                pairs = []
                for qb in range(NQB):
                    for j in range(4 * qb + 4):
                        pairs.append((qb, j))
                LA = 2
                slots = {}

                def qk(idx):
                    qb, j = pairs[idx]
                    d = max(0, j - 4 * qb)
                    c0 = 128 * d
                    n = 512 - c0
                    si = sc[0] % 3
                    sc[0] += 1
                    slots[idx] = si
                    q0 = qb * 512
                    S.op("pe", lambda e, hd=hd: e.matmul(banks[si][:, :n], KT[hd][:, j * 128:(j + 1) * 128], QT[hd][:, q0 + c0:q0 + 512], start=True, stop=True),
                         reads=[t_KT[hd], t_QT[hd]], writes=[t_bk[si]])

                for idx in range(min(LA, len(pairs))):
                    qk(idx)
                for idx, (qb, j) in enumerate(pairs):
                    q0 = qb * 512
                    oi = qb % 2
                    ops_, t_ops = banks[3 + oi], t_bk[3 + oi]
                    dps, t_dps = banks[5 + oi], t_bk[5 + oi]
                    nj = 4 * qb + 4
                    d = max(0, j - 4 * qb)
                    c0 = 128 * d
                    n = 512 - c0
                    si = slots.pop(idx)
                    sps, t_sps = banks[si], t_bk[si]
                    pi = pc[0] % NP
                    pc[0] += 1
                    p_, t_p = pT[pi], t_pT[pi]
                    S.op("act", lambda e, sps=sps, p_=p_, qb=qb, j=j, n=n: e.activation(
                        out=p_[:, :n], in_=sps[:, :n], func=AF.Exp, bias=bias[:, qb, j:j + 1], scale=1.0),
                        reads=[t_sps, t_bias], writes=[t_p])
                    if j >= 4 * qb:
                        S.op("pool", lambda e, p_=p_: e.affine_select(
                            out=p_[:, 0:128], in_=p_[:, 0:128], pattern=[[1, 128]], compare_op=ALU.is_ge, fill=0.0,
                            base=0, channel_multiplier=-1), reads=[t_p], writes=[t_p])
                    if idx + LA < len(pairs):
                        qk(idx + LA)
                    S.op("pe", lambda e, ops_=ops_, p_=p_, hd=hd, j=j, c0=c0, n=n, nj=nj: e.matmul(
                        ops_[:, c0:512], V[:, hd, j, :], p_[:, :n], start=(j == 0), stop=(j == nj - 1)),
                        reads=[t_V, t_p], writes=[t_ops])
                    S.op("pe", lambda e, dps=dps, p_=p_, j=j, c0=c0, n=n, nj=nj: e.matmul(
                        dps[:, c0:512], obf[:], p_[:, :n], start=(j == 0), stop=(j == nj - 1)),
                        reads=[cx.t_ones_bf, t_p], writes=[t_dps])
                    if j == nj - 1:
                        S.op("dve", lambda e, dps=dps: e.reciprocal(out=rden[:], in_=dps[:]), reads=[t_dps], writes=[t_rden])
                        obb, t_obb = ob[oi], t_ob[oi]
                        S.op("dve", lambda e, ops_=ops_, obb=obb: e.tensor_tensor(out=obb[:], in0=ops_[:], in1=rden[:], op=ALU.mult),
                             reads=[t_ops, t_rden], writes=[t_obb])
                        tk = b * SL + q0
                        S.op("sp", lambda e, obb=obb, hd=hd, tk=tk: e.dma_start(out=oT[hd, :, tk:tk + 512], in_=obb[:]),
                             reads=[t_obb], dma=True, dma_tile=t_obb, final=True)
        S.finalize()
        with nc.Block() as block:
            S.emit(block)
    return nc


DIL_R = (1, 4, 16)


def build_dil(SL=S_LEN, NB=2):
    nc = bass.Bass("TRN2", target_bir_lowering=False)
    with ExitStack() as es:
        cx = Ctx(nc, es)
        S = cx.S
        TA = 256
        SB = 2048
        NTK = SL * NB
        hT_d = cx.dram_in("hT", [NTK // TA, 128, 16 * TA], BF16)
        wqk_d = cx.dram_in("wqk", [128, 16, 768], F32)
        wv_d = cx.dram_in("wv", [128, 2, 2048], F32)
        qg_d = cx.dram_in("qg", [128, 3], F32)
        kg_d = cx.dram_in("kg", [128, 3], F32)
        bm_d = cx.dram_in("bmat", [128, 3, 256], F32)
        vd = cx.dram_out("vscratch", [NTK, 256], BF16)
        oT = cx.dram_out("oT", [2, 128, NTK], BF16)
        cx.consts()
        make_eps(cx)
        qg, t_qg0 = load_small(cx, qg_d, [128, 3], F32, "qg")
        kg, t_kg = load_small(cx, kg_d, [128, 3], F32, "kg")
        bm, t_bm = load_small(cx, bm_d, [128, 3, 256], F32, "bm")
        qgs = cx.sb([128, 3], F32, "qgs")
        t_qg = S.tile("qgs")
        S.op("dve", lambda e: e.tensor_scalar(out=qgs[:], in0=qg[:], scalar1=float(128 ** -0.5), scalar2=None, op0=ALU.mult),
             reads=[t_qg0], writes=[t_qg])
        wqk_f, t_wqk = load_cast_weight(cx, wqk_d, [128, 16, 768], "wqk")
        wv_f, t_wv = load_cast_weight(cx, wv_d, [128, 2, 2048], "wv")
        wqk = wqk_f[:]
        wv = wv_f[:].rearrange("p a (c n) -> p (a c) n", n=256)

        hT = [cx.sb([128, 16, TA], BF16, "hT") for _ in range(2)]
        t_hT = S.tiles("hT", 2)
        lnb = cx.sb([128, TA], F32, "lnb")
        t_ln = S.tile("ln")
        NQ = 3
        q32 = [cx.sb([128, TA], F32, "q32") for _ in range(NQ)]
        t_q32 = S.tiles("q32", NQ)
        sq32 = [cx.sb([128, TA], BF16, "sq32") for _ in range(NQ)]
        t_sq32 = S.tiles("sq32", NQ)
        r2 = [cx.sb([128, TA], F32, "r2") for _ in range(NQ)]
        t_r2 = S.tiles("r2", NQ)
        Qs = [cx.sb([128, SB], BF16, "Qs") for _ in range(3)]
        t_Qs = S.tiles("Qs", 3)
        Ks = [[cx.sb([128, SB], BF16, "Ks") for _ in range(2)] for _ in range(3)]
        t_Ks = [S.tiles("Ks", 2) for _ in range(3)]
        vst = [cx.sb([128, 256], BF16, "vst") for _ in range(4)]
        t_vst = S.tiles("vst", 4)
        vbuf = [cx.sb([128, 8192], BF16, "vbuf") for _ in range(2)]
        t_vbuf = S.tiles("vbuf", 2)
        acc = cx.sb([128, 3, SB], F32, "acc")
        t_acc = S.tile("acc")
        rden = cx.sb([128, SB], F32, "rden")
        t_rden = S.tile("rden")
        ob = cx.sb([128, 2, SB], BF16, "ob")
        t_ob = S.tile("ob")
        NST = 3
        st = [cx.sb([128, 256], F32, "st") for _ in range(NST)]
        t_st = S.tiles("st", NST)
        pT = [cx.sb([128, 256], BF16, "pT") for _ in range(NST)]
        t_pT = S.tiles("pT", NST)
        banks = [cx.ps([128, 512], F32, "bk") for _ in range(8)]
        t_bk = S.tiles("bk", 8)
        obf = cx.ones_bf
        SPS_B = (0, 1, 2)
        PO_B = (3, 4, 7)

        t_vstore_pool = S.tiles("vstore", 16)
        vctr = [0]
        vbc = [0]
        pa = [0]
        qc = [0]
        NSB = SL // SB
        NTI = SB // TA
        tiles_all = [(b, sb, ti) for b in range(NB) for sb in range(NSB) for ti in range(NTI)]

        def load_h(gi):
            b_, sb_, ti_ = tiles_all[gi]
            tok0 = b_ * SL + sb_ * SB + ti_ * TA
            hTb = hT[gi % 2]
            gt = tok0 // TA
            S.op("sp", lambda e: e.dma_start(out=hTb[:].rearrange("p c t -> p (c t)"), in_=hT_d[gt]),
                 writes=[t_hT[gi % 2]], dma=True)

        load_h(0)
        gi = 0
        for b in range(NB):
            t_vstore = []
            for sb in range(NSB):
                cur = sb % 2
                prv = 1 - cur
                stores_this = []

                def part1(gi, ti, k):
                    which, g = divmod(k, 3)
                    hTb = hT[gi % 2]
                    pbi = pa[0] % 5
                    pa[0] += 1
                    qi_ = qc[0] % NQ
                    qc[0] += 1
                    pb, tpb = banks[pbi], t_bk[pbi]
                    col0 = k * 128
                    for dc in range(16):
                        S.op("pe", lambda e, dc=dc: e.matmul(pb[:, :TA], wqk[:, dc, col0:col0 + 128], hTb[:, dc, :], start=(dc == 0), stop=(dc == 15)),
                             reads=[t_wqk, t_hT[gi % 2]], writes=[tpb])
                    S.op("act", lambda e: e.activation(out=q32[qi_][:], in_=pb[:, :TA], func=AF.Copy), reads=[tpb], writes=[t_q32[qi_]])
                    S.op("pool", lambda e: e.tensor_tensor(out=sq32[qi_][:], in0=q32[qi_][:], in1=q32[qi_][:], op=ALU.mult),
                         reads=[t_q32[qi_]], writes=[t_sq32[qi_]])
                    return qi_

                def part2(gi, ti, k, qi_, cur=cur):
                    which, g = divmod(k, 3)
                    r = DIL_R[g]
                    if which == 0:
                        dst_t, t_dst, gain, t_gain = Qs[g], t_Qs[g], qgs[:, g:g + 1], t_qg
                    else:
                        dst_t, t_dst, gain, t_gain = Ks[g][cur], t_Ks[g][cur], kg[:, g:g + 1], t_kg
                    a0 = ti * TA // r
                    if r == 1:
                        dst_ap = dst_t[:, ti * TA:(ti + 1) * TA]
                        in0 = q32[qi_][:]
                        in1 = r2[qi_][:]
                    else:
                        dst_ap = dst_t[:].rearrange("p (b a) -> p b a", b=r)[:, :, a0:a0 + TA // r]
                        in0 = q32[qi_][:].rearrange("p (a b) -> p b a", b=r)
                        in1 = r2[qi_][:].rearrange("p (a b) -> p b a", b=r)
                    sb_ = 5 + (qi_ % 2)
                    S.op("pe", lambda e: e.matmul(banks[sb_][:, :TA], obf[:], sq32[qi_][:], start=True, stop=True),
                         reads=[t_sq32[qi_], cx.t_ones_bf], writes=[t_bk[sb_]])
                    rstd_from_sumsq(cx, banks[sb_], t_bk[sb_], TA, 1.0 / 128, lnb, t_ln, r2[qi_], t_r2[qi_])
                    S.op("dve", lambda e: e.scalar_tensor_tensor(out=dst_ap, in0=in0, scalar=gain, in1=in1, op0=ALU.mult, op1=ALU.mult),
                         reads=[t_q32[qi_], t_r2[qi_], t_gain], writes=[t_dst])

                def vproj(gi, ti, sub, b=b, sb=sb):
                    hTb = hT[gi % 2]
                    pbi = pa[0] % 5
                    pa[0] += 1
                    pb, tpb = banks[pbi], t_bk[pbi]
                    for dc in range(16):
                        S.op("pe", lambda e, dc=dc: e.matmul(pb[:, :256], hTb[:, dc, sub * 128:(sub + 1) * 128], wv[:, dc, :], start=(dc == 0), stop=(dc == 15)),
                             reads=[t_hT[gi % 2], t_wv], writes=[tpb])
                    vi = vctr[0] % 4
                    vctr[0] += 1
                    S.op("act", lambda e: e.activation(out=vst[vi][:], in_=pb[:, 0:256], func=AF.Copy), reads=[tpb], writes=[t_vst[vi]])
                    tk = b * SL + sb * SB + ti * TA + sub * 128
                    t_store = t_vstore_pool[(ti * (TA // 128) + sub) % 16]
                    S.op("sp", lambda e: e.dma_start(out=vd[tk:tk + 128, :], in_=vst[vi][:]),
                         reads=[t_vst[vi]], writes=[t_store], dma=True, dma_tile=t_store)
                    stores_this.append(t_store)

                for ti in range(NTI):
                    if gi + 1 < len(tiles_all):
                        load_h(gi + 1)
                    a = [None] * 6
                    a[0] = part1(gi, ti, 0)
                    a[1] = part1(gi, ti, 1)
                    for k in range(2, 6):
                        part2(gi, ti, k - 2, a[k - 2])
                        a[k] = part1(gi, ti, k)
                    part2(gi, ti, 4, a[4])
                    vproj(gi, ti, 0)
                    part2(gi, ti, 5, a[5])
                    vproj(gi, ti, 1)
                    gi += 1

                first = True
                has_prev_sb = sb > 0
                lo = -1 if has_prev_sb else 0
                for g in range(3):
                    r = DIL_R[g]
                    nrow = 16 // r
                    Lsb = SB // r
                    vb_i = vbc[0] % 2
                    vbc[0] += 1
                    vb, t_vb = vbuf[vb_i], t_vbuf[vb_i]
                    ntile = nrow - lo
                    base_row = (b * SL + sb * SB) // r
                    vsrc = vd.rearrange("(n x) d -> n (x d)", x=r)
                    r0 = base_row + lo * 128
                    src = vsrc[r0:r0 + ntile * 128, :].rearrange("(n i) x -> i n x", i=128)
                    vview = vb[:, 0:ntile * r * 256].rearrange("p (n x) -> p n x", x=r * 256)
                    deps = list(stores_this) + (list(t_vstore) if has_prev_sb else [])
                    S.op("sp", lambda e, vview=vview, src=src: e.dma_start(out=vview, in_=src),
                         reads=deps, writes=[t_vb], dma=True)
                    accv = acc[:].rearrange("p h (a b) -> p h a b", b=r)
                    blocks = [(rb, n_) for rb in range(r) for n_ in range(nrow)]
                    LA = 3

                    def sps_mm(i, g=g, cur=cur, prv=prv, Lsb=Lsb):
                        rb, n_ = blocks[i]
                        has_prev = has_prev_sb or n_ > 0
                        c_cur = rb * Lsb + n_ * 128
                        qblk = Qs[g][:, c_cur:c_cur + 128]
                        kcur = Ks[g][cur][:, c_cur:c_cur + 128]
                        if n_ > 0:
                            kprev = Ks[g][cur][:, c_cur - 128:c_cur]
                            t_kprev = t_Ks[g][cur]
                        else:
                            kprev = Ks[g][prv][:, rb * Lsb + Lsb - 128:rb * Lsb + Lsb]
                            t_kprev = t_Ks[g][prv]
                        bi_ = SPS_B[i % 3]
                        sps, t_sps = banks[bi_], t_bk[bi_]
                        if has_prev:
                            S.op("pe", lambda e: e.matmul(sps[:, 0:128], kprev, qblk, start=True, stop=True),
                                 reads=[t_kprev, t_Qs[g]], writes=[t_sps])
                        S.op("pe", lambda e: e.matmul(sps[:, 128:256], kcur, qblk, start=True, stop=True),
                             reads=[t_Ks[g][cur], t_Qs[g]], writes=[t_sps])

                    def add_bias(i, g=g):
                        rb, n_ = blocks[i]
                        has_prev = has_prev_sb or n_ > 0
                        c_lo = 0 if has_prev else 128
                        bi_ = SPS_B[i % 3]
                        sps, t_sps = banks[bi_], t_bk[bi_]
                        stb, t_stb = st[i % NST], t_st[i % NST]
                        S.op("dve", lambda e: e.tensor_tensor(
                            out=stb[:, c_lo:256], in0=sps[:, c_lo:256], in1=bm[:, g, c_lo:256], op=ALU.add),
                            reads=[t_sps, t_bm], writes=[t_stb])

                    for i in range(min(LA, len(blocks))):
                        sps_mm(i)
                    add_bias(0)
                    for i, (rb, n_) in enumerate(blocks):
                        has_prev = has_prev_sb or n_ > 0
                        c_lo = 0 if has_prev else 128
                        stb, t_stb = st[i % NST], t_st[i % NST]
                        p_, t_p = pT[i % NST], t_pT[i % NST]
                        S.op("act", lambda e, stb=stb, p_=p_, c_lo=c_lo: e.activation(out=p_[:, c_lo:256], in_=stb[:, c_lo:256], func=AF.Exp),
                             reads=[t_stb], writes=[t_p])
                        if i + 1 < len(blocks):
                            add_bias(i + 1)
                        if i + LA < len(blocks):
                            sps_mm(i + LA)
                        pb_ = PO_B[i % 3]
                        po, t_po = banks[pb_], t_bk[pb_]
                        ti_v = n_ - lo
                        for h in range(3):
                            if h < 2:
                                lc = vview[:, ti_v, rb * 256 + h * 128:rb * 256 + (h + 1) * 128]
                                lp = vview[:, ti_v - 1, rb * 256 + h * 128:rb * 256 + (h + 1) * 128] if has_prev else None
                                rd = [t_vb, t_p]
                            else:
                                lc = obf[:]
                                lp = obf[:]
                                rd = [cx.t_ones_bf, t_p]
                            if has_prev:
                                S.op("pe", lambda e, po=po, lp=lp, p_=p_, h=h: e.matmul(po[:, h * 128:(h + 1) * 128], lp, p_[:, 0:128], start=True, stop=False),
                                     reads=rd, writes=[t_po])
                            S.op("pe", lambda e, po=po, lc=lc, p_=p_, h=h, has_prev=has_prev: e.matmul(
                                po[:, h * 128:(h + 1) * 128], lc, p_[:, 128:256], start=(not has_prev), stop=True),
                                reads=rd, writes=[t_po])
                        dst = accv[:, :, n_ * 128:(n_ + 1) * 128, rb]
                        src_po = po[:, 0:384].rearrange("p (h q) -> p h q", h=3)
                        if first:
                            S.op("dve", lambda e, dst=dst, src_po=src_po: e.tensor_copy(out=dst, in_=src_po),
                                 reads=[t_po], writes=[t_acc])
                        else:
                            S.op("dve", lambda e, dst=dst, src_po=src_po: e.tensor_tensor(out=dst, in0=src_po, in1=dst, op=ALU.add),
                                 reads=[t_po, t_acc], writes=[t_acc])
                    first = False
                t_vstore = stores_this
                S.op("dve", lambda e: e.reciprocal(out=rden[:], in_=acc[:, 2, :]), reads=[t_acc], writes=[t_rden])
                for h in range(2):
                    S.op("dve", lambda e, h=h: e.tensor_tensor(out=ob[:, h, :], in0=acc[:, h, :], in1=rden[:], op=ALU.mult),
                         reads=[t_acc, t_rden], writes=[t_ob])
                tk = b * SL + sb * SB
                t_ost = S.tile("ost")
                S.op("sp", lambda e, tk=tk: e.dma_start(out=oT[:, :, tk:tk + SB].rearrange("h p t -> p h t"), in_=ob[:]),
                     reads=[t_ob], writes=[t_ost], dma=True, dma_tile=t_ost, final=True)
        S.finalize()
        with nc.Block() as block:
            S.emit(block)
    return nc


def qk_proj_perm(cx, W, t_W, col0, hT, t_hT, n, pb, tpb, ss2, t_ss2, q32, t_q32, sq32, t_sq32, lnb, t_ln, r2, t_r2,
                 gain, t_gain, dst_ap, t_dst, r):
    S = cx.S
    for dc in range(16):
        S.op("pe", lambda e, dc=dc: e.matmul(pb[:, :n], W[:, dc, col0:col0 + 128], hT[:, dc, :n], start=(dc == 0), stop=(dc == 15)),
             reads=[t_W, t_hT], writes=[tpb])
    S.op("act", lambda e: e.activation(out=q32[:, :n], in_=pb[:, :n], func=AF.Copy), reads=[tpb], writes=[t_q32])
    S.op("dve", lambda e: e.tensor_tensor(out=sq32[:, :n], in0=q32[:, :n], in1=q32[:, :n], op=ALU.mult),
         reads=[t_q32], writes=[t_sq32])
    ofb = cx.ones_bf
    S.op("pe", lambda e: e.matmul(ss2[:, :n], ofb[:], sq32[:, :n], start=True, stop=True),
         reads=[t_sq32, cx.t_ones_bf], writes=[t_ss2])
    rstd_from_sumsq(cx, ss2, t_ss2, n, 1.0 / 128, lnb, t_ln, r2, t_r2)
    if r == 1:
        in0 = q32[:, :n]
        in1 = r2[:, :n]
    else:
        in0 = q32[:, :n].rearrange("p (a b) -> p b a", b=r)
        in1 = r2[:, :n].rearrange("p (a b) -> p b a", b=r)
    S.op("dve", lambda e: e.scalar_tensor_tensor(out=dst_ap, in0=in0, scalar=gain, in1=in1, op0=ALU.mult, op1=ALU.mult),
         reads=[t_q32, t_r2, t_gain], writes=[t_dst])


_CACHE = {}


def _prog(name, fn, *a):
    if name not in _CACHE:
        _CACHE[name] = fn(*a)
    return _CACHE[name]


def _fm(xT2d):
    return np.ascontiguousarray(xT2d.reshape(16, 128, xT2d.shape[1]))


def _tiles(xT2d, ta=256):
    T = xT2d.shape[1]
    a = xT2d.reshape(16, 128, T // ta, ta).transpose(2, 1, 0, 3)
    return np.ascontiguousarray(a.reshape(T // ta, 128, 16 * ta))


def _wchunks(w, ngrp, per):
    K, N = w.shape
    kc = K // 128
    cb = N // 128
    a = w.reshape(kc, 128, cb, 128)
    a = a.transpose(2, 1, 0, 3)
    a = a.reshape(ngrp, per, 128, kc, 128).transpose(0, 2, 1, 3, 4)
    return np.ascontiguousarray(a.reshape(ngrp, 128, 4, (per * kc * 128) // 4))


def _wcols(w, cols):
    a = w[:, cols].reshape(16, 128, len(cols)).transpose(1, 0, 2)
    return np.ascontiguousarray(a)


def _gvec(g):
    return np.ascontiguousarray(g.reshape(16, 128).T)


def _run(nc, in_maps):
    res = run_bass_kernel_spmd(nc, in_maps, core_ids=list(range(NCORES)))
    return res.results


def kernel(x, fox_w_in, fox_b_f, fox_q_gain, fox_k_gain, fox_w_out, dil_w_in, dil_q_gain, dil_k_gain, dil_w_out,
           mix_norm_g, mlp_norm_g, mlp_w_up, mlp_w_down):
    f32 = np.float32
    x = np.asarray(x, f32)
    xT = np.ascontiguousarray(x.reshape(NTOK, D).T)
    xT_tl = _tiles(xT)

    w_in = np.asarray(fox_w_in[0], f32)
    H = 16
    fox = _prog("fox", build_fox)
    maps = []
    for c in range(NCORES):
        hA, hB = 2 * c, 2 * c + 1
        qcols = list(range(hA * 128, hA * 128 + 128)) + list(range(hB * 128, hB * 128 + 128))
        kcols = [H * 128 + q for q in qcols]
        vcols = [2 * H * 128 + q for q in qcols] + [3 * H * 128 + hA, 3 * H * 128 + hB]
        maps.append({
            "xT": xT_tl,
            "wq": _wcols(w_in, qcols).reshape(128, 2, 2048),
            "wk": _wcols(w_in, kcols).reshape(128, 2, 2048),
            "wv": _wcols(w_in, vcols[:256]),
            "wf": np.ascontiguousarray(np.pad(_wcols(w_in, vcols[256:]), ((0, 0), (0, 0), (0, 2)))),
            "gmix": _gvec(np.asarray(mix_norm_g[0], f32)),
            "qg": np.asarray(fox_q_gain[0], f32).reshape(128, 1),
            "kg": np.asarray(fox_k_gain[0], f32).reshape(128, 1),
            "bfb": np.ascontiguousarray(np.pad(np.broadcast_to(np.asarray(fox_b_f[0], f32)[[hA, hB]][None, :], (128, 2)), ((0, 0), (0, 2)))),
        })
    r = _run(fox, maps)
    oT_all = np.concatenate([r[c]["oT"].reshape(256, NTOK) for c in range(NCORES)], axis=0)

    mlp_h = _prog("mlp_h", build_mlp, True)
    wout_l = _wchunks(np.asarray(fox_w_out[0], f32), 4, 4)
    wup_l = _wchunks(np.asarray(mlp_w_up[0], f32), 16, 4)
    wdn = np.asarray(mlp_w_down[0], f32)
    wdn_l = np.ascontiguousarray(wdn.reshape(64, 128, 16, 128).transpose(2, 1, 0, 3).reshape(16, 128, 4, 2048))
    maps = []
    for c in range(NCORES):
        sl = slice(c * TPC, (c + 1) * TPC)
        maps.append({
            "xT": _fm(xT[:, sl]), "oT": _fm(oT_all[:, sl]),
            "wout": wout_l, "wup": wup_l, "wdn": wdn_l,
            "g_mlp": _gvec(np.asarray(mlp_norm_g[0], f32)),
            "g_nxt": _gvec(np.asarray(mix_norm_g[1], f32)),
        })
    r = _run(mlp_h, maps)
    x1T = np.concatenate([r[c]["x1T"].reshape(D, TPC) for c in range(NCORES)], axis=1)
    h1T = np.concatenate([r[c]["h1T"].reshape(D, TPC) for c in range(NCORES)], axis=1)
    h1T_tl = _tiles(h1T)

    dil = _prog("dil", build_dil)
    w_in1 = np.asarray(dil_w_in[0], f32)
    G, HD = 3, 8
    maps = []
    ki = np.arange(128)[:, None]
    qi = np.arange(128)[None, :]
    for c in range(NCORES):
        cols = []
        for which in range(2):
            for g in range(G):
                base = which * G * HD * 128 + (g * HD + c) * 128
                cols += list(range(base, base + 128))
        vcols = list(range(2 * G * HD * 128 + c * 256, 2 * G * HD * 128 + (c + 1) * 256))
        bmat = np.empty((128, 3, 256), f32)
        for g in range(G):
            slope = np.float32(2.0) ** (np.float32(-8.0) * np.float32(g * HD + c + 1) / np.float32(G * HD))
            rr = DIL_R[g]
            dprev = (qi + 128 - ki).astype(f32)
            dcur = (qi - ki).astype(f32)
            bmat[:, g, 0:128] = np.where(qi <= ki, -slope * dprev * rr, NEG)
            bmat[:, g, 128:256] = np.where(qi >= ki, -slope * dcur * rr, NEG)
        maps.append({
            "hT": h1T_tl,
            "wqk": _wcols(w_in1, cols),
            "wv": _wcols(w_in1, vcols).reshape(128, 2, 2048),
            "qg": np.ascontiguousarray(np.asarray(dil_q_gain[0], f32).T),
            "kg": np.ascontiguousarray(np.asarray(dil_k_gain[0], f32).T),
            "bmat": bmat,
        })
    r = _run(dil, maps)
    o1T_all = np.concatenate([r[c]["oT"].reshape(256, NTOK) for c in range(NCORES)], axis=0)

    mlp_l = _prog("mlp_l", build_mlp, False)
    wout_l = _wchunks(np.asarray(dil_w_out[0], f32), 4, 4)
    wup_l = _wchunks(np.asarray(mlp_w_up[1], f32), 16, 4)
    wdn = np.asarray(mlp_w_down[1], f32)
    wdn_l = np.ascontiguousarray(wdn.reshape(64, 128, 16, 128).transpose(2, 1, 0, 3).reshape(16, 128, 4, 2048))
    maps = []
    for c in range(NCORES):
        sl = slice(c * TPC, (c + 1) * TPC)
        maps.append({
            "xT": _fm(x1T[:, sl]), "oT": _fm(o1T_all[:, sl]),
            "wout": wout_l, "wup": wup_l, "wdn": wdn_l,
            "g_mlp": _gvec(np.asarray(mlp_norm_g[1], f32)),
        })
    r = _run(mlp_l, maps)
    x2T = np.concatenate([r[c]["x1T"].reshape(D, TPC) for c in range(NCORES)], axis=1)
    return np.ascontiguousarray(x2T.T).reshape(2, S_LEN, D).astype(f32)
```

```python
import numpy as np
from contextlib import ExitStack
import ml_dtypes
import concourse.bass as bass
import concourse.mybir as mybir
from concourse.bass_utils import run_bass_kernel_spmd

F32 = mybir.dt.float32
BF16 = mybir.dt.bfloat16
AF = mybir.ActivationFunctionType
ALU = mybir.AluOpType

NCORES = 8
D = 2048
S_LEN = 8192
NTOK = 16384
TPC = NTOK // NCORES
EPS = 1e-6
NEG = -30000.0

ENGS = ("pe", "act", "dve", "pool", "sp")


class Tile:
    __slots__ = ("name", "last_w", "readers", "dsem", "dcnt")

    def __init__(self, name):
        self.name = name
        self.last_w = None
        self.readers = []
        self.dsem = None
        self.dcnt = 0


class Op:
    __slots__ = ("idx", "eng", "fn", "deps", "dma", "dsem", "dval", "has_dep", "inc", "waits")

    def __init__(self, idx, eng, fn, dma):
        self.idx = idx
        self.eng = eng
        self.fn = fn
        self.deps = set()
        self.dma = dma
        self.dsem = None
        self.dval = 0
        self.has_dep = False
        self.inc = 0
        self.waits = []


class Sched:
    def __init__(self, nc, es):
        self.nc = nc
        self.es = es
        self.ops = []
        self.sem = {e: es.enter_context(nc.semaphore("sem_" + e)) for e in ENGS}
        self.final_dma = []
        self.ntile = 0

    def tile(self, name="t"):
        self.ntile += 1
        return Tile(f"{name}_{self.ntile}")

    def tiles(self, name, n):
        return [self.tile(name) for _ in range(n)]

    def _dma_sem(self, t):
        if t.dsem is None:
            t.dsem = self.es.enter_context(self.nc.semaphore("d_" + t.name))
        return t.dsem

    def op(self, eng, fn, reads=(), writes=(), dma=False, dma_tile=None, final=False):
        o = Op(len(self.ops), eng, fn, dma)
        for t in reads:
            if t.last_w is not None:
                o.deps.add(t.last_w)
        for t in writes:
            if t.last_w is not None:
                o.deps.add(t.last_w)
            o.deps.update(t.readers)
        for t in reads:
            t.readers.append(o.idx)
        for t in writes:
            t.last_w = o.idx
            t.readers = []
        if dma:
            t = dma_tile if dma_tile is not None else (writes[0] if writes else reads[0])
            o.dsem = self._dma_sem(t)
            t.dcnt += 16
            o.dval = t.dcnt
            if final:
                self.final_dma.append(o)
        self.ops.append(o)
        return o

    def finalize(self):
        ops = self.ops
        for o in ops:
            best = {}
            keep = []
            for d in o.deps:
                p = ops[d]
                if p.dma:
                    keep.append(d)
                    continue
                if p.eng == "pe" and o.eng == "pe":
                    continue
                if p.eng not in best or best[p.eng] < d:
                    best[p.eng] = d
            o.deps = keep + list(best.values())
            for d in o.deps:
                ops[d].has_dep = True
        cnt = {e: 0 for e in ENGS}
        for o in ops:
            if not o.dma and o.has_dep:
                cnt[o.eng] += 1
                o.inc = cnt[o.eng]
        known = {e: {} for e in ENGS}
        for o in ops:
            w = {}
            for d in o.deps:
                p = ops[d]
                if p.dma:
                    key = ("d", id(p.dsem))
                    sem, val = p.dsem, p.dval
                else:
                    key = ("e", p.eng)
                    sem, val = self.sem[p.eng], p.inc
                if known[o.eng].get(key, 0) >= val:
                    continue
                if key not in w or w[key][1] < val:
                    w[key] = (sem, val)
            for key, (sem, val) in w.items():
                known[o.eng][key] = val
            o.waits = list(w.values())

    def emit(self, block):
        per = {e: [o for o in self.ops if o.eng == e] for e in ENGS}
        finals = self.final_dma
        sems = self.sem

        def run(h, lst):
            for o in lst:
                for sem, val in o.waits:
                    h.wait_ge(sem, val)
                ins = o.fn(h)
                if o.dma:
                    ins.then_inc(o.dsem, 16)
                elif o.inc:
                    ins.then_inc(sems[o.eng], 1)

        @block.tensor
        def _(e):
            run(e, per["pe"])

        @block.scalar
        def _(e):
            run(e, per["act"])

        @block.vector
        def _(e):
            run(e, per["dve"])

        @block.gpsimd
        def _(e):
            run(e, per["pool"])

        @block.sync
        def _(e):
            run(e, per["sp"])
            for o in finals:
                e.wait_ge(o.dsem, o.dval)


class Ctx:
    def __init__(self, nc, es):
        self.nc = nc
        self.es = es
        self.S = Sched(nc, es)
        self.nalloc = 0

    def sb(self, shape, dt, name="sb"):
        self.nalloc += 1
        return self.es.enter_context(self.nc.sbuf_tensor(f"{name}{self.nalloc}", list(shape), dt))

    def ps(self, shape, dt=F32, name="ps"):
        self.nalloc += 1
        return self.es.enter_context(self.nc.psum_tensor(f"{name}{self.nalloc}", list(shape), dt))

    def dram_in(self, name, shape, dt):
        return self.nc.dram_tensor(name, list(shape), dt, kind="ExternalInput").ap()

    def dram_out(self, name, shape, dt):
        return self.nc.dram_tensor(name, list(shape), dt, kind="ExternalOutput").ap()

    def consts(self):
        S = self.S
        self.ones_bf = self.sb([128, 128], BF16, "ones_bf")
        self.ones_f = self.sb([128, 128], F32, "ones_f")
        self.t_ones_bf = S.tile("ones_bf")
        self.t_ones_f = S.tile("ones_f")
        ob, of = self.ones_bf, self.ones_f
        S.op("dve", lambda e: e.memset(ob[:], 1.0), writes=[self.t_ones_bf])
        S.op("dve", lambda e: e.memset(of[:], 1.0), writes=[self.t_ones_f])


def load_small(cx, dram_ap, shape, dt=F32, name="c", q="sp"):
    t = cx.sb(shape, dt, name)
    tl = cx.S.tile(name)
    cx.S.op(q, lambda e: e.dma_start(out=t[:], in_=dram_ap), writes=[tl], dma=True)
    return t, tl


def load_cast_weight(cx, dram_ap, shape, name):
    t = cx.sb(shape, BF16, name)
    tl = cx.S.tile(name)
    cx.S.op("pool", lambda e: e.dma_start(out=t[:], in_=dram_ap), writes=[tl], dma=True)
    return t, tl


def rstd_from_sumsq(cx, ss_ps, t_ss, n, inv_count, lnb, t_ln, rstd, t_rstd):
    S = cx.S
    eps_t = cx.eps_t
    S.op("act", lambda e: e.activation(out=lnb[:, :n], in_=ss_ps[:, :n], func=AF.Ln, bias=eps_t[:, 0:1], scale=inv_count),
         reads=[t_ss, cx.t_eps], writes=[t_ln])
    S.op("act", lambda e: e.activation(out=rstd[:, :n], in_=lnb[:, :n], func=AF.Exp, scale=-0.5),
         reads=[t_ln], writes=[t_rstd])


def make_eps(cx):
    cx.eps_t = cx.sb([128, 1], F32, "eps")
    cx.t_eps = cx.S.tile("eps")
    et = cx.eps_t
    cx.S.op("dve", lambda e: e.memset(et[:], EPS), writes=[cx.t_eps])


def norm_tile(cx, xt, t_xt, g_sb, t_g, hT, t_hT, n, sqb, t_sq, ss_ps, t_ss, lnb, t_ln, rstd, t_rstd):
    S = cx.S
    S.op("act", lambda e: e.activation(out=sqb[:, :, :n], in_=xt[:, :, :n], func=AF.Square),
         reads=t_xt, writes=[t_sq])
    ob = cx.ones_bf
    for dc in range(16):
        S.op("pe", lambda e, dc=dc: e.matmul(ss_ps[:, :n], ob[:], sqb[:, dc, :n], start=(dc == 0), stop=(dc == 15)),
             reads=[t_sq, cx.t_ones_bf], writes=[t_ss])
    rstd_from_sumsq(cx, ss_ps, t_ss, n, 1.0 / D, lnb, t_ln, rstd, t_rstd)
    for dc in range(16):
        S.op("dve", lambda e, dc=dc: e.scalar_tensor_tensor(out=hT[:, dc, :n], in0=xt[:, dc, :n], scalar=g_sb[:, dc:dc + 1],
                                                             in1=rstd[:, :n], op0=ALU.mult, op1=ALU.mult),
             reads=[t_xt[dc], t_g, t_rstd], writes=[t_hT])


def build_mlp(emit_h):
    nc = bass.Bass("TRN2", target_bir_lowering=False)
    with ExitStack() as es:
        cx = Ctx(nc, es)
        S = cx.S
        TT = 512
        NT = TPC // TT
        xT = cx.dram_in("xT", [16, 128, TPC], F32)
        oT = cx.dram_in("oT", [16, 128, TPC], BF16)
        wout = cx.dram_in("wout", [4, 128, 4, 2048], F32)
        wup = cx.dram_in("wup", [16, 128, 4, 2048], F32)
        wdn = cx.dram_in("wdn", [16, 128, 4, 2048], F32)
        g_mlp = cx.dram_in("g_mlp", [128, 16], F32)
        x1T = cx.dram_out("x1T", [16, 128, TPC], F32)
        if emit_h:
            g_nxt = cx.dram_in("g_nxt", [128, 16], F32)
            h1T = cx.dram_out("h1T", [16, 128, TPC], BF16)
        cx.consts()
        make_eps(cx)
        g_sb, t_g = load_small(cx, g_mlp, [128, 16], F32, "g_mlp")
        if emit_h:
            gn_sb, t_gn = load_small(cx, g_nxt, [128, 16], F32, "g_nxt")

        xt = cx.sb([128, 16, TT], F32, "xt")
        t_xt = S.tiles("xt", 16)
        t_xt_ld = S.tile("xt_ld")
        ot = cx.sb([128, 16, TT], BF16, "ot")
        t_ot = S.tile("ot")
        hT = cx.sb([128, 16, TT], BF16, "hT")
        t_hT = S.tile("hT")
        aT = cx.sb([128, 64, TT], BF16, "aT")
        t_aT = S.tiles("aT", 64)
        sqb = cx.sb([128, 16, TT], BF16, "sqb")
        t_sq = S.tile("sq")
        lnb = cx.sb([128, TT], F32, "lnb")
        t_ln = S.tile("ln")
        rstd = cx.sb([128, TT], F32, "rstd")
        t_rstd = S.tile("rstd")
        r32 = [cx.sb([128, TT], F32, "r32") for _ in range(2)]
        t_r32 = S.tiles("r32", 2)
        NW = 3
        wslot = [cx.sb([128, 4, 2048], BF16, "wslot") for _ in range(NW)]
        t_w = S.tiles("w", NW)
        pbank = [cx.ps([128, 512], F32, "pb") for _ in range(5)]
        t_pb = S.tiles("pb", 5)
        ss_ps = cx.ps([128, 512], F32, "ss")
        t_ss = S.tile("ss")
        wctr = [0]
        pctr = [0]

        def wload(src):
            i = wctr[0] % NW
            wctr[0] += 1
            S.op("pool", lambda e: e.dma_start(out=wslot[i][:], in_=src), writes=[t_w[i]], dma=True)
            return wslot[i], t_w[i]

        def nextbank():
            i = pctr[0] % 5
            pctr[0] += 1
            return pbank[i], t_pb[i]

        for tt in range(NT):
            t0 = tt * TT
            S.op("sp", lambda e, t0=t0: e.dma_start(out=ot[:], in_=oT[:, :, t0:t0 + TT].rearrange("c p t -> p c t")),
                 writes=[t_ot], dma=True)
            S.op("sp", lambda e, t0=t0: e.dma_start(out=xt[:], in_=xT[:, :, t0:t0 + TT].rearrange("c p t -> p c t")),
                 writes=t_xt, dma=True, dma_tile=t_xt_ld)
            for grp in range(4):
                ws, tw = wload(wout[grp])
                for dc4 in range(4):
                    dc = grp * 4 + dc4
                    pb, tpb = nextbank()
                    for kc in range(16):
                        S.op("pe", lambda e, ws=ws, pb=pb, dc4=dc4, kc=kc: e.matmul(
                            pb[:], ws[:, dc4, kc * 128:(kc + 1) * 128], ot[:, kc, :], start=(kc == 0), stop=(kc == 15)),
                            reads=[tw, t_ot], writes=[tpb])
                    S.op("dve", lambda e, pb=pb, dc=dc: e.tensor_tensor(out=xt[:, dc, :], in0=pb[:], in1=xt[:, dc, :], op=ALU.add),
                         reads=[tpb, t_xt[dc]], writes=[t_xt[dc]])
            norm_tile(cx, xt, t_xt, g_sb, t_g, hT, t_hT, TT, sqb, t_sq, ss_ps, t_ss, lnb, t_ln, rstd, t_rstd)
            for grp in range(16):
                ws, tw = wload(wup[grp])
                for fc4 in range(4):
                    fc = grp * 4 + fc4
                    pb, tpb = nextbank()
                    for dc in range(16):
                        S.op("pe", lambda e, ws=ws, pb=pb, fc4=fc4, dc=dc: e.matmul(
                            pb[:], ws[:, fc4, dc * 128:(dc + 1) * 128], hT[:, dc, :], start=(dc == 0), stop=(dc == 15)),
                            reads=[tw, t_hT], writes=[tpb])
                    rb = r32[fc % 2]
                    trb = t_r32[fc % 2]
                    S.op("act", lambda e, pb=pb, rb=rb: e.activation(out=rb[:], in_=pb[:], func=AF.Relu),
                         reads=[tpb], writes=[trb])
                    S.op("dve", lambda e, pb=pb, rb=rb, fc=fc: e.scalar_tensor_tensor(
                        out=aT[:, fc, :], in0=pb[:], scalar=0.0, in1=rb[:], op0=ALU.max, op1=ALU.mult),
                        reads=[tpb, trb], writes=[t_aT[fc]])
            for dc in range(16):
                ws, tw = wload(wdn[dc])
                pb, tpb = nextbank()
                for fc in range(64):
                    S.op("pe", lambda e, ws=ws, pb=pb, fc=fc: e.matmul(
                        pb[:], ws[:, fc // 16, (fc % 16) * 128:(fc % 16 + 1) * 128], aT[:, fc, :], start=(fc == 0), stop=(fc == 63)),
                        reads=[tw, t_aT[fc]], writes=[tpb])
                S.op("dve", lambda e, pb=pb, dc=dc: e.tensor_tensor(out=xt[:, dc, :], in0=pb[:], in1=xt[:, dc, :], op=ALU.add),
                     reads=[tpb, t_xt[dc]], writes=[t_xt[dc]])
            S.op("sp", lambda e, t0=t0: e.dma_start(out=x1T[:, :, t0:t0 + TT].rearrange("c p t -> p c t"), in_=xt[:]),
                 reads=t_xt, dma=True, dma_tile=S.tile("x1st"), final=True)
            if emit_h:
                norm_tile(cx, xt, t_xt, gn_sb, t_gn, hT, t_hT, TT, sqb, t_sq, ss_ps, t_ss, lnb, t_ln, rstd, t_rstd)
                S.op("sp", lambda e, t0=t0: e.dma_start(out=h1T[:, :, t0:t0 + TT].rearrange("c p t -> p c t"), in_=hT[:]),
                     reads=[t_hT], dma=True, dma_tile=S.tile("h1st"), final=True)
        S.finalize()
        with nc.Block() as block:
            S.emit(block)
    return nc


def qk_proj(cx, W, t_W, col0, hT, t_hT, n, pb, tpb, ss2, t_ss2, q32, t_q32, sq32, t_sq32, lnb, t_ln, r2, t_r2,
            gain, t_gain, dst_ap, t_dst):
    S = cx.S
    for dc in range(16):
        S.op("pe", lambda e, dc=dc: e.matmul(pb[:, :n], W[:, dc, col0:col0 + 128], hT[:, dc, :n], start=(dc == 0), stop=(dc == 15)),
             reads=[t_W, t_hT], writes=[tpb])
    S.op("act", lambda e: e.activation(out=q32[:, :n], in_=pb[:, :n], func=AF.Copy), reads=[tpb], writes=[t_q32])
    S.op("dve", lambda e: e.tensor_tensor(out=sq32[:, :n], in0=q32[:, :n], in1=q32[:, :n], op=ALU.mult),
         reads=[t_q32], writes=[t_sq32])
    ofb = cx.ones_bf
    S.op("pe", lambda e: e.matmul(ss2[:, :n], ofb[:], sq32[:, :n], start=True, stop=True),
         reads=[t_sq32, cx.t_ones_bf], writes=[t_ss2])
    rstd_from_sumsq(cx, ss2, t_ss2, n, 1.0 / 128, lnb, t_ln, r2, t_r2)
    S.op("dve", lambda e: e.scalar_tensor_tensor(out=dst_ap, in0=q32[:, :n], scalar=gain, in1=r2[:, :n],
                                                 op0=ALU.mult, op1=ALU.mult),
         reads=[t_q32, t_r2, t_gain], writes=[t_dst])


def build_fox(SL=S_LEN, NB=2, stop=9):
    nc = bass.Bass("TRN2", target_bir_lowering=False)
    with ExitStack() as es:
        cx = Ctx(nc, es)
        S = cx.S
        TA = 256
        NTK = SL * NB
        NJ = SL // 128
        NQB = SL // 512
        xT = cx.dram_in("xT", [NTK // TA, 128, 16 * TA], F32)
        wq_d = cx.dram_in("wq", [128, 2, 2048], F32)
        wk_d = cx.dram_in("wk", [128, 2, 2048], F32)
        wv_d = cx.dram_in("wv", [128, 16, 256], F32)
        wf_d = cx.dram_in("wf", [128, 16, 4], F32)
        gmix_d = cx.dram_in("gmix", [128, 16], F32)
        qg_d = cx.dram_in("qg", [128, 1], F32)
        kg_d = cx.dram_in("kg", [128, 1], F32)
        bf_d = cx.dram_in("bfb", [128, 4], F32)
        oT = cx.dram_out("oT", [2, 128, NTK], BF16)
        cx.consts()
        make_eps(cx)
        g_sb, t_g = load_small(cx, gmix_d, [128, 16], F32, "gmix")
        qg, t_qg0 = load_small(cx, qg_d, [128, 1], F32, "qg")
        kg, t_kg = load_small(cx, kg_d, [128, 1], F32, "kg")
        bfb, t_bfb = load_small(cx, bf_d, [128, 4], F32, "bfb")
        qgs = cx.sb([128, 1], F32, "qgs")
        t_qg = S.tile("qgs")
        S.op("dve", lambda e: e.tensor_scalar(out=qgs[:], in0=qg[:], scalar1=float(128 ** -0.5), scalar2=None, op0=ALU.mult),
             reads=[t_qg0], writes=[t_qg])
        wq_f, t_wq = load_cast_weight(cx, wq_d, [128, 2, 2048], "wq")
        wk_f, t_wk = load_cast_weight(cx, wk_d, [128, 2, 2048], "wk")
        wvf_f = cx.sb([128, 16, 260], BF16, "wvf")
        t_wvf = S.tile("wvf")
        wf32 = cx.sb([128, 16, 4], F32, "wf32")
        t_wf32 = S.tile("wf32")
        S.op("pool", lambda e: e.dma_start(out=wvf_f[:, :, 0:256], in_=wv_d), writes=[t_wvf], dma=True)
        S.op("sp", lambda e: e.dma_start(out=wf32[:], in_=wf_d), writes=[t_wf32], dma=True)
        S.op("dve", lambda e: e.tensor_copy(out=wvf_f[:, :, 256:260], in_=wf32[:]), reads=[t_wf32, t_wvf], writes=[t_wvf])
        wq = wq_f[:].rearrange("p a (c n) -> p (a c) n", n=256)
        wk = wk_f[:].rearrange("p a (c n) -> p (a c) n", n=256)
        wvf = wvf_f[:]
        tri = cx.sb([128, 128], BF16, "tri")
        t_tri = S.tile("tri")
        of = cx.ones_f
        obf_ = cx.ones_bf
        S.op("pool", lambda e: e.affine_select(out=tri[:], in_=obf_[:], pattern=[[1, 128]], compare_op=ALU.is_ge, fill=0.0,
                                               base=0, channel_multiplier=-1),
             reads=[cx.t_ones_bf], writes=[t_tri])

        if stop == 0:
            S.finalize()
            with nc.Block() as block:
                S.emit(block)
            return nc
        xt = [cx.sb([128, 16, TA], F32, "xt") for _ in range(2)]
        t_xt = [S.tiles("xt", 1) * 16 for _ in range(2)]
        hT = [cx.sb([128, 16, TA], BF16, "hT") for _ in range(2)]
        t_hT = [S.tiles("hT", 16) for _ in range(2)]
        sqb = cx.sb([128, 16, TA], BF16, "sqb")
        t_sq = S.tile("sq")
        lnb = [cx.sb([128, TA], F32, "lnb") for _ in range(2)]
        t_ln = S.tiles("ln", 2)
        rstd = cx.sb([128, TA], F32, "rstd")
        t_rstd = S.tile("rstd")
        NQ = 4
        q32 = [cx.sb([128, TA], F32, "q32") for _ in range(NQ)]
        t_q32 = S.tiles("q32", NQ)
        sq32 = [cx.sb([128, TA], BF16, "sq32") for _ in range(NQ)]
        t_sq32 = S.tiles("sq32", NQ)
        r2 = [cx.sb([128, TA], F32, "r2") for _ in range(NQ)]
        t_r2 = S.tiles("r2", NQ)
        QT = [cx.sb([128, SL], BF16, "QT") for _ in range(2)]
        KT = [cx.sb([128, SL], BF16, "KT") for _ in range(2)]
        V = cx.sb([128, 2, NJ, 128], BF16, "V")
        t_QT = S.tiles("QT", 2)
        t_KT = S.tiles("KT", 2)
        t_V = S.tile("V")
        Z = cx.sb([128, NJ, 4], F32, "Z")
        t_Z = S.tile("Z")
        E = cx.sb([128, NJ, 2], F32, "E")
        t_E = S.tile("E")
        SP = cx.sb([128, 2, NJ], F32, "SP")
        t_SP = S.tile("SP")
        SPp = [cx.sb([128, 2, NJ], BF16, "SPp") for _ in range(3)]
        t_SPp = S.tile("SPp")
        SPr = cx.sb([128, 2, NJ], F32, "SPr")
        t_SPr = S.tile("SPr")
        tot = cx.sb([128, NJ], F32, "tot")
        t_tot = S.tile("tot")
        cum = cx.sb([128, NJ], F32, "cum")
        t_cum = S.tile("cum")
        excl = cx.sb([128, NJ], F32, "excl")
        t_excl = S.tile("excl")
        negc = cx.sb([128, NJ], F32, "negc")
        t_negc = S.tile("negc")
        bias = cx.sb([128, NQB, NJ], F32, "bias")
        t_bias = S.tile("bias")
        NP = 4
        pT = [cx.sb([128, 512], BF16, "pT") for _ in range(NP)]
        t_pT = S.tiles("pT", NP)
        rden = cx.sb([128, 512], F32, "rden")
        t_rden = S.tile("rden")
        ob = [cx.sb([128, 512], BF16, "ob") for _ in range(2)]
        t_ob = S.tiles("ob", 2)
        banks = [cx.ps([128, 512], F32, "bk") for _ in range(8)]
        t_bk = S.tiles("bk", 8)
        obf = cx.ones_bf
        NTI = SL // TA

        for b in range(NB):
            pa = [0]
            qc = [0]

            def load_x(ti):
                tok0 = b * SL + ti * TA
                xtb = xt[ti % 2]
                gt = tok0 // TA
                S.op("sp", lambda e, xtb=xtb, gt=gt: e.dma_start(out=xtb[:].rearrange("p c t -> p (c t)"), in_=xT[gt]),
                     writes=[t_xt[ti % 2][0]], dma=True)

            def norm_p1(ti):
                xtb = xt[ti % 2]
                S.op("act", lambda e: e.activation(out=sqb[:], in_=xtb[:], func=AF.Square), reads=t_xt[ti % 2][:1], writes=[t_sq])

            def norm_p2(ti):
                for dc in range(16):
                    S.op("pe", lambda e, dc=dc: e.matmul(banks[7][:, :TA], obf[:], sqb[:, dc, :], start=(dc == 0), stop=(dc == 15)),
                         reads=[t_sq, cx.t_ones_bf], writes=[t_bk[7]])
                rstd_from_sumsq(cx, banks[7], t_bk[7], TA, 1.0 / D, lnb[0], t_ln[0], rstd, t_rstd)

            def norm_p3(ti, lo_=0, hi_=16):
                xtb, hTb = xt[ti % 2], hT[ti % 2]
                for dc in range(lo_, hi_):
                    S.op("dve", lambda e, dc=dc: e.scalar_tensor_tensor(out=hTb[:, dc, :], in0=xtb[:, dc, :], scalar=g_sb[:, dc:dc + 1],
                                                                         in1=rstd[:], op0=ALU.mult, op1=ALU.mult),
                         reads=[t_xt[ti % 2][0], t_g, t_rstd], writes=[t_hT[ti % 2][dc]])

            combos = [(wq, t_wq, QT, t_QT, qgs, t_qg, 0), (wq, t_wq, QT, t_QT, qgs, t_qg, 1),
                      (wk, t_wk, KT, t_KT, kg, t_kg, 0), (wk, t_wk, KT, t_KT, kg, t_kg, 1)]

            def part1(ti, k):
                W, t_W, dstl, t_dstl, gain, t_gain, hd = combos[k]
                hTb = hT[ti % 2]
                pbi = pa[0] % 5
                pa[0] += 1
                qi_ = qc[0] % NQ
                qc[0] += 1
                pb, tpb = banks[pbi], t_bk[pbi]
                for dc in range(16):
                    S.op("pe", lambda e, dc=dc: e.matmul(pb[:, :TA], W[:, dc, hd * 128:hd * 128 + 128], hTb[:, dc, :], start=(dc == 0), stop=(dc == 15)),
                         reads=[t_W, t_hT[ti % 2][dc]], writes=[tpb])
                S.op("act", lambda e: e.activation(out=q32[qi_][:], in_=pb[:, :TA], func=AF.Copy), reads=[tpb], writes=[t_q32[qi_]])
                S.op("pool", lambda e: e.tensor_tensor(out=sq32[qi_][:], in0=q32[qi_][:], in1=q32[qi_][:], op=ALU.mult),
                     reads=[t_q32[qi_]], writes=[t_sq32[qi_]])
                return qi_

            def part2(ti, k, qi_):
                W, t_W, dstl, t_dstl, gain, t_gain, hd = combos[k]
                sb_ = 5 + (qi_ % 2)
                S.op("pe", lambda e: e.matmul(banks[sb_][:, :TA], obf[:], sq32[qi_][:], start=True, stop=True),
                     reads=[t_sq32[qi_], cx.t_ones_bf], writes=[t_bk[sb_]])
                rstd_from_sumsq(cx, banks[sb_], t_bk[sb_], TA, 1.0 / 128, lnb[1], t_ln[1], r2[qi_], t_r2[qi_])
                S.op("dve", lambda e: e.scalar_tensor_tensor(out=dstl[hd][:, ti * TA:(ti + 1) * TA], in0=q32[qi_][:], scalar=gain[:, 0:1],
                                                             in1=r2[qi_][:], op0=ALU.mult, op1=ALU.mult),
                     reads=[t_q32[qi_], t_r2[qi_], t_gain], writes=[t_dstl[hd]])

            def vproj(ti, sub):
                hTb = hT[ti % 2]
                j = ti * (TA // 128) + sub
                pbi = pa[0] % 5
                pa[0] += 1
                pb, tpb = banks[pbi], t_bk[pbi]
                for dc in range(16):
                    S.op("pe", lambda e, dc=dc: e.matmul(pb[:, :260], hTb[:, dc, sub * 128:(sub + 1) * 128], wvf[:, dc, :], start=(dc == 0), stop=(dc == 15)),
                         reads=[t_hT[ti % 2][dc], t_wvf], writes=[tpb])
                S.op("dve", lambda e: e.tensor_tensor(out=Z[:, j, :], in0=pb[:, 256:260], in1=bfb[:], op=ALU.add),
                     reads=[tpb, t_bfb], writes=[t_Z])
                for hd_ in range(2):
                    S.op("act", lambda e, hd_=hd_: e.activation(out=V[:, hd_, j, :], in_=pb[:, hd_ * 128:(hd_ + 1) * 128], func=AF.Copy),
                         reads=[tpb, t_Z], writes=[t_V])

            load_x(0)
            if NTI > 1:
                load_x(1)
            norm_p1(0)
            norm_p2(0)
            norm_p3(0)
            if NTI > 1:
                norm_p1(1)
            for ti in range(NTI):
                nxt = ti + 1 < NTI
                if ti + 2 < NTI:
                    load_x(ti + 2)
                a0 = part1(ti, 0)
                a1 = part1(ti, 1)
                if nxt:
                    norm_p2(ti + 1)
                    norm_p3(ti + 1)
                part2(ti, 0, a0)
                a2 = part1(ti, 2)
                part2(ti, 1, a1)
                a3 = part1(ti, 3)
                part2(ti, 2, a2)
                vproj(ti, 0)
                if ti + 2 < NTI:
                    norm_p1(ti + 2)
                part2(ti, 3, a3)
                vproj(ti, 1)
            if stop == 1:
                break
            S.op("act", lambda e: e.activation(out=E[:], in_=Z[:, :, 0:2], func=AF.Exp, scale=-1.0), reads=[t_Z], writes=[t_E])
            S.op("act", lambda e: e.activation(out=SP[:].rearrange("p h j -> p j h"), in_=E[:], func=AF.Ln, bias=1.0, scale=1.0),
                 reads=[t_E], writes=[t_SP])
            S.op("dve", lambda e: e.tensor_copy(out=SPp[0][:], in_=SP[:]), reads=[t_SP], writes=[t_SPp])
            S.op("dve", lambda e: e.tensor_tensor(out=SPr[:], in0=SP[:], in1=SPp[0][:], op=ALU.subtract), reads=[t_SP, t_SPp], writes=[t_SPr])
            S.op("dve", lambda e: e.tensor_copy(out=SPp[1][:], in_=SPr[:]), reads=[t_SPr], writes=[t_SPp])
            S.op("dve", lambda e: e.tensor_tensor(out=SPr[:], in0=SPr[:], in1=SPp[1][:], op=ALU.subtract), reads=[t_SPr, t_SPp], writes=[t_SPr])
            S.op("dve", lambda e: e.tensor_copy(out=SPp[2][:], in_=SPr[:]), reads=[t_SPr], writes=[t_SPp])
            pc = [0]
            sc = [0]
            for hd in range(2):
                for pc_ in range(3):
                    S.op("pe", lambda e, hd=hd, pc_=pc_: e.matmul(banks[0][:, :NJ], tri[:], SPp[pc_][:, hd, :], start=(pc_ == 0), stop=(pc_ == 2)),
                         reads=[t_tri, t_SPp], writes=[t_bk[0]])
                for pc_ in range(3):
                    S.op("pe", lambda e, hd=hd, pc_=pc_: e.matmul(banks[1][:, :NJ], obf_[:], SPp[pc_][:, hd, :], start=(pc_ == 0), stop=(pc_ == 2)),
                         reads=[cx.t_ones_bf, t_SPp], writes=[t_bk[1]])
                S.op("dve", lambda e: e.tensor_copy(out=tot[:], in_=banks[1][:, :NJ]), reads=[t_bk[1]], writes=[t_tot])
                S.op("dve", lambda e: e.tensor_tensor_scan(out=cum[:], data0=of[:, 0:NJ], data1=tot[:], initial=0.0,
                                                           op0=ALU.mult, op1=ALU.add),
                     reads=[t_tot, cx.t_ones_f], writes=[t_cum])
                S.op("dve", lambda e: e.tensor_tensor(out=excl[:], in0=cum[:], in1=tot[:], op=ALU.subtract),
                     reads=[t_cum, t_tot], writes=[t_excl])
                S.op("dve", lambda e: e.tensor_tensor(out=negc[:], in0=banks[0][:, :NJ], in1=excl[:], op=ALU.add),
                     reads=[t_bk[0], t_excl], writes=[t_negc])
                for qb in range(NQB):
                    nj = 4 * qb + 4
                    S.op("dve", lambda e, qb=qb, nj=nj: e.tensor_scalar(
                        out=bias[:, qb, 0:nj], in0=negc[:, 0:nj], scalar1=excl[:, 4 * qb:4 * qb + 1], scalar2=None, op0=ALU.subtract),
                        reads=[t_negc, t_excl], writes=[t_bias])
                if stop == 2:
                    continue
                pairs = []
                for qb in range(NQB):
                    for j in range(4 * qb + 4):
                        pairs.append((qb, j))
                LA = 2
                slots = {}

                def qk(idx):
                    qb, j = pairs[idx]
                    d = max(0, j - 4 * qb)
                    c0 = 128 * d
                    n = 512 - c0
                    si = sc[0] % 3
                    sc[0] += 1
                    slots[idx] = si
                    q0 = qb * 512
                    S.op("pe", lambda e, hd=hd: e.matmul(banks[si][:, :n], KT[hd][:, j * 128:(j + 1) * 128], QT[hd][:, q0 + c0:q0 + 512], start=True, stop=True),
                         reads=[t_KT[hd], t_QT[hd]], writes=[t_bk[si]])

                for idx in range(min(LA, len(pairs))):
                    qk(idx)
                for idx, (qb, j) in enumerate(pairs):
                    q0 = qb * 512
                    oi = qb % 2
                    ops_, t_ops = banks[3 + oi], t_bk[3 + oi]
                    dps, t_dps = banks[5 + oi], t_bk[5 + oi]
                    nj = 4 * qb + 4
                    d = max(0, j - 4 * qb)
                    c0 = 128 * d
                    n = 512 - c0
                    si = slots.pop(idx)
                    sps, t_sps = banks[si], t_bk[si]
                    pi = pc[0] % NP
                    pc[0] += 1
                    p_, t_p = pT[pi], t_pT[pi]
                    S.op("act", lambda e, sps=sps, p_=p_, qb=qb, j=j, n=n: e.activation(
                        out=p_[:, :n], in_=sps[:, :n], func=AF.Exp, bias=bias[:, qb, j:j + 1], scale=1.0),
                        reads=[t_sps, t_bias], writes=[t_p])
                    if j >= 4 * qb:
                        S.op("pool", lambda e, p_=p_: e.affine_select(
                            out=p_[:, 0:128], in_=p_[:, 0:128], pattern=[[1, 128]], compare_op=ALU.is_ge, fill=0.0,
                            base=0, channel_multiplier=-1), reads=[t_p], writes=[t_p])
                    if idx + LA < len(pairs):
                        qk(idx + LA)
                    S.op("pe", lambda e, ops_=ops_, p_=p_, hd=hd, j=j, c0=c0, n=n, nj=nj: e.matmul(
                        ops_[:, c0:512], V[:, hd, j, :], p_[:, :n], start=(j == 0), stop=(j == nj - 1)),
                        reads=[t_V, t_p], writes=[t_ops])
                    S.op("pe", lambda e, dps=dps, p_=p_, j=j, c0=c0, n=n, nj=nj: e.matmul(
                        dps[:, c0:512], obf[:], p_[:, :n], start=(j == 0), stop=(j == nj - 1)),
                        reads=[cx.t_ones_bf, t_p], writes=[t_dps])
                    if j == nj - 1:
                        S.op("dve", lambda e, dps=dps: e.reciprocal(out=rden[:], in_=dps[:]), reads=[t_dps], writes=[t_rden])
                        obb, t_obb = ob[oi], t_ob[oi]
                        S.op("dve", lambda e, ops_=ops_, obb=obb: e.tensor_tensor(out=obb[:], in0=ops_[:], in1=rden[:], op=ALU.mult),
                             reads=[t_ops, t_rden], writes=[t_obb])
                        tk = b * SL + q0
                        S.op("sp", lambda e, obb=obb, hd=hd, tk=tk: e.dma_start(out=oT[hd, :, tk:tk + 512], in_=obb[:]),
                             reads=[t_obb], dma=True, dma_tile=t_obb, final=True)
        S.finalize()
        with nc.Block() as block:
            S.emit(block)
    return nc


DIL_R = (1, 4, 16)


def build_dil(SL=S_LEN, NB=2):
    nc = bass.Bass("TRN2", target_bir_lowering=False)
    with ExitStack() as es:
        cx = Ctx(nc, es)
        S = cx.S
        TA = 256
        SB = 2048
        NTK = SL * NB
        hT_d = cx.dram_in("hT", [NTK // TA, 128, 16 * TA], BF16)
        wqk_d = cx.dram_in("wqk", [128, 16, 768], F32)
        wv_d = cx.dram_in("wv", [128, 2, 2048], F32)
        qg_d = cx.dram_in("qg", [128, 3], F32)
        kg_d = cx.dram_in("kg", [128, 3], F32)
        bm_d = cx.dram_in("bmat", [128, 3, 256], F32)
        vd = cx.dram_out("vscratch", [NTK, 256], BF16)
        oT = cx.dram_out("oT", [2, 128, NTK], BF16)
        cx.consts()
        make_eps(cx)
        qg, t_qg0 = load_small(cx, qg_d, [128, 3], F32, "qg")
        kg, t_kg = load_small(cx, kg_d, [128, 3], F32, "kg")
        bm, t_bm = load_small(cx, bm_d, [128, 3, 256], F32, "bm")
        qgs = cx.sb([128, 3], F32, "qgs")
        t_qg = S.tile("qgs")
        S.op("dve", lambda e: e.tensor_scalar(out=qgs[:], in0=qg[:], scalar1=float(128 ** -0.5), scalar2=None, op0=ALU.mult),
             reads=[t_qg0], writes=[t_qg])
        wqk_f, t_wqk = load_cast_weight(cx, wqk_d, [128, 16, 768], "wqk")
        wv_f, t_wv = load_cast_weight(cx, wv_d, [128, 2, 2048], "wv")
        wqk = wqk_f[:]
        wv = wv_f[:].rearrange("p a (c n) -> p (a c) n", n=256)

        hT = [cx.sb([128, 16, TA], BF16, "hT") for _ in range(2)]
        t_hT = S.tiles("hT", 2)
        lnb = cx.sb([128, TA], F32, "lnb")
        t_ln = S.tile("ln")
        NQ = 3
        q32 = [cx.sb([128, TA], F32, "q32") for _ in range(NQ)]
        t_q32 = S.tiles("q32", NQ)
        sq32 = [cx.sb([128, TA], BF16, "sq32") for _ in range(NQ)]
        t_sq32 = S.tiles("sq32", NQ)
        r2 = [cx.sb([128, TA], F32, "r2") for _ in range(NQ)]
        t_r2 = S.tiles("r2", NQ)
        Qs = [cx.sb([128, SB], BF16, "Qs") for _ in range(3)]
        t_Qs = S.tiles("Qs", 3)
        Ks = [[cx.sb([128, SB], BF16, "Ks") for _ in range(2)] for _ in range(3)]
        t_Ks = [S.tiles("Ks", 2) for _ in range(3)]
        vst = [cx.sb([128, 256], BF16, "vst") for _ in range(4)]
        t_vst = S.tiles("vst", 4)
        vbuf = [cx.sb([128, 8192], BF16, "vbuf") for _ in range(2)]
        t_vbuf = S.tiles("vbuf", 2)
        acc = cx.sb([128, 3, SB], F32, "acc")
        t_acc = S.tile("acc")
        rden = cx.sb([128, SB], F32, "rden")
        t_rden = S.tile("rden")
        ob = cx.sb([128, 2, SB], BF16, "ob")
        t_ob = S.tile("ob")
        NST = 3
        st = [cx.sb([128, 256], F32, "st") for _ in range(NST)]
        t_st = S.tiles("st", NST)
        pT = [cx.sb([128, 256], BF16, "pT") for _ in range(NST)]
        t_pT = S.tiles("pT", NST)
        banks = [cx.ps([128, 512], F32, "bk") for _ in range(8)]
        t_bk = S.tiles("bk", 8)
        obf = cx.ones_bf
        SPS_B = (0, 1, 2)
        PO_B = (3, 4, 7)

        t_vstore_pool = S.tiles("vstore", 16)
        vctr = [0]
        vbc = [0]
        pa = [0]
        qc = [0]
        NSB = SL // SB
        NTI = SB // TA
        tiles_all = [(b, sb, ti) for b in range(NB) for sb in range(NSB) for ti in range(NTI)]

        def load_h(gi):
            b_, sb_, ti_ = tiles_all[gi]
            tok0 = b_ * SL + sb_ * SB + ti_ * TA
            hTb = hT[gi % 2]
            gt = tok0 // TA
            S.op("sp", lambda e: e.dma_start(out=hTb[:].rearrange("p c t -> p (c t)"), in_=hT_d[gt]),
                 writes=[t_hT[gi % 2]], dma=True)

        load_h(0)
        gi = 0
        for b in range(NB):
            t_vstore = []
            for sb in range(NSB):
                cur = sb % 2
                prv = 1 - cur
                stores_this = []

                def part1(gi, ti, k):
                    which, g = divmod(k, 3)
                    hTb = hT[gi % 2]
                    pbi = pa[0] % 5
                    pa[0] += 1
                    qi_ = qc[0] % NQ
                    qc[0] += 1
                    pb, tpb = banks[pbi], t_bk[pbi]
                    col0 = k * 128
                    for dc in range(16):
                        S.op("pe", lambda e, dc=dc: e.matmul(pb[:, :TA], wqk[:, dc, col0:col0 + 128], hTb[:, dc, :], start=(dc == 0), stop=(dc == 15)),
                             reads=[t_wqk, t_hT[gi % 2]], writes=[tpb])
                    S.op("act", lambda e: e.activation(out=q32[qi_][:], in_=pb[:, :TA], func=AF.Copy), reads=[tpb], writes=[t_q32[qi_]])
                    S.op("pool", lambda e: e.tensor_tensor(out=sq32[qi_][:], in0=q32[qi_][:], in1=q32[qi_][:], op=ALU.mult),
                         reads=[t_q32[qi_]], writes=[t_sq32[qi_]])
                    return qi_

                def part2(gi, ti, k, qi_, cur=cur):
                    which, g = divmod(k, 3)
                    r = DIL_R[g]
                    if which == 0:
                        dst_t, t_dst, gain, t_gain = Qs[g], t_Qs[g], qgs[:, g:g + 1], t_qg
                    else:
                        dst_t, t_dst, gain, t_gain = Ks[g][cur], t_Ks[g][cur], kg[:, g:g + 1], t_kg
                    a0 = ti * TA // r
                    if r == 1:
                        dst_ap = dst_t[:, ti * TA:(ti + 1) * TA]
                        in0 = q32[qi_][:]
                        in1 = r2[qi_][:]
                    else:
                        dst_ap = dst_t[:].rearrange("p (b a) -> p b a", b=r)[:, :, a0:a0 + TA // r]
                        in0 = q32[qi_][:].rearrange("p (a b) -> p b a", b=r)
                        in1 = r2[qi_][:].rearrange("p (a b) -> p b a", b=r)
                    sb_ = 5 + (qi_ % 2)
                    S.op("pe", lambda e: e.matmul(banks[sb_][:, :TA], obf[:], sq32[qi_][:], start=True, stop=True),
                         reads=[t_sq32[qi_], cx.t_ones_bf], writes=[t_bk[sb_]])
                    rstd_from_sumsq(cx, banks[sb_], t_bk[sb_], TA, 1.0 / 128, lnb, t_ln, r2[qi_], t_r2[qi_])
                    S.op("dve", lambda e: e.scalar_tensor_tensor(out=dst_ap, in0=in0, scalar=gain, in1=in1, op0=ALU.mult, op1=ALU.mult),
                         reads=[t_q32[qi_], t_r2[qi_], t_gain], writes=[t_dst])

                def vproj(gi, ti, sub, b=b, sb=sb):
                    hTb = hT[gi % 2]
                    pbi = pa[0] % 5
                    pa[0] += 1
                    pb, tpb = banks[pbi], t_bk[pbi]
                    for dc in range(16):
                        S.op("pe", lambda e, dc=dc: e.matmul(pb[:, :256], hTb[:, dc, sub * 128:(sub + 1) * 128], wv[:, dc, :], start=(dc == 0), stop=(dc == 15)),
                             reads=[t_hT[gi % 2], t_wv], writes=[tpb])
                    vi = vctr[0] % 4
                    vctr[0] += 1
                    S.op("act", lambda e: e.activation(out=vst[vi][:], in_=pb[:, 0:256], func=AF.Copy), reads=[tpb], writes=[t_vst[vi]])
                    tk = b * SL + sb * SB + ti * TA + sub * 128
                    t_store = t_vstore_pool[(ti * (TA // 128) + sub) % 16]
                    S.op("sp", lambda e: e.dma_start(out=vd[tk:tk + 128, :], in_=vst[vi][:]),
                         reads=[t_vst[vi]], writes=[t_store], dma=True, dma_tile=t_store)
                    stores_this.append(t_store)

                for ti in range(NTI):
                    if gi + 1 < len(tiles_all):
                        load_h(gi + 1)
                    a = [None] * 6
                    a[0] = part1(gi, ti, 0)
                    a[1] = part1(gi, ti, 1)
                    for k in range(2, 6):
                        part2(gi, ti, k - 2, a[k - 2])
                        a[k] = part1(gi, ti, k)
                    part2(gi, ti, 4, a[4])
                    vproj(gi, ti, 0)
                    part2(gi, ti, 5, a[5])
                    vproj(gi, ti, 1)
                    gi += 1

                first = True
                has_prev_sb = sb > 0
                lo = -1 if has_prev_sb else 0
                for g in range(3):
                    r = DIL_R[g]
                    nrow = 16 // r
                    Lsb = SB // r
                    vb_i = vbc[0] % 2
                    vbc[0] += 1
                    vb, t_vb = vbuf[vb_i], t_vbuf[vb_i]
                    ntile = nrow - lo
                    base_row = (b * SL + sb * SB) // r
                    vsrc = vd.rearrange("(n x) d -> n (x d)", x=r)
                    r0 = base_row + lo * 128
                    src = vsrc[r0:r0 + ntile * 128, :].rearrange("(n i) x -> i n x", i=128)
                    vview = vb[:, 0:ntile * r * 256].rearrange("p (n x) -> p n x", x=r * 256)
                    deps = list(stores_this) + (list(t_vstore) if has_prev_sb else [])
                    S.op("sp", lambda e, vview=vview, src=src: e.dma_start(out=vview, in_=src),
                         reads=deps, writes=[t_vb], dma=True)
                    accv = acc[:].rearrange("p h (a b) -> p h a b", b=r)
                    blocks = [(rb, n_) for rb in range(r) for n_ in range(nrow)]
                    LA = 3

                    def sps_mm(i, g=g, cur=cur, prv=prv, Lsb=Lsb):
                        rb, n_ = blocks[i]
                        has_prev = has_prev_sb or n_ > 0
                        c_cur = rb * Lsb + n_ * 128
                        qblk = Qs[g][:, c_cur:c_cur + 128]
                        kcur = Ks[g][cur][:, c_cur:c_cur + 128]
                        if n_ > 0:
                            kprev = Ks[g][cur][:, c_cur - 128:c_cur]
                            t_kprev = t_Ks[g][cur]
                        else:
                            kprev = Ks[g][prv][:, rb * Lsb + Lsb - 128:rb * Lsb + Lsb]
                            t_kprev = t_Ks[g][prv]
                        bi_ = SPS_B[i % 3]
                        sps, t_sps = banks[bi_], t_bk[bi_]
                        if has_prev:
                            S.op("pe", lambda e: e.matmul(sps[:, 0:128], kprev, qblk, start=True, stop=True),
                                 reads=[t_kprev, t_Qs[g]], writes=[t_sps])
                        S.op("pe", lambda e: e.matmul(sps[:, 128:256], kcur, qblk, start=True, stop=True),
                             reads=[t_Ks[g][cur], t_Qs[g]], writes=[t_sps])

                    def add_bias(i, g=g):
                        rb, n_ = blocks[i]
                        has_prev = has_prev_sb or n_ > 0
                        c_lo = 0 if has_prev else 128
                        bi_ = SPS_B[i % 3]
                        sps, t_sps = banks[bi_], t_bk[bi_]
                        stb, t_stb = st[i % NST], t_st[i % NST]
                        S.op("dve", lambda e: e.tensor_tensor(
                            out=stb[:, c_lo:256], in0=sps[:, c_lo:256], in1=bm[:, g, c_lo:256], op=ALU.add),
                            reads=[t_sps, t_bm], writes=[t_stb])

                    for i in range(min(LA, len(blocks))):
                        sps_mm(i)
                    add_bias(0)
                    for i, (rb, n_) in enumerate(blocks):
                        has_prev = has_prev_sb or n_ > 0
                        c_lo = 0 if has_prev else 128
                        stb, t_stb = st[i % NST], t_st[i % NST]
                        p_, t_p = pT[i % NST], t_pT[i % NST]
                        S.op("act", lambda e, stb=stb, p_=p_, c_lo=c_lo: e.activation(out=p_[:, c_lo:256], in_=stb[:, c_lo:256], func=AF.Exp),
                             reads=[t_stb], writes=[t_p])
                        if i + 1 < len(blocks):
                            add_bias(i + 1)
                        if i + LA < len(blocks):
                            sps_mm(i + LA)
                        pb_ = PO_B[i % 3]
                        po, t_po = banks[pb_], t_bk[pb_]
                        ti_v = n_ - lo
                        for h in range(3):
                            if h < 2:
                                lc = vview[:, ti_v, rb * 256 + h * 128:rb * 256 + (h + 1) * 128]
                                lp = vview[:, ti_v - 1, rb * 256 + h * 128:rb * 256 + (h + 1) * 128] if has_prev else None
                                rd = [t_vb, t_p]
                            else:
                                lc = obf[:]
                                lp = obf[:]
                                rd = [cx.t_ones_bf, t_p]
                            if has_prev:
                                S.op("pe", lambda e, po=po, lp=lp, p_=p_, h=h: e.matmul(po[:, h * 128:(h + 1) * 128], lp, p_[:, 0:128], start=True, stop=False),
                                     reads=rd, writes=[t_po])
                            S.op("pe", lambda e, po=po, lc=lc, p_=p_, h=h, has_prev=has_prev: e.matmul(
                                po[:, h * 128:(h + 1) * 128], lc, p_[:, 128:256], start=(not has_prev), stop=True),
                                reads=rd, writes=[t_po])
                        dst = accv[:, :, n_ * 128:(n_ + 1) * 128, rb]
                        src_po = po[:, 0:384].rearrange("p (h q) -> p h q", h=3)
                        if first:
                            S.op("dve", lambda e, dst=dst, src_po=src_po: e.tensor_copy(out=dst, in_=src_po),
                                 reads=[t_po], writes=[t_acc])
                        else:
                            S.op("dve", lambda e, dst=dst, src_po=src_po: e.tensor_tensor(out=dst, in0=src_po, in1=dst, op=ALU.add),
                                 reads=[t_po, t_acc], writes=[t_acc])
                    first = False
                t_vstore = stores_this
                S.op("dve", lambda e: e.reciprocal(out=rden[:], in_=acc[:, 2, :]), reads=[t_acc], writes=[t_rden])
                for h in range(2):
                    S.op("dve", lambda e, h=h: e.tensor_tensor(out=ob[:, h, :], in0=acc[:, h, :], in1=rden[:], op=ALU.mult),
                         reads=[t_acc, t_rden], writes=[t_ob])
                tk = b * SL + sb * SB
                t_ost = S.tile("ost")
                S.op("sp", lambda e, tk=tk: e.dma_start(out=oT[:, :, tk:tk + SB].rearrange("h p t -> p h t"), in_=ob[:]),
                     reads=[t_ob], writes=[t_ost], dma=True, dma_tile=t_ost, final=True)
        S.finalize()
        with nc.Block() as block:
            S.emit(block)
    return nc


def qk_proj_perm(cx, W, t_W, col0, hT, t_hT, n, pb, tpb, ss2, t_ss2, q32, t_q32, sq32, t_sq32, lnb, t_ln, r2, t_r2,
                 gain, t_gain, dst_ap, t_dst, r):
    S = cx.S
    for dc in range(16):
        S.op("pe", lambda e, dc=dc: e.matmul(pb[:, :n], W[:, dc, col0:col0 + 128], hT[:, dc, :n], start=(dc == 0), stop=(dc == 15)),
             reads=[t_W, t_hT], writes=[tpb])
    S.op("act", lambda e: e.activation(out=q32[:, :n], in_=pb[:, :n], func=AF.Copy), reads=[tpb], writes=[t_q32])
    S.op("dve", lambda e: e.tensor_tensor(out=sq32[:, :n], in0=q32[:, :n], in1=q32[:, :n], op=ALU.mult),
         reads=[t_q32], writes=[t_sq32])
    ofb = cx.ones_bf
    S.op("pe", lambda e: e.matmul(ss2[:, :n], ofb[:], sq32[:, :n], start=True, stop=True),
         reads=[t_sq32, cx.t_ones_bf], writes=[t_ss2])
    rstd_from_sumsq(cx, ss2, t_ss2, n, 1.0 / 128, lnb, t_ln, r2, t_r2)
    if r == 1:
        in0 = q32[:, :n]
        in1 = r2[:, :n]
    else:
        in0 = q32[:, :n].rearrange("p (a b) -> p b a", b=r)
        in1 = r2[:, :n].rearrange("p (a b) -> p b a", b=r)
    S.op("dve", lambda e: e.scalar_tensor_tensor(out=dst_ap, in0=in0, scalar=gain, in1=in1, op0=ALU.mult, op1=ALU.mult),
         reads=[t_q32, t_r2, t_gain], writes=[t_dst])


_CACHE = {}


def _prog(name, fn, *a):
    if name not in _CACHE:
        _CACHE[name] = fn(*a)
    return _CACHE[name]


def _fm(xT2d):
    return np.ascontiguousarray(xT2d.reshape(16, 128, xT2d.shape[1]))


def _tiles(xT2d, ta=256):
    T = xT2d.shape[1]
    a = xT2d.reshape(16, 128, T // ta, ta).transpose(2, 1, 0, 3)
    return np.ascontiguousarray(a.reshape(T // ta, 128, 16 * ta))


def _wchunks(w, ngrp, per):
    K, N = w.shape
    kc = K // 128
    cb = N // 128
    a = w.reshape(kc, 128, cb, 128)
    a = a.transpose(2, 1, 0, 3)
    a = a.reshape(ngrp, per, 128, kc, 128).transpose(0, 2, 1, 3, 4)
    return np.ascontiguousarray(a.reshape(ngrp, 128, 4, (per * kc * 128) // 4))


def _wcols(w, cols):
    a = w[:, cols].reshape(16, 128, len(cols)).transpose(1, 0, 2)
    return np.ascontiguousarray(a)


def _gvec(g):
    return np.ascontiguousarray(g.reshape(16, 128).T)


def _run(nc, in_maps):
    res = run_bass_kernel_spmd(nc, in_maps, core_ids=list(range(NCORES)))
    return res.results


def kernel(x, fox_w_in, fox_b_f, fox_q_gain, fox_k_gain, fox_w_out, dil_w_in, dil_q_gain, dil_k_gain, dil_w_out,
           mix_norm_g, mlp_norm_g, mlp_w_up, mlp_w_down):
    f32 = np.float32
    x = np.asarray(x, f32)
    xT = np.ascontiguousarray(x.reshape(NTOK, D).T)
    xT_tl = _tiles(xT)

    w_in = np.asarray(fox_w_in[0], f32)
    H = 16
    fox = _prog("fox", build_fox)
    maps = []
    for c in range(NCORES):
        hA, hB = 2 * c, 2 * c + 1
        qcols = list(range(hA * 128, hA * 128 + 128)) + list(range(hB * 128, hB * 128 + 128))
        kcols = [H * 128 + q for q in qcols]
        vcols = [2 * H * 128 + q for q in qcols] + [3 * H * 128 + hA, 3 * H * 128 + hB]
        maps.append({
            "xT": xT_tl,
            "wq": _wcols(w_in, qcols).reshape(128, 2, 2048),
            "wk": _wcols(w_in, kcols).reshape(128, 2, 2048),
            "wv": _wcols(w_in, vcols[:256]),
            "wf": np.ascontiguousarray(np.pad(_wcols(w_in, vcols[256:]), ((0, 0), (0, 0), (0, 2)))),
            "gmix": _gvec(np.asarray(mix_norm_g[0], f32)),
            "qg": np.asarray(fox_q_gain[0], f32).reshape(128, 1),
            "kg": np.asarray(fox_k_gain[0], f32).reshape(128, 1),
            "bfb": np.ascontiguousarray(np.pad(np.broadcast_to(np.asarray(fox_b_f[0], f32)[[hA, hB]][None, :], (128, 2)), ((0, 0), (0, 2)))),
        })
    r = _run(fox, maps)
    oT_all = np.concatenate([r[c]["oT"].reshape(256, NTOK) for c in range(NCORES)], axis=0)

    mlp_h = _prog("mlp_h", build_mlp, True)
    wout_l = _wchunks(np.asarray(fox_w_out[0], f32), 4, 4)
    wup_l = _wchunks(np.asarray(mlp_w_up[0], f32), 16, 4)
    wdn = np.asarray(mlp_w_down[0], f32)
    wdn_l = np.ascontiguousarray(wdn.reshape(64, 128, 16, 128).transpose(2, 1, 0, 3).reshape(16, 128, 4, 2048))
    maps = []
    for c in range(NCORES):
        sl = slice(c * TPC, (c + 1) * TPC)
        maps.append({
            "xT": _fm(xT[:, sl]), "oT": _fm(oT_all[:, sl]),
            "wout": wout_l, "wup": wup_l, "wdn": wdn_l,
            "g_mlp": _gvec(np.asarray(mlp_norm_g[0], f32)),
            "g_nxt": _gvec(np.asarray(mix_norm_g[1], f32)),
        })
    r = _run(mlp_h, maps)
    x1T = np.concatenate([r[c]["x1T"].reshape(D, TPC) for c in range(NCORES)], axis=1)
    h1T = np.concatenate([r[c]["h1T"].reshape(D, TPC) for c in range(NCORES)], axis=1)
    h1T_tl = _tiles(h1T)

    dil = _prog("dil", build_dil)
    w_in1 = np.asarray(dil_w_in[0], f32)
    G, HD = 3, 8
    maps = []
    ki = np.arange(128)[:, None]
    qi = np.arange(128)[None, :]
    for c in range(NCORES):
        cols = []
        for which in range(2):
            for g in range(G):
                base = which * G * HD * 128 + (g * HD + c) * 128
                cols += list(range(base, base + 128))
        vcols = list(range(2 * G * HD * 128 + c * 256, 2 * G * HD * 128 + (c + 1) * 256))
        bmat = np.empty((128, 3, 256), f32)
        for g in range(G):
            slope = np.float32(2.0) ** (np.float32(-8.0) * np.float32(g * HD + c + 1) / np.float32(G * HD))
            rr = DIL_R[g]
            dprev = (qi + 128 - ki).astype(f32)
            dcur = (qi - ki).astype(f32)
            bmat[:, g, 0:128] = np.where(qi <= ki, -slope * dprev * rr, NEG)
            bmat[:, g, 128:256] = np.where(qi >= ki, -slope * dcur * rr, NEG)
        maps.append({
            "hT": h1T_tl,
            "wqk": _wcols(w_in1, cols),
            "wv": _wcols(w_in1, vcols).reshape(128, 2, 2048),
            "qg": np.ascontiguousarray(np.asarray(dil_q_gain[0], f32).T),
            "kg": np.ascontiguousarray(np.asarray(dil_k_gain[0], f32).T),
            "bmat": bmat,
        })
    r = _run(dil, maps)
    o1T_all = np.concatenate([r[c]["oT"].reshape(256, NTOK) for c in range(NCORES)], axis=0)

    mlp_l = _prog("mlp_l", build_mlp, False)
    wout_l = _wchunks(np.asarray(dil_w_out[0], f32), 4, 4)
    wup_l = _wchunks(np.asarray(mlp_w_up[1], f32), 16, 4)
    wdn = np.asarray(mlp_w_down[1], f32)
    wdn_l = np.ascontiguousarray(wdn.reshape(64, 128, 16, 128).transpose(2, 1, 0, 3).reshape(16, 128, 4, 2048))
    maps = []
    for c in range(NCORES):
        sl = slice(c * TPC, (c + 1) * TPC)
        maps.append({
            "xT": _fm(x1T[:, sl]), "oT": _fm(o1T_all[:, sl]),
            "wout": wout_l, "wup": wup_l, "wdn": wdn_l,
            "g_mlp": _gvec(np.asarray(mlp_norm_g[1], f32)),
        })
    r = _run(mlp_l, maps)
    x2T = np.concatenate([r[c]["x1T"].reshape(D, TPC) for c in range(NCORES)], axis=1)
    return np.ascontiguousarray(x2T.T).reshape(2, S_LEN, D).astype(f32)
```

```python
import numpy as np
from contextlib import ExitStack
import ml_dtypes
import concourse.bass as bass
import concourse.mybir as mybir
from concourse.bass_utils import run_bass_kernel_spmd

F32 = mybir.dt.float32
BF16 = mybir.dt.bfloat16
AF = mybir.ActivationFunctionType
ALU = mybir.AluOpType

NCORES = 8
D = 2048
S_LEN = 8192
NTOK = 16384
TPC = NTOK // NCORES
EPS = 1e-6
NEG = -30000.0

ENGS = ("pe", "act", "dve", "pool", "sp")


class Tile:
    __slots__ = ("name", "last_w", "readers", "dsem", "dcnt")

    def __init__(self, name):
        self.name = name
        self.last_w = None
        self.readers = []
        self.dsem = None
        self.dcnt = 0


class Op:
    __slots__ = ("idx", "eng", "fn", "deps", "dma", "dsem", "dval", "has_dep", "inc", "waits")

    def __init__(self, idx, eng, fn, dma):
        self.idx = idx
        self.eng = eng
        self.fn = fn
        self.deps = set()
        self.dma = dma
        self.dsem = None
        self.dval = 0
        self.has_dep = False
        self.inc = 0
        self.waits = []


class Sched:
    def __init__(self, nc, es):
        self.nc = nc
        self.es = es
        self.ops = []
        self.sem = {e: es.enter_context(nc.semaphore("sem_" + e)) for e in ENGS}
        self.final_dma = []
        self.ntile = 0

    def tile(self, name="t"):
        self.ntile += 1
        return Tile(f"{name}_{self.ntile}")

    def tiles(self, name, n):
        return [self.tile(name) for _ in range(n)]

    def _dma_sem(self, t):
        if t.dsem is None:
            t.dsem = self.es.enter_context(self.nc.semaphore("d_" + t.name))
        return t.dsem

    def op(self, eng, fn, reads=(), writes=(), dma=False, dma_tile=None, final=False):
        o = Op(len(self.ops), eng, fn, dma)
        for t in reads:
            if t.last_w is not None:
                o.deps.add(t.last_w)
        for t in writes:
            if t.last_w is not None:
                o.deps.add(t.last_w)
            o.deps.update(t.readers)
        for t in reads:
            t.readers.append(o.idx)
        for t in writes:
            t.last_w = o.idx
            t.readers = []
        if dma:
            t = dma_tile if dma_tile is not None else (writes[0] if writes else reads[0])
            o.dsem = self._dma_sem(t)
            t.dcnt += 16
            o.dval = t.dcnt
            if final:
                self.final_dma.append(o)
        self.ops.append(o)
        return o

    def finalize(self):
        ops = self.ops
        for o in ops:
            best = {}
            keep = []
            for d in o.deps:
                p = ops[d]
                if p.dma:
                    keep.append(d)
                    continue
                if p.eng == "pe" and o.eng == "pe":
                    continue
                if p.eng not in best or best[p.eng] < d:
                    best[p.eng] = d
            o.deps = keep + list(best.values())
            for d in o.deps:
                ops[d].has_dep = True
        cnt = {e: 0 for e in ENGS}
        for o in ops:
            if not o.dma and o.has_dep:
                cnt[o.eng] += 1
                o.inc = cnt[o.eng]
        known = {e: {} for e in ENGS}
        for o in ops:
            w = {}
            for d in o.deps:
                p = ops[d]
                if p.dma:
                    key = ("d", id(p.dsem))
                    sem, val = p.dsem, p.dval
                else:
                    key = ("e", p.eng)
                    sem, val = self.sem[p.eng], p.inc
                if known[o.eng].get(key, 0) >= val:
                    continue
                if key not in w or w[key][1] < val:
                    w[key] = (sem, val)
            for key, (sem, val) in w.items():
                known[o.eng][key] = val
            o.waits = list(w.values())

    def emit(self, block):
        per = {e: [o for o in self.ops if o.eng == e] for e in ENGS}
        finals = self.final_dma
        sems = self.sem

        def run(h, lst):
            for o in lst:
                for sem, val in o.waits:
                    h.wait_ge(sem, val)
                ins = o.fn(h)
                if o.dma:
                    ins.then_inc(o.dsem, 16)
                elif o.inc:
                    ins.then_inc(sems[o.eng], 1)

        @block.tensor
        def _(e):
            run(e, per["pe"])

        @block.scalar
        def _(e):
            run(e, per["act"])

        @block.vector
        def _(e):
            run(e, per["dve"])

        @block.gpsimd
        def _(e):
            run(e, per["pool"])

        @block.sync
        def _(e):
            run(e, per["sp"])
            for o in finals:
                e.wait_ge(o.dsem, o.dval)


class Ctx:
    def __init__(self, nc, es):
        self.nc = nc
        self.es = es
        self.S = Sched(nc, es)
        self.nalloc = 0

    def sb(self, shape, dt, name="sb"):
        self.nalloc += 1
        return self.es.enter_context(self.nc.sbuf_tensor(f"{name}{self.nalloc}", list(shape), dt))

    def ps(self, shape, dt=F32, name="ps"):
        self.nalloc += 1
        return self.es.enter_context(self.nc.psum_tensor(f"{name}{self.nalloc}", list(shape), dt))

    def dram_in(self, name, shape, dt):
        return self.nc.dram_tensor(name, list(shape), dt, kind="ExternalInput").ap()

    def dram_out(self, name, shape, dt):
        return self.nc.dram_tensor(name, list(shape), dt, kind="ExternalOutput").ap()

    def consts(self):
        S = self.S
        self.ones_bf = self.sb([128, 128], BF16, "ones_bf")
        self.ones_f = self.sb([128, 128], F32, "ones_f")
        self.t_ones_bf = S.tile("ones_bf")
        self.t_ones_f = S.tile("ones_f")
        ob, of = self.ones_bf, self.ones_f
        S.op("dve", lambda e: e.memset(ob[:], 1.0), writes=[self.t_ones_bf])
        S.op("dve", lambda e: e.memset(of[:], 1.0), writes=[self.t_ones_f])


def load_small(cx, dram_ap, shape, dt=F32, name="c", q="sp"):
    t = cx.sb(shape, dt, name)
    tl = cx.S.tile(name)
    cx.S.op(q, lambda e: e.dma_start(out=t[:], in_=dram_ap), writes=[tl], dma=True)
    return t, tl


def load_cast_weight(cx, dram_ap, shape, name):
    t = cx.sb(shape, BF16, name)
    tl = cx.S.tile(name)
    cx.S.op("pool", lambda e: e.dma_start(out=t[:], in_=dram_ap), writes=[tl], dma=True)
    return t, tl


def rstd_from_sumsq(cx, ss_ps, t_ss, n, inv_count, lnb, t_ln, rstd, t_rstd):
    S = cx.S
    eps_t = cx.eps_t
    S.op("act", lambda e: e.activation(out=lnb[:, :n], in_=ss_ps[:, :n], func=AF.Ln, bias=eps_t[:, 0:1], scale=inv_count),
         reads=[t_ss, cx.t_eps], writes=[t_ln])
    S.op("act", lambda e: e.activation(out=rstd[:, :n], in_=lnb[:, :n], func=AF.Exp, scale=-0.5),
         reads=[t_ln], writes=[t_rstd])


def make_eps(cx):
    cx.eps_t = cx.sb([128, 1], F32, "eps")
    cx.t_eps = cx.S.tile("eps")
    et = cx.eps_t
    cx.S.op("dve", lambda e: e.memset(et[:], EPS), writes=[cx.t_eps])


def norm_tile(cx, xt, t_xt, g_sb, t_g, hT, t_hT, n, sqb, t_sq, ss_ps, t_ss, lnb, t_ln, rstd, t_rstd):
    S = cx.S
    S.op("act", lambda e: e.activation(out=sqb[:, :, :n], in_=xt[:, :, :n], func=AF.Square),
         reads=t_xt, writes=[t_sq])
    ob = cx.ones_bf
    for dc in range(16):
        S.op("pe", lambda e, dc=dc: e.matmul(ss_ps[:, :n], ob[:], sqb[:, dc, :n], start=(dc == 0), stop=(dc == 15)),
             reads=[t_sq, cx.t_ones_bf], writes=[t_ss])
    rstd_from_sumsq(cx, ss_ps, t_ss, n, 1.0 / D, lnb, t_ln, rstd, t_rstd)
    for dc in range(16):
        S.op("dve", lambda e, dc=dc: e.scalar_tensor_tensor(out=hT[:, dc, :n], in0=xt[:, dc, :n], scalar=g_sb[:, dc:dc + 1],
                                                             in1=rstd[:, :n], op0=ALU.mult, op1=ALU.mult),
             reads=[t_xt[dc], t_g, t_rstd], writes=[t_hT])


def build_mlp(emit_h):
    nc = bass.Bass("TRN2", target_bir_lowering=False)
    with ExitStack() as es:
        cx = Ctx(nc, es)
        S = cx.S
        TT = 512
        NT = TPC // TT
        xT = cx.dram_in("xT", [16, 128, TPC], F32)
        oT = cx.dram_in("oT", [16, 128, TPC], BF16)
        wout = cx.dram_in("wout", [4, 128, 4, 2048], F32)
        wup = cx.dram_in("wup", [16, 128, 4, 2048], F32)
        wdn = cx.dram_in("wdn", [16, 128, 4, 2048], F32)
        g_mlp = cx.dram_in("g_mlp", [128, 16], F32)
        x1T = cx.dram_out("x1T", [16, 128, TPC], F32)
        if emit_h:
            g_nxt = cx.dram_in("g_nxt", [128, 16], F32)
            h1T = cx.dram_out("h1T", [16, 128, TPC], BF16)
        cx.consts()
        make_eps(cx)
        g_sb, t_g = load_small(cx, g_mlp, [128, 16], F32, "g_mlp")
        if emit_h:
            gn_sb, t_gn = load_small(cx, g_nxt, [128, 16], F32, "g_nxt")

        xt = cx.sb([128, 16, TT], F32, "xt")
        t_xt = S.tiles("xt", 16)
        t_xt_ld = S.tile("xt_ld")
        ot = cx.sb([128, 16, TT], BF16, "ot")
        t_ot = S.tile("ot")
        hT = cx.sb([128, 16, TT], BF16, "hT")
        t_hT = S.tile("hT")
        aT = cx.sb([128, 64, TT], BF16, "aT")
        t_aT = S.tiles("aT", 64)
        sqb = cx.sb([128, 16, TT], BF16, "sqb")
        t_sq = S.tile("sq")
        lnb = cx.sb([128, TT], F32, "lnb")
        t_ln = S.tile("ln")
        rstd = cx.sb([128, TT], F32, "rstd")
        t_rstd = S.tile("rstd")
        r32 = [cx.sb([128, TT], F32, "r32") for _ in range(2)]
        t_r32 = S.tiles("r32", 2)
        NW = 3
        wslot = [cx.sb([128, 4, 2048], BF16, "wslot") for _ in range(NW)]
        t_w = S.tiles("w", NW)
        pbank = [cx.ps([128, 512], F32, "pb") for _ in range(5)]
        t_pb = S.tiles("pb", 5)
        ss_ps = cx.ps([128, 512], F32, "ss")
        t_ss = S.tile("ss")
        wctr = [0]
        pctr = [0]

        def wload(src):
            i = wctr[0] % NW
            wctr[0] += 1
            S.op("pool", lambda e: e.dma_start(out=wslot[i][:], in_=src), writes=[t_w[i]], dma=True)
            return wslot[i], t_w[i]

        def nextbank():
            i = pctr[0] % 5
            pctr[0] += 1
            return pbank[i], t_pb[i]

        for tt in range(NT):
            t0 = tt * TT
            S.op("sp", lambda e, t0=t0: e.dma_start(out=ot[:], in_=oT[:, :, t0:t0 + TT].rearrange("c p t -> p c t")),
                 writes=[t_ot], dma=True)
            S.op("sp", lambda e, t0=t0: e.dma_start(out=xt[:], in_=xT[:, :, t0:t0 + TT].rearrange("c p t -> p c t")),
                 writes=t_xt, dma=True, dma_tile=t_xt_ld)
            for grp in range(4):
                ws, tw = wload(wout[grp])
                for dc4 in range(4):
                    dc = grp * 4 + dc4
                    pb, tpb = nextbank()
                    for kc in range(16):
                        S.op("pe", lambda e, ws=ws, pb=pb, dc4=dc4, kc=kc: e.matmul(
                            pb[:], ws[:, dc4, kc * 128:(kc + 1) * 128], ot[:, kc, :], start=(kc == 0), stop=(kc == 15)),
                            reads=[tw, t_ot], writes=[tpb])
                    S.op("dve", lambda e, pb=pb, dc=dc: e.tensor_tensor(out=xt[:, dc, :], in0=pb[:], in1=xt[:, dc, :], op=ALU.add),
                         reads=[tpb, t_xt[dc]], writes=[t_xt[dc]])
            norm_tile(cx, xt, t_xt, g_sb, t_g, hT, t_hT, TT, sqb, t_sq, ss_ps, t_ss, lnb, t_ln, rstd, t_rstd)
            for grp in range(16):
                ws, tw = wload(wup[grp])
                for fc4 in range(4):
                    fc = grp * 4 + fc4
                    pb, tpb = nextbank()
                    for dc in range(16):
                        S.op("pe", lambda e, ws=ws, pb=pb, fc4=fc4, dc=dc: e.matmul(
                            pb[:], ws[:, fc4, dc * 128:(dc + 1) * 128], hT[:, dc, :], start=(dc == 0), stop=(dc == 15)),
                            reads=[tw, t_hT], writes=[tpb])
                    rb = r32[fc % 2]
                    trb = t_r32[fc % 2]
                    S.op("act", lambda e, pb=pb, rb=rb: e.activation(out=rb[:], in_=pb[:], func=AF.Relu),
                         reads=[tpb], writes=[trb])
                    S.op("dve", lambda e, pb=pb, rb=rb, fc=fc: e.scalar_tensor_tensor(
                        out=aT[:, fc, :], in0=pb[:], scalar=0.0, in1=rb[:], op0=ALU.max, op1=ALU.mult),
                        reads=[tpb, trb], writes=[t_aT[fc]])
            for dc in range(16):
                ws, tw = wload(wdn[dc])
                pb, tpb = nextbank()
                for fc in range(64):
                    S.op("pe", lambda e, ws=ws, pb=pb, fc=fc: e.matmul(
                        pb[:], ws[:, fc // 16, (fc % 16) * 128:(fc % 16 + 1) * 128], aT[:, fc, :], start=(fc == 0), stop=(fc == 63)),
                        reads=[tw, t_aT[fc]], writes=[tpb])
                S.op("dve", lambda e, pb=pb, dc=dc: e.tensor_tensor(out=xt[:, dc, :], in0=pb[:], in1=xt[:, dc, :], op=ALU.add),
                     reads=[tpb, t_xt[dc]], writes=[t_xt[dc]])
            S.op("sp", lambda e, t0=t0: e.dma_start(out=x1T[:, :, t0:t0 + TT].rearrange("c p t -> p c t"), in_=xt[:]),
                 reads=t_xt, dma=True, dma_tile=S.tile("x1st"), final=True)
            if emit_h:
                norm_tile(cx, xt, t_xt, gn_sb, t_gn, hT, t_hT, TT, sqb, t_sq, ss_ps, t_ss, lnb, t_ln, rstd, t_rstd)
                S.op("sp", lambda e, t0=t0: e.dma_start(out=h1T[:, :, t0:t0 + TT].rearrange("c p t -> p c t"), in_=hT[:]),
                     reads=[t_hT], dma=True, dma_tile=S.tile("h1st"), final=True)
        S.finalize()
        with nc.Block() as block:
            S.emit(block)
    return nc


def qk_proj(cx, W, t_W, col0, hT, t_hT, n, pb, tpb, ss2, t_ss2, q32, t_q32, sq32, t_sq32, lnb, t_ln, r2, t_r2,
            gain, t_gain, dst_ap, t_dst):
    S = cx.S
    for dc in range(16):
        S.op("pe", lambda e, dc=dc: e.matmul(pb[:, :n], W[:, dc, col0:col0 + 128], hT[:, dc, :n], start=(dc == 0), stop=(dc == 15)),
             reads=[t_W, t_hT], writes=[tpb])
    S.op("act", lambda e: e.activation(out=q32[:, :n], in_=pb[:, :n], func=AF.Copy), reads=[tpb], writes=[t_q32])
    S.op("dve", lambda e: e.tensor_tensor(out=sq32[:, :n], in0=q32[:, :n], in1=q32[:, :n], op=ALU.mult),
         reads=[t_q32], writes=[t_sq32])
    ofb = cx.ones_bf
    S.op("pe", lambda e: e.matmul(ss2[:, :n], ofb[:], sq32[:, :n], start=True, stop=True),
         reads=[t_sq32, cx.t_ones_bf], writes=[t_ss2])
    rstd_from_sumsq(cx, ss2, t_ss2, n, 1.0 / 128, lnb, t_ln, r2, t_r2)
    S.op("dve", lambda e: e.scalar_tensor_tensor(out=dst_ap, in0=q32[:, :n], scalar=gain, in1=r2[:, :n],
                                                 op0=ALU.mult, op1=ALU.mult),
         reads=[t_q32, t_r2, t_gain], writes=[t_dst])


def build_fox(SL=S_LEN, NB=2, stop=9):
    nc = bass.Bass("TRN2", target_bir_lowering=False)
    with ExitStack() as es:
        cx = Ctx(nc, es)
        S = cx.S
        TA = 256
        NTK = SL * NB
        NJ = SL // 128
        NQB = SL // 512
        xT = cx.dram_in("xT", [NTK // TA, 128, 16 * TA], F32)
        wq_d = cx.dram_in("wq", [128, 2, 2048], F32)
        wk_d = cx.dram_in("wk", [128, 2, 2048], F32)
        wv_d = cx.dram_in("wv", [128, 16, 256], F32)
        wf_d = cx.dram_in("wf", [128, 16, 4], F32)
        gmix_d = cx.dram_in("gmix", [128, 16], F32)
        qg_d = cx.dram_in("qg", [128, 1], F32)
        kg_d = cx.dram_in("kg", [128, 1], F32)
        bf_d = cx.dram_in("bfb", [128, 4], F32)
        oT = cx.dram_out("oT", [2, 128, NTK], BF16)
        cx.consts()
        make_eps(cx)
        g_sb, t_g = load_small(cx, gmix_d, [128, 16], F32, "gmix")
        qg, t_qg0 = load_small(cx, qg_d, [128, 1], F32, "qg")
        kg, t_kg = load_small(cx, kg_d, [128, 1], F32, "kg")
        bfb, t_bfb = load_small(cx, bf_d, [128, 4], F32, "bfb")
        qgs = cx.sb([128, 1], F32, "qgs")
        t_qg = S.tile("qgs")
        S.op("dve", lambda e: e.tensor_scalar(out=qgs[:], in0=qg[:], scalar1=float(128 ** -0.5), scalar2=None, op0=ALU.mult),
             reads=[t_qg0], writes=[t_qg])
        wq_f, t_wq = load_cast_weight(cx, wq_d, [128, 2, 2048], "wq")
        wk_f, t_wk = load_cast_weight(cx, wk_d, [128, 2, 2048], "wk")
        wvf_f = cx.sb([128, 16, 260], BF16, "wvf")
        t_wvf = S.tile("wvf")
        wf32 = cx.sb([128, 16, 4], F32, "wf32")
        t_wf32 = S.tile("wf32")
        S.op("pool", lambda e: e.dma_start(out=wvf_f[:, :, 0:256], in_=wv_d), writes=[t_wvf], dma=True)
        S.op("sp", lambda e: e.dma_start(out=wf32[:], in_=wf_d), writes=[t_wf32], dma=True)
        S.op("dve", lambda e: e.tensor_copy(out=wvf_f[:, :, 256:260], in_=wf32[:]), reads=[t_wf32, t_wvf], writes=[t_wvf])
        wq = wq_f[:].rearrange("p a (c n) -> p (a c) n", n=256)
        wk = wk_f[:].rearrange("p a (c n) -> p (a c) n", n=256)
        wvf = wvf_f[:]
        tri = cx.sb([128, 128], BF16, "tri")
        t_tri = S.tile("tri")
        of = cx.ones_f
        obf_ = cx.ones_bf
        S.op("pool", lambda e: e.affine_select(out=tri[:], in_=obf_[:], pattern=[[1, 128]], compare_op=ALU.is_ge, fill=0.0,
                                               base=0, channel_multiplier=-1),
             reads=[cx.t_ones_bf], writes=[t_tri])

        if stop == 0:
            S.finalize()
            with nc.Block() as block:
                S.emit(block)
            return nc
        xt = [cx.sb([128, 16, TA], F32, "xt") for _ in range(2)]
        t_xt = [S.tiles("xt", 1) * 16 for _ in range(2)]
        hT = [cx.sb([128, 16, TA], BF16, "hT") for _ in range(2)]
        t_hT = [S.tiles("hT", 16) for _ in range(2)]
        sqb = cx.sb([128, 16, TA], BF16, "sqb")
        t_sq = S.tile("sq")
        lnb = [cx.sb([128, TA], F32, "lnb") for _ in range(2)]
        t_ln = S.tiles("ln", 2)
        rstd = cx.sb([128, TA], F32, "rstd")
        t_rstd = S.tile("rstd")
        NQ = 4
        q32 = [cx.sb([128, TA], F32, "q32") for _ in range(NQ)]
        t_q32 = S.tiles("q32", NQ)
        sq32 = [cx.sb([128, TA], BF16, "sq32") for _ in range(NQ)]
        t_sq32 = S.tiles("sq32", NQ)
        r2 = [cx.sb([128, TA], F32, "r2") for _ in range(NQ)]
        t_r2 = S.tiles("r2", NQ)
        QT = [cx.sb([128, SL], BF16, "QT") for _ in range(2)]
        KT = [cx.sb([128, SL], BF16, "KT") for _ in range(2)]
        V = cx.sb([128, 2, NJ, 128], BF16, "V")
        t_QT = S.tiles("QT", 2)
        t_KT = S.tiles("KT", 2)
        t_V = S.tile("V")
        Z = cx.sb([128, NJ, 4], F32, "Z")
        t_Z = S.tile("Z")
        E = cx.sb([128, NJ, 2], F32, "E")
        t_E = S.tile("E")
        SP = cx.sb([128, 2, NJ], F32, "SP")
        t_SP = S.tile("SP")
        SPp = [cx.sb([128, 2, NJ], BF16, "SPp") for _ in range(3)]
        t_SPp = S.tile("SPp")
        SPr = cx.sb([128, 2, NJ], F32, "SPr")
        t_SPr = S.tile("SPr")
        tot = cx.sb([128, NJ], F32, "tot")
        t_tot = S.tile("tot")
        cum = cx.sb([128, NJ], F32, "cum")
        t_cum = S.tile("cum")
        excl = cx.sb([128, NJ], F32, "excl")
        t_excl = S.tile("excl")
        negc = cx.sb([128, NJ], F32, "negc")
        t_negc = S.tile("negc")
        bias = cx.sb([128, NQB, NJ], F32, "bias")
        t_bias = S.tile("bias")
        NP = 4
        pT = [cx.sb([128, 512], BF16, "pT") for _ in range(NP)]
        t_pT = S.tiles("pT", NP)
        rden = cx.sb([128, 512], F32, "rden")
        t_rden = S.tile("rden")
        ob = [cx.sb([128, 512], BF16, "ob") for _ in range(2)]
        t_ob = S.tiles("ob", 2)
        banks = [cx.ps([128, 512], F32, "bk") for _ in range(8)]
        t_bk = S.tiles("bk", 8)
        obf = cx.ones_bf
        NTI = SL // TA

        for b in range(NB):
            pa = [0]
            qc = [0]

            def load_x(ti):
                tok0 = b * SL + ti * TA
                xtb = xt[ti % 2]
                gt = tok0 // TA
                S.op("sp", lambda e, xtb=xtb, gt=gt: e.dma_start(out=xtb[:].rearrange("p c t -> p (c t)"), in_=xT[gt]),
                     writes=[t_xt[ti % 2][0]], dma=True)

            def norm_p1(ti):
                xtb = xt[ti % 2]
                S.op("act", lambda e: e.activation(out=sqb[:], in_=xtb[:], func=AF.Square), reads=t_xt[ti % 2][:1], writes=[t_sq])

            def norm_p2(ti):
                for dc in range(16):
                    S.op("pe", lambda e, dc=dc: e.matmul(banks[7][:, :TA], obf[:], sqb[:, dc, :], start=(dc == 0), stop=(dc == 15)),
                         reads=[t_sq, cx.t_ones_bf], writes=[t_bk[7]])
                rstd_from_sumsq(cx, banks[7], t_bk[7], TA, 1.0 / D, lnb[0], t_ln[0], rstd, t_rstd)

            def norm_p3(ti, lo_=0, hi_=16):
                xtb, hTb = xt[ti % 2], hT[ti % 2]
                for dc in range(lo_, hi_):
                    S.op("dve", lambda e, dc=dc: e.scalar_tensor_tensor(out=hTb[:, dc, :], in0=xtb[:, dc, :], scalar=g_sb[:, dc:dc + 1],
                                                                         in1=rstd[:], op0=ALU.mult, op1=ALU.mult),
                         reads=[t_xt[ti % 2][0], t_g, t_rstd], writes=[t_hT[ti % 2][dc]])

            combos = [(wq, t_wq, QT, t_QT, qgs, t_qg, 0), (wq, t_wq, QT, t_QT, qgs, t_qg, 1),
                      (wk, t_wk, KT, t_KT, kg, t_kg, 0), (wk, t_wk, KT, t_KT, kg, t_kg, 1)]

            def part1(ti, k):
                W, t_W, dstl, t_dstl, gain, t_gain, hd = combos[k]
                hTb = hT[ti % 2]
                pbi = pa[0] % 5
                pa[0] += 1
                qi_ = qc[0] % NQ
                qc[0] += 1
                pb, tpb = banks[pbi], t_bk[pbi]
                for dc in range(16):
                    S.op("pe", lambda e, dc=dc: e.matmul(pb[:, :TA], W[:, dc, hd * 128:hd * 128 + 128], hTb[:, dc, :], start=(dc == 0), stop=(dc == 15)),
                         reads=[t_W, t_hT[ti % 2][dc]], writes=[tpb])
                S.op("act", lambda e: e.activation(out=q32[qi_][:], in_=pb[:, :TA], func=AF.Copy), reads=[tpb], writes=[t_q32[qi_]])
                S.op("pool", lambda e: e.tensor_tensor(out=sq32[qi_][:], in0=q32[qi_][:], in1=q32[qi_][:], op=ALU.mult),
                     reads=[t_q32[qi_]], writes=[t_sq32[qi_]])
                return qi_

            def part2(ti, k, qi_):
                W, t_W, dstl, t_dstl, gain, t_gain, hd = combos[k]
                sb_ = 5 + (qi_ % 2)
                S.op("pe", lambda e: e.matmul(banks[sb_][:, :TA], obf[:], sq32[qi_][:], start=True, stop=True),
                     reads=[t_sq32[qi_], cx.t_ones_bf], writes=[t_bk[sb_]])
                rstd_from_sumsq(cx, banks[sb_], t_bk[sb_], TA, 1.0 / 128, lnb[1], t_ln[1], r2[qi_], t_r2[qi_])
                S.op("dve", lambda e: e.scalar_tensor_tensor(out=dstl[hd][:, ti * TA:(ti + 1) * TA], in0=q32[qi_][:], scalar=gain[:, 0:1],
                                                             in1=r2[qi_][:], op0=ALU.mult, op1=ALU.mult),
                     reads=[t_q32[qi_], t_r2[qi_], t_gain], writes=[t_dstl[hd]])

            def vproj(ti, sub):
                hTb = hT[ti % 2]
                j = ti * (TA // 128) + sub
                pbi = pa[0] % 5
                pa[0] += 1
                pb, tpb = banks[pbi], t_bk[pbi]
                for dc in range(16):
                    S.op("pe", lambda e, dc=dc: e.matmul(pb[:, :260], hTb[:, dc, sub * 128:(sub + 1) * 128], wvf[:, dc, :], start=(dc == 0), stop=(dc == 15)),
                         reads=[t_hT[ti % 2][dc], t_wvf], writes=[tpb])
                S.op("dve", lambda e: e.tensor_tensor(out=Z[:, j, :], in0=pb[:, 256:260], in1=bfb[:], op=ALU.add),
                     reads=[tpb, t_bfb], writes=[t_Z])
                for hd_ in range(2):
                    S.op("act", lambda e, hd_=hd_: e.activation(out=V[:, hd_, j, :], in_=pb[:, hd_ * 128:(hd_ + 1) * 128], func=AF.Copy),
                         reads=[tpb, t_Z], writes=[t_V])

            load_x(0)
            if NTI > 1:
                load_x(1)
            norm_p1(0)
            norm_p2(0)
            norm_p3(0)
            if NTI > 1:
                norm_p1(1)
            for ti in range(NTI):
                nxt = ti + 1 < NTI
                if ti + 2 < NTI:
                    load_x(ti + 2)
                a0 = part1(ti, 0)
                a1 = part1(ti, 1)
                if nxt:
                    norm_p2(ti + 1)
                    norm_p3(ti + 1)
                part2(ti, 0, a0)
                a2 = part1(ti, 2)
                part2(ti, 1, a1)
                a3 = part1(ti, 3)
                part2(ti, 2, a2)
                vproj(ti, 0)
                if ti + 2 < NTI:
                    norm_p1(ti + 2)
                part2(ti, 3, a3)
                vproj(ti, 1)
            if stop == 1:
                break
            S.op("act", lambda e: e.activation(out=E[:], in_=Z[:, :, 0:2], func=AF.Exp, scale=-1.0), reads=[t_Z], writes=[t_E])
            S.op("act", lambda e: e.activation(out=SP[:].rearrange("p h j -> p j h"), in_=E[:], func=AF.Ln, bias=1.0, scale=1.0),
                 reads=[t_E], writes=[t_SP])
            S.op("dve", lambda e: e.tensor_copy(out=SPp[0][:], in_=SP[:]), reads=[t_SP], writes=[t_SPp])
            S.op("dve", lambda e: e.tensor_tensor(out=SPr[:], in0=SP[:], in1=SPp[0][:], op=ALU.subtract), reads=[t_SP, t_SPp], writes=[t_SPr])
            S.op("dve", lambda e: e.tensor_copy(out=SPp[1][:], in_=SPr[:]), reads=[t_SPr], writes=[t_SPp])
            S.op("dve", lambda e: e.tensor_tensor(out=SPr[:], in0=SPr[:], in1=SPp[1][:], op=ALU.subtract), reads=[t_SPr, t_SPp], writes=[t_SPr])
            S.op("dve", lambda e: e.tensor_copy(out=SPp[2][:], in_=SPr[:]), reads=[t_SPr], writes=[t_SPp])
            pc = [0]
            sc = [0]
            for hd in range(2):
                for pc_ in range(3):
                    S.op("pe", lambda e, hd=hd, pc_=pc_: e.matmul(banks[0][:, :NJ], tri[:], SPp[pc_][:, hd, :], start=(pc_ == 0), stop=(pc_ == 2)),
                         reads=[t_tri, t_SPp], writes=[t_bk[0]])
                for pc_ in range(3):
                    S.op("pe", lambda e, hd=hd, pc_=pc_: e.matmul(banks[1][:, :NJ], obf_[:], SPp[pc_][:, hd, :], start=(pc_ == 0), stop=(pc_ == 2)),
                         reads=[cx.t_ones_bf, t_SPp], writes=[t_bk[1]])
                S.op("dve", lambda e: e.tensor_copy(out=tot[:], in_=banks[1][:, :NJ]), reads=[t_bk[1]], writes=[t_tot])
                S.op("dve", lambda e: e.tensor_tensor_scan(out=cum[:], data0=of[:, 0:NJ], data1=tot[:], initial=0.0,
                                                           op0=ALU.mult, op1=ALU.add),
                     reads=[t_tot, cx.t_ones_f], writes=[t_cum])
                S.op("dve", lambda e: e.tensor_tensor(out=excl[:], in0=cum[:], in1=tot[:], op=ALU.subtract),
                     reads=[t_cum, t_tot], writes=[t_excl])
                S.op("dve", lambda e: e.tensor_tensor(out=negc[:], in0=banks[0][:, :NJ], in1=excl[:], op=ALU.add),
                     reads=[t_bk[0], t_excl], writes=[t_negc])
                for qb in range(NQB):
                    nj = 4 * qb + 4
                    S.op("dve", lambda e, qb=qb, nj=nj: e.tensor_scalar(
                        out=bias[:, qb, 0:nj], in0=negc[:, 0:nj], scalar1=excl[:, 4 * qb:4 * qb + 1], scalar2=None, op0=ALU.subtract),
                        reads=[t_negc, t_excl], writes=[t_bias])
                if stop == 2:
                    continue
                pairs = []
                for qb in range(NQB):
                    for j in range(4 * qb + 4):
                        pairs.append((qb, j))
                LA = 2
                slots = {}

                def qk(idx):
                    qb, j = pairs[idx]
                    d = max(0, j - 4 * qb)
                    c0 = 128 * d
                    n = 512 - c0
                    si = sc[0] % 3
                    sc[0] += 1
                    slots[idx] = si
                    q0 = qb * 512
                    S.op("pe", lambda e, hd=hd: e.matmul(banks[si][:, :n], KT[hd][:, j * 128:(j + 1) * 128], QT[hd][:, q0 + c0:q0 + 512], start=True, stop=True),
                         reads=[t_KT[hd], t_QT[hd]], writes=[t_bk[si]])

                for idx in range(min(LA, len(pairs))):
                    qk(idx)
                for idx, (qb, j) in enumerate(pairs):
                    q0 = qb * 512
                    oi = qb % 2
                    ops_, t_ops = banks[3 + oi], t_bk[3 + oi]
                    dps, t_dps = banks[5 + oi], t_bk[5 + oi]
                    nj = 4 * qb + 4
                    d = max(0, j - 4 * qb)
                    c0 = 128 * d
                    n = 512 - c0
                    si = slots.pop(idx)
                    sps, t_sps = banks[si], t_bk[si]
                    pi = pc[0] % NP
                    pc[0] += 1
                    p_, t_p = pT[pi], t_pT[pi]
                    S.op("act", lambda e, sps=sps, p_=p_, qb=qb, j=j, n=n: e.activation(
                        out=p_[:, :n], in_=sps[:, :n], func=AF.Exp, bias=bias[:, qb, j:j + 1], scale=1.0),
                        reads=[t_sps, t_bias], writes=[t_p])
                    if j >= 4 * qb:
                        S.op("pool", lambda e, p_=p_: e.affine_select(
                            out=p_[:, 0:128], in_=p_[:, 0:128], pattern=[[1, 128]], compare_op=ALU.is_ge, fill=0.0,
                            base=0, channel_multiplier=-1), reads=[t_p], writes=[t_p])
                    if idx + LA < len(pairs):
                        qk(idx + LA)
                    S.op("pe", lambda e, ops_=ops_, p_=p_, hd=hd, j=j, c0=c0, n=n, nj=nj: e.matmul(
                        ops_[:, c0:512], V[:, hd, j, :], p_[:, :n], start=(j == 0), stop=(j == nj - 1)),
                        reads=[t_V, t_p], writes=[t_ops])
                    S.op("pe", lambda e, dps=dps, p_=p_, j=j, c0=c0, n=n, nj=nj: e.matmul(
                        dps[:, c0:512], obf[:], p_[:, :n], start=(j == 0), stop=(j == nj - 1)),
                        reads=[cx.t_ones_bf, t_p], writes=[t_dps])
                    if j == nj - 1:
                        S.op("dve", lambda e, dps=dps: e.reciprocal(out=rden[:], in_=dps[:]), reads=[t_dps], writes=[t_rden])
                        obb, t_obb = ob[oi], t_ob[oi]
                        S.op("dve", lambda e, ops_=ops_, obb=obb: e.tensor_tensor(out=obb[:], in0=ops_[:], in1=rden[:], op=ALU.mult),
                             reads=[t_ops, t_rden], writes=[t_obb])
                        tk = b * SL + q0
                        S.op("sp", lambda e, obb=obb, hd=hd, tk=tk: e.dma_start(out=oT[hd, :, tk:tk + 512], in_=obb[:]),
                             reads=[t_obb], dma=True, dma_tile=t_obb, final=True)
        S.finalize()
        with nc.Block() as block:
            S.emit(block)
    return nc


DIL_R = (1, 4, 16)


def build_dil(SL=S_LEN, NB=2):
    nc = bass.Bass("TRN2", target_bir_lowering=False)
    with ExitStack() as es:
        cx = Ctx(nc, es)
        S = cx.S
        TA = 256
        SB = 2048
        NTK = SL * NB
        hT_d = cx.dram_in("hT", [NTK // TA, 128, 16 * TA], BF16)
        wqk_d = cx.dram_in("wqk", [128, 16, 768], F32)
        wv_d = cx.dram_in("wv", [128, 2, 2048], F32)
        qg_d = cx.dram_in("qg", [128, 3], F32)
        kg_d = cx.dram_in("kg", [128, 3], F32)
        bm_d = cx.dram_in("bmat", [128, 3, 256], F32)
        vd = cx.dram_out("vscratch", [NTK, 256], BF16)
        oT = cx.dram_out("oT", [2, 128, NTK], BF16)
        cx.consts()
        make_eps(cx)
        qg, t_qg0 = load_small(cx, qg_d, [128, 3], F32, "qg")
        kg, t_kg = load_small(cx, kg_d, [128, 3], F32, "kg")
        bm, t_bm = load_small(cx, bm_d, [128, 3, 256], F32, "bm")
        qgs = cx.sb([128, 3], F32, "qgs")
        t_qg = S.tile("qgs")
        S.op("dve", lambda e: e.tensor_scalar(out=qgs[:], in0=qg[:], scalar1=float(128 ** -0.5), scalar2=None, op0=ALU.mult),
             reads=[t_qg0], writes=[t_qg])
        wqk_f, t_wqk = load_cast_weight(cx, wqk_d, [128, 16, 768], "wqk")
        wv_f, t_wv = load_cast_weight(cx, wv_d, [128, 2, 2048], "wv")
        wqk = wqk_f[:]
        wv = wv_f[:].rearrange("p a (c n) -> p (a c) n", n=256)

        hT = [cx.sb([128, 16, TA], BF16, "hT") for _ in range(2)]
        t_hT = S.tiles("hT", 2)
        lnb = cx.sb([128, TA], F32, "lnb")
        t_ln = S.tile("ln")
        NQ = 3
        q32 = [cx.sb([128, TA], F32, "q32") for _ in range(NQ)]
        t_q32 = S.tiles("q32", NQ)
        sq32 = [cx.sb([128, TA], BF16, "sq32") for _ in range(NQ)]
        t_sq32 = S.tiles("sq32", NQ)
        r2 = [cx.sb([128, TA], F32, "r2") for _ in range(NQ)]
        t_r2 = S.tiles("r2", NQ)
        Qs = [cx.sb([128, SB], BF16, "Qs") for _ in range(3)]
        t_Qs = S.tiles("Qs", 3)
        Ks = [[cx.sb([128, SB], BF16, "Ks") for _ in range(2)] for _ in range(3)]
        t_Ks = [S.tiles("Ks", 2) for _ in range(3)]
        vst = [cx.sb([128, 256], BF16, "vst") for _ in range(4)]
        t_vst = S.tiles("vst", 4)
        vbuf = [cx.sb([128, 8192], BF16, "vbuf") for _ in range(2)]
        t_vbuf = S.tiles("vbuf", 2)
        acc = cx.sb([128, 3, SB], F32, "acc")
        t_acc = S.tile("acc")
        rden = cx.sb([128, SB], F32, "rden")
        t_rden = S.tile("rden")
        ob = cx.sb([128, 2, SB], BF16, "ob")
        t_ob = S.tile("ob")
        NST = 4
        st = [cx.sb([128, 256], F32, "st") for _ in range(NST)]
        t_st = S.tiles("st", NST)
        pT = [cx.sb([128, 256], BF16, "pT") for _ in range(NST)]
        t_pT = S.tiles("pT", NST)
        banks = [cx.ps([128, 512], F32, "bk") for _ in range(8)]
        t_bk = S.tiles("bk", 8)
        obf = cx.ones_bf
        SPS_B = (0, 1, 2, 5)
        PO_B = (3, 4, 7, 6)

        t_vstore_pool = S.tiles("vstore", 16)
        vctr = [0]
        vbc = [0]
        pa = [0]
        qc = [0]
        NSB = SL // SB
        NTI = SB // TA
        tiles_all = [(b, sb, ti) for b in range(NB) for sb in range(NSB) for ti in range(NTI)]

        def load_h(gi):
            b_, sb_, ti_ = tiles_all[gi]
            tok0 = b_ * SL + sb_ * SB + ti_ * TA
            hTb = hT[gi % 2]
            gt = tok0 // TA
            S.op("sp", lambda e: e.dma_start(out=hTb[:].rearrange("p c t -> p (c t)"), in_=hT_d[gt]),
                 writes=[t_hT[gi % 2]], dma=True)

        load_h(0)
        gi = 0
        for b in range(NB):
            t_vstore = []
            for sb in range(NSB):
                cur = sb % 2
                prv = 1 - cur
                stores_this = []

                def part1(gi, ti, k):
                    which, g = divmod(k, 3)
                    hTb = hT[gi % 2]
                    pbi = pa[0] % 5
                    pa[0] += 1
                    qi_ = qc[0] % NQ
                    qc[0] += 1
                    pb, tpb = banks[pbi], t_bk[pbi]
                    col0 = k * 128
                    for dc in range(16):
                        S.op("pe", lambda e, dc=dc: e.matmul(pb[:, :TA], wqk[:, dc, col0:col0 + 128], hTb[:, dc, :], start=(dc == 0), stop=(dc == 15)),
                             reads=[t_wqk, t_hT[gi % 2]], writes=[tpb])
                    S.op("act", lambda e: e.activation(out=q32[qi_][:], in_=pb[:, :TA], func=AF.Copy), reads=[tpb], writes=[t_q32[qi_]])
                    S.op("pool", lambda e: e.tensor_tensor(out=sq32[qi_][:], in0=q32[qi_][:], in1=q32[qi_][:], op=ALU.mult),
                         reads=[t_q32[qi_]], writes=[t_sq32[qi_]])
                    return qi_

                def part2(gi, ti, k, qi_, cur=cur):
                    which, g = divmod(k, 3)
                    r = DIL_R[g]
                    if which == 0:
                        dst_t, t_dst, gain, t_gain = Qs[g], t_Qs[g], qgs[:, g:g + 1], t_qg
                    else:
                        dst_t, t_dst, gain, t_gain = Ks[g][cur], t_Ks[g][cur], kg[:, g:g + 1], t_kg
                    a0 = ti * TA // r
                    if r == 1:
                        dst_ap = dst_t[:, ti * TA:(ti + 1) * TA]
                        in0 = q32[qi_][:]
                        in1 = r2[qi_][:]
                    else:
                        dst_ap = dst_t[:].rearrange("p (b a) -> p b a", b=r)[:, :, a0:a0 + TA // r]
                        in0 = q32[qi_][:].rearrange("p (a b) -> p b a", b=r)
                        in1 = r2[qi_][:].rearrange("p (a b) -> p b a", b=r)
                    sb_ = 5 + (qi_ % 2)
                    S.op("pe", lambda e: e.matmul(banks[sb_][:, :TA], obf[:], sq32[qi_][:], start=True, stop=True),
                         reads=[t_sq32[qi_], cx.t_ones_bf], writes=[t_bk[sb_]])
                    rstd_from_sumsq(cx, banks[sb_], t_bk[sb_], TA, 1.0 / 128, lnb, t_ln, r2[qi_], t_r2[qi_])
                    S.op("dve", lambda e: e.scalar_tensor_tensor(out=dst_ap, in0=in0, scalar=gain, in1=in1, op0=ALU.mult, op1=ALU.mult),
                         reads=[t_q32[qi_], t_r2[qi_], t_gain], writes=[t_dst])

                def vproj(gi, ti, sub, b=b, sb=sb):
                    hTb = hT[gi % 2]
                    pbi = pa[0] % 5
                    pa[0] += 1
                    pb, tpb = banks[pbi], t_bk[pbi]
                    for dc in range(16):
                        S.op("pe", lambda e, dc=dc: e.matmul(pb[:, :256], hTb[:, dc, sub * 128:(sub + 1) * 128], wv[:, dc, :], start=(dc == 0), stop=(dc == 15)),
                             reads=[t_hT[gi % 2], t_wv], writes=[tpb])
                    vi = vctr[0] % 4
                    vctr[0] += 1
                    S.op("act", lambda e: e.activation(out=vst[vi][:], in_=pb[:, 0:256], func=AF.Copy), reads=[tpb], writes=[t_vst[vi]])
                    tk = b * SL + sb * SB + ti * TA + sub * 128
                    t_store = t_vstore_pool[(ti * (TA // 128) + sub) % 16]
                    S.op("sp", lambda e: e.dma_start(out=vd[tk:tk + 128, :], in_=vst[vi][:]),
                         reads=[t_vst[vi]], writes=[t_store], dma=True, dma_tile=t_store)
                    stores_this.append(t_store)

                for ti in range(NTI):
                    if gi + 1 < len(tiles_all):
                        load_h(gi + 1)
                    a = [None] * 6
                    a[0] = part1(gi, ti, 0)
                    a[1] = part1(gi, ti, 1)
                    for k in range(2, 6):
                        part2(gi, ti, k - 2, a[k - 2])
                        a[k] = part1(gi, ti, k)
                    part2(gi, ti, 4, a[4])
                    vproj(gi, ti, 0)
                    part2(gi, ti, 5, a[5])
                    vproj(gi, ti, 1)
                    gi += 1

                first = True
                has_prev_sb = sb > 0
                lo = -1 if has_prev_sb else 0
                for g in range(3):
                    r = DIL_R[g]
                    nrow = 16 // r
                    Lsb = SB // r
                    vb_i = vbc[0] % 2
                    vbc[0] += 1
                    vb, t_vb = vbuf[vb_i], t_vbuf[vb_i]
                    ntile = nrow - lo
                    base_row = (b * SL + sb * SB) // r
                    vsrc = vd.rearrange("(n x) d -> n (x d)", x=r)
                    r0 = base_row + lo * 128
                    src = vsrc[r0:r0 + ntile * 128, :].rearrange("(n i) x -> i n x", i=128)
                    vview = vb[:, 0:ntile * r * 256].rearrange("p (n x) -> p n x", x=r * 256)
                    deps = list(stores_this) + (list(t_vstore) if has_prev_sb else [])
                    S.op("sp", lambda e, vview=vview, src=src: e.dma_start(out=vview, in_=src),
                         reads=deps, writes=[t_vb], dma=True)
                    accv = acc[:].rearrange("p h (a b) -> p h a b", b=r)
                    blocks = [(rb, n_) for rb in range(r) for n_ in range(nrow)]
                    LA = 4

                    def sps_mm(i, g=g, cur=cur, prv=prv, Lsb=Lsb):
                        rb, n_ = blocks[i]
                        has_prev = has_prev_sb or n_ > 0
                        c_cur = rb * Lsb + n_ * 128
                        qblk = Qs[g][:, c_cur:c_cur + 128]
                        kcur = Ks[g][cur][:, c_cur:c_cur + 128]
                        if n_ > 0:
                            kprev = Ks[g][cur][:, c_cur - 128:c_cur]
                            t_kprev = t_Ks[g][cur]
                        else:
                            kprev = Ks[g][prv][:, rb * Lsb + Lsb - 128:rb * Lsb + Lsb]
                            t_kprev = t_Ks[g][prv]
                        bi_ = SPS_B[i % 4]
                        sps, t_sps = banks[bi_], t_bk[bi_]
                        if has_prev:
                            S.op("pe", lambda e: e.matmul(sps[:, 0:128], kprev, qblk, start=True, stop=True),
                                 reads=[t_kprev, t_Qs[g]], writes=[t_sps])
                        S.op("pe", lambda e: e.matmul(sps[:, 128:256], kcur, qblk, start=True, stop=True),
                             reads=[t_Ks[g][cur], t_Qs[g]], writes=[t_sps])

                    def add_bias(i, g=g):
                        rb, n_ = blocks[i]
                        has_prev = has_prev_sb or n_ > 0
                        c_lo = 0 if has_prev else 128
                        bi_ = SPS_B[i % 4]
                        sps, t_sps = banks[bi_], t_bk[bi_]
                        stb, t_stb = st[i % NST], t_st[i % NST]
                        S.op("dve", lambda e: e.tensor_tensor(
                            out=stb[:, c_lo:256], in0=sps[:, c_lo:256], in1=bm[:, g, c_lo:256], op=ALU.add),
                            reads=[t_sps, t_bm], writes=[t_stb])

                    for i in range(min(LA, len(blocks))):
                        sps_mm(i)
                    add_bias(0)
                    if len(blocks) > 1:
                        add_bias(1)
                    for i, (rb, n_) in enumerate(blocks):
                        has_prev = has_prev_sb or n_ > 0
                        c_lo = 0 if has_prev else 128
                        stb, t_stb = st[i % NST], t_st[i % NST]
                        p_, t_p = pT[i % NST], t_pT[i % NST]
                        S.op("act", lambda e, stb=stb, p_=p_, c_lo=c_lo: e.activation(out=p_[:, c_lo:256], in_=stb[:, c_lo:256], func=AF.Exp),
                             reads=[t_stb], writes=[t_p])
                        if i + 2 < len(blocks):
                            add_bias(i + 2)
                        if i + LA < len(blocks):
                            sps_mm(i + LA)
                        pb_ = PO_B[i % 4]
                        po, t_po = banks[pb_], t_bk[pb_]
                        ti_v = n_ - lo
                        for h in range(3):
                            if h < 2:
                                lc = vview[:, ti_v, rb * 256 + h * 128:rb * 256 + (h + 1) * 128]
                                lp = vview[:, ti_v - 1, rb * 256 + h * 128:rb * 256 + (h + 1) * 128] if has_prev else None
                                rd = [t_vb, t_p]
                            else:
                                lc = obf[:]
                                lp = obf[:]
                                rd = [cx.t_ones_bf, t_p]
                            if has_prev:
                                S.op("pe", lambda e, po=po, lp=lp, p_=p_, h=h: e.matmul(po[:, h * 128:(h + 1) * 128], lp, p_[:, 0:128], start=True, stop=False),
                                     reads=rd, writes=[t_po])
                            S.op("pe", lambda e, po=po, lc=lc, p_=p_, h=h, has_prev=has_prev: e.matmul(
                                po[:, h * 128:(h + 1) * 128], lc, p_[:, 128:256], start=(not has_prev), stop=True),
                                reads=rd, writes=[t_po])
                        dst = accv[:, :, n_ * 128:(n_ + 1) * 128, rb]
                        src_po = po[:, 0:384].rearrange("p (h q) -> p h q", h=3)
                        if first:
                            S.op("dve", lambda e, dst=dst, src_po=src_po: e.tensor_copy(out=dst, in_=src_po),
                                 reads=[t_po], writes=[t_acc])
                        else:
                            S.op("dve", lambda e, dst=dst, src_po=src_po: e.tensor_tensor(out=dst, in0=src_po, in1=dst, op=ALU.add),
                                 reads=[t_po, t_acc], writes=[t_acc])
                    first = False
                t_vstore = stores_this
                S.op("dve", lambda e: e.reciprocal(out=rden[:], in_=acc[:, 2, :]), reads=[t_acc], writes=[t_rden])
                for h in range(2):
                    S.op("dve", lambda e, h=h: e.tensor_tensor(out=ob[:, h, :], in0=acc[:, h, :], in1=rden[:], op=ALU.mult),
                         reads=[t_acc, t_rden], writes=[t_ob])
                tk = b * SL + sb * SB
                t_ost = S.tile("ost")
                S.op("sp", lambda e, tk=tk: e.dma_start(out=oT[:, :, tk:tk + SB].rearrange("h p t -> p h t"), in_=ob[:]),
                     reads=[t_ob], writes=[t_ost], dma=True, dma_tile=t_ost, final=True)
        S.finalize()
        with nc.Block() as block:
            S.emit(block)
    return nc


def qk_proj_perm(cx, W, t_W, col0, hT, t_hT, n, pb, tpb, ss2, t_ss2, q32, t_q32, sq32, t_sq32, lnb, t_ln, r2, t_r2,
                 gain, t_gain, dst_ap, t_dst, r):
    S = cx.S
    for dc in range(16):
        S.op("pe", lambda e, dc=dc: e.matmul(pb[:, :n], W[:, dc, col0:col0 + 128], hT[:, dc, :n], start=(dc == 0), stop=(dc == 15)),
             reads=[t_W, t_hT], writes=[tpb])
    S.op("act", lambda e: e.activation(out=q32[:, :n], in_=pb[:, :n], func=AF.Copy), reads=[tpb], writes=[t_q32])
    S.op("dve", lambda e: e.tensor_tensor(out=sq32[:, :n], in0=q32[:, :n], in1=q32[:, :n], op=ALU.mult),
         reads=[t_q32], writes=[t_sq32])
    ofb = cx.ones_bf
    S.op("pe", lambda e: e.matmul(ss2[:, :n], ofb[:], sq32[:, :n], start=True, stop=True),
         reads=[t_sq32, cx.t_ones_bf], writes=[t_ss2])
    rstd_from_sumsq(cx, ss2, t_ss2, n, 1.0 / 128, lnb, t_ln, r2, t_r2)
    if r == 1:
        in0 = q32[:, :n]
        in1 = r2[:, :n]
    else:
        in0 = q32[:, :n].rearrange("p (a b) -> p b a", b=r)
        in1 = r2[:, :n].rearrange("p (a b) -> p b a", b=r)
    S.op("dve", lambda e: e.scalar_tensor_tensor(out=dst_ap, in0=in0, scalar=gain, in1=in1, op0=ALU.mult, op1=ALU.mult),
         reads=[t_q32, t_r2, t_gain], writes=[t_dst])


_CACHE = {}


def _prog(name, fn, *a):
    if name not in _CACHE:
        _CACHE[name] = fn(*a)
    return _CACHE[name]


def _fm(xT2d):
    return np.ascontiguousarray(xT2d.reshape(16, 128, xT2d.shape[1]))


def _tiles(xT2d, ta=256):
    T = xT2d.shape[1]
    a = xT2d.reshape(16, 128, T // ta, ta).transpose(2, 1, 0, 3)
    return np.ascontiguousarray(a.reshape(T // ta, 128, 16 * ta))


def _wchunks(w, ngrp, per):
    K, N = w.shape
    kc = K // 128
    cb = N // 128
    a = w.reshape(kc, 128, cb, 128)
    a = a.transpose(2, 1, 0, 3)
    a = a.reshape(ngrp, per, 128, kc, 128).transpose(0, 2, 1, 3, 4)
    return np.ascontiguousarray(a.reshape(ngrp, 128, 4, (per * kc * 128) // 4))


def _wcols(w, cols):
    a = w[:, cols].reshape(16, 128, len(cols)).transpose(1, 0, 2)
    return np.ascontiguousarray(a)


def _gvec(g):
    return np.ascontiguousarray(g.reshape(16, 128).T)


def _run(nc, in_maps):
    res = run_bass_kernel_spmd(nc, in_maps, core_ids=list(range(NCORES)))
    return res.results


def kernel(x, fox_w_in, fox_b_f, fox_q_gain, fox_k_gain, fox_w_out, dil_w_in, dil_q_gain, dil_k_gain, dil_w_out,
           mix_norm_g, mlp_norm_g, mlp_w_up, mlp_w_down):
    f32 = np.float32
    x = np.asarray(x, f32)
    xT = np.ascontiguousarray(x.reshape(NTOK, D).T)
    xT_tl = _tiles(xT)

    w_in = np.asarray(fox_w_in[0], f32)
    H = 16
    fox = _prog("fox", build_fox)
    maps = []
    for c in range(NCORES):
        hA, hB = 2 * c, 2 * c + 1
        qcols = list(range(hA * 128, hA * 128 + 128)) + list(range(hB * 128, hB * 128 + 128))
        kcols = [H * 128 + q for q in qcols]
        vcols = [2 * H * 128 + q for q in qcols] + [3 * H * 128 + hA, 3 * H * 128 + hB]
        maps.append({
            "xT": xT_tl,
            "wq": _wcols(w_in, qcols).reshape(128, 2, 2048),
            "wk": _wcols(w_in, kcols).reshape(128, 2, 2048),
            "wv": _wcols(w_in, vcols[:256]),
            "wf": np.ascontiguousarray(np.pad(_wcols(w_in, vcols[256:]), ((0, 0), (0, 0), (0, 2)))),
            "gmix": _gvec(np.asarray(mix_norm_g[0], f32)),
            "qg": np.asarray(fox_q_gain[0], f32).reshape(128, 1),
            "kg": np.asarray(fox_k_gain[0], f32).reshape(128, 1),
            "bfb": np.ascontiguousarray(np.pad(np.broadcast_to(np.asarray(fox_b_f[0], f32)[[hA, hB]][None, :], (128, 2)), ((0, 0), (0, 2)))),
        })
    r = _run(fox, maps)
    oT_all = np.concatenate([r[c]["oT"].reshape(256, NTOK) for c in range(NCORES)], axis=0)

    mlp_h = _prog("mlp_h", build_mlp, True)
    wout_l = _wchunks(np.asarray(fox_w_out[0], f32), 4, 4)
    wup_l = _wchunks(np.asarray(mlp_w_up[0], f32), 16, 4)
    wdn = np.asarray(mlp_w_down[0], f32)
    wdn_l = np.ascontiguousarray(wdn.reshape(64, 128, 16, 128).transpose(2, 1, 0, 3).reshape(16, 128, 4, 2048))
    maps = []
    for c in range(NCORES):
        sl = slice(c * TPC, (c + 1) * TPC)
        maps.append({
            "xT": _fm(xT[:, sl]), "oT": _fm(oT_all[:, sl]),
            "wout": wout_l, "wup": wup_l, "wdn": wdn_l,
            "g_mlp": _gvec(np.asarray(mlp_norm_g[0], f32)),
            "g_nxt": _gvec(np.asarray(mix_norm_g[1], f32)),
        })
    r = _run(mlp_h, maps)
    x1T = np.concatenate([r[c]["x1T"].reshape(D, TPC) for c in range(NCORES)], axis=1)
    h1T = np.concatenate([r[c]["h1T"].reshape(D, TPC) for c in range(NCORES)], axis=1)
    h1T_tl = _tiles(h1T)

    dil = _prog("dil", build_dil)
    w_in1 = np.asarray(dil_w_in[0], f32)
    G, HD = 3, 8
    maps = []
    ki = np.arange(128)[:, None]
    qi = np.arange(128)[None, :]
    for c in range(NCORES):
        cols = []
        for which in range(2):
            for g in range(G):
                base = which * G * HD * 128 + (g * HD + c) * 128
                cols += list(range(base, base + 128))
        vcols = list(range(2 * G * HD * 128 + c * 256, 2 * G * HD * 128 + (c + 1) * 256))
        bmat = np.empty((128, 3, 256), f32)
        for g in range(G):
            slope = np.float32(2.0) ** (np.float32(-8.0) * np.float32(g * HD + c + 1) / np.float32(G * HD))
            rr = DIL_R[g]
            dprev = (qi + 128 - ki).astype(f32)
            dcur = (qi - ki).astype(f32)
            bmat[:, g, 0:128] = np.where(qi <= ki, -slope * dprev * rr, NEG)
            bmat[:, g, 128:256] = np.where(qi >= ki, -slope * dcur * rr, NEG)
        maps.append({
            "hT": h1T_tl,
            "wqk": _wcols(w_in1, cols),
            "wv": _wcols(w_in1, vcols).reshape(128, 2, 2048),
            "qg": np.ascontiguousarray(np.asarray(dil_q_gain[0], f32).T),
            "kg": np.ascontiguousarray(np.asarray(dil_k_gain[0], f32).T),
            "bmat": bmat,
        })
    r = _run(dil, maps)
    o1T_all = np.concatenate([r[c]["oT"].reshape(256, NTOK) for c in range(NCORES)], axis=0)

    mlp_l = _prog("mlp_l", build_mlp, False)
    wout_l = _wchunks(np.asarray(dil_w_out[0], f32), 4, 4)
    wup_l = _wchunks(np.asarray(mlp_w_up[1], f32), 16, 4)
    wdn = np.asarray(mlp_w_down[1], f32)
    wdn_l = np.ascontiguousarray(wdn.reshape(64, 128, 16, 128).transpose(2, 1, 0, 3).reshape(16, 128, 4, 2048))
    maps = []
    for c in range(NCORES):
        sl = slice(c * TPC, (c + 1) * TPC)
        maps.append({
            "xT": _fm(x1T[:, sl]), "oT": _fm(o1T_all[:, sl]),
            "wout": wout_l, "wup": wup_l, "wdn": wdn_l,
            "g_mlp": _gvec(np.asarray(mlp_norm_g[1], f32)),
        })
    r = _run(mlp_l, maps)
    x2T = np.concatenate([r[c]["x1T"].reshape(D, TPC) for c in range(NCORES)], axis=1)
    return np.ascontiguousarray(x2T.T).reshape(2, S_LEN, D).astype(f32)
```

```python
import numpy as np
from contextlib import ExitStack
import ml_dtypes
import concourse.bass as bass
import concourse.mybir as mybir
from concourse.bass_utils import run_bass_kernel_spmd

F32 = mybir.dt.float32
BF16 = mybir.dt.bfloat16
AF = mybir.ActivationFunctionType
ALU = mybir.AluOpType

NCORES = 8
D = 2048
S_LEN = 8192
NTOK = 16384
TPC = NTOK // NCORES
EPS = 1e-6
NEG = -30000.0

ENGS = ("pe", "act", "dve", "pool", "sp")


class Tile:
    __slots__ = ("name", "last_w", "readers", "dsem", "dcnt")

    def __init__(self, name):
        self.name = name
        self.last_w = None
        self.readers = []
        self.dsem = None
        self.dcnt = 0


class Op:
    __slots__ = ("idx", "eng", "fn", "deps", "dma", "dsem", "dval", "has_dep", "inc", "waits")

    def __init__(self, idx, eng, fn, dma):
        self.idx = idx
        self.eng = eng
        self.fn = fn
        self.deps = set()
        self.dma = dma
        self.dsem = None
        self.dval = 0
        self.has_dep = False
        self.inc = 0
        self.waits = []


class Sched:
    def __init__(self, nc, es):
        self.nc = nc
        self.es = es
        self.ops = []
        self.sem = {e: es.enter_context(nc.semaphore("sem_" + e)) for e in ENGS}
        self.final_dma = []
        self.ntile = 0

    def tile(self, name="t"):
        self.ntile += 1
        return Tile(f"{name}_{self.ntile}")

    def tiles(self, name, n):
        return [self.tile(name) for _ in range(n)]

    def _dma_sem(self, t):
        if t.dsem is None:
            t.dsem = self.es.enter_context(self.nc.semaphore("d_" + t.name))
        return t.dsem

    def op(self, eng, fn, reads=(), writes=(), dma=False, dma_tile=None, final=False):
        o = Op(len(self.ops), eng, fn, dma)
        for t in reads:
            if t.last_w is not None:
                o.deps.add(t.last_w)
        for t in writes:
            if t.last_w is not None:
                o.deps.add(t.last_w)
            o.deps.update(t.readers)
        for t in reads:
            t.readers.append(o.idx)
        for t in writes:
            t.last_w = o.idx
            t.readers = []
        if dma:
            t = dma_tile if dma_tile is not None else (writes[0] if writes else reads[0])
            o.dsem = self._dma_sem(t)
            t.dcnt += 16
            o.dval = t.dcnt
            if final:
                self.final_dma.append(o)
        self.ops.append(o)
        return o

    def finalize(self):
        ops = self.ops
        for o in ops:
            best = {}
            keep = []
            for d in o.deps:
                p = ops[d]
                if p.dma:
                    keep.append(d)
                    continue
                if p.eng == "pe" and o.eng == "pe":
                    continue
                if p.eng not in best or best[p.eng] < d:
                    best[p.eng] = d
            o.deps = keep + list(best.values())
            for d in o.deps:
                ops[d].has_dep = True
        cnt = {e: 0 for e in ENGS}
        for o in ops:
            if not o.dma and o.has_dep:
                cnt[o.eng] += 1
                o.inc = cnt[o.eng]
        known = {e: {} for e in ENGS}
        for o in ops:
            w = {}
            for d in o.deps:
                p = ops[d]
                if p.dma:
                    key = ("d", id(p.dsem))
                    sem, val = p.dsem, p.dval
                else:
                    key = ("e", p.eng)
                    sem, val = self.sem[p.eng], p.inc
                if known[o.eng].get(key, 0) >= val:
                    continue
                if key not in w or w[key][1] < val:
                    w[key] = (sem, val)
            for key, (sem, val) in w.items():
                known[o.eng][key] = val
            o.waits = list(w.values())

    def emit(self, block):
        per = {e: [o for o in self.ops if o.eng == e] for e in ENGS}
        finals = self.final_dma
        sems = self.sem

        def run(h, lst):
            for o in lst:
                for sem, val in o.waits:
                    h.wait_ge(sem, val)
                ins = o.fn(h)
                if o.dma:
                    ins.then_inc(o.dsem, 16)
                elif o.inc:
                    ins.then_inc(sems[o.eng], 1)

        @block.tensor
        def _(e):
            run(e, per["pe"])

        @block.scalar
        def _(e):
            run(e, per["act"])

        @block.vector
        def _(e):
            run(e, per["dve"])

        @block.gpsimd
        def _(e):
            run(e, per["pool"])

        @block.sync
        def _(e):
            run(e, per["sp"])
            for o in finals:
                e.wait_ge(o.dsem, o.dval)


class Ctx:
    def __init__(self, nc, es):
        self.nc = nc
        self.es = es
        self.S = Sched(nc, es)
        self.nalloc = 0

    def sb(self, shape, dt, name="sb"):
        self.nalloc += 1
        return self.es.enter_context(self.nc.sbuf_tensor(f"{name}{self.nalloc}", list(shape), dt))

    def ps(self, shape, dt=F32, name="ps"):
        self.nalloc += 1
        return self.es.enter_context(self.nc.psum_tensor(f"{name}{self.nalloc}", list(shape), dt))

    def dram_in(self, name, shape, dt):
        return self.nc.dram_tensor(name, list(shape), dt, kind="ExternalInput").ap()

    def dram_out(self, name, shape, dt):
        return self.nc.dram_tensor(name, list(shape), dt, kind="ExternalOutput").ap()

    def consts(self):
        S = self.S
        self.ones_bf = self.sb([128, 128], BF16, "ones_bf")
        self.ones_f = self.sb([128, 128], F32, "ones_f")
        self.t_ones_bf = S.tile("ones_bf")
        self.t_ones_f = S.tile("ones_f")
        ob, of = self.ones_bf, self.ones_f
        S.op("dve", lambda e: e.memset(ob[:], 1.0), writes=[self.t_ones_bf])
        S.op("dve", lambda e: e.memset(of[:], 1.0), writes=[self.t_ones_f])


def load_small(cx, dram_ap, shape, dt=F32, name="c", q="sp"):
    t = cx.sb(shape, dt, name)
    tl = cx.S.tile(name)
    cx.S.op(q, lambda e: e.dma_start(out=t[:], in_=dram_ap), writes=[tl], dma=True)
    return t, tl


def load_cast_weight(cx, dram_ap, shape, name):
    t = cx.sb(shape, BF16, name)
    tl = cx.S.tile(name)
    cx.S.op("pool", lambda e: e.dma_start(out=t[:], in_=dram_ap), writes=[tl], dma=True)
    return t, tl


def rstd_from_sumsq(cx, ss_ps, t_ss, n, inv_count, lnb, t_ln, rstd, t_rstd):
    S = cx.S
    eps_t = cx.eps_t
    S.op("act", lambda e: e.activation(out=lnb[:, :n], in_=ss_ps[:, :n], func=AF.Ln, bias=eps_t[:, 0:1], scale=inv_count),
         reads=[t_ss, cx.t_eps], writes=[t_ln])
    S.op("act", lambda e: e.activation(out=rstd[:, :n], in_=lnb[:, :n], func=AF.Exp, scale=-0.5),
         reads=[t_ln], writes=[t_rstd])


def make_eps(cx):
    cx.eps_t = cx.sb([128, 1], F32, "eps")
    cx.t_eps = cx.S.tile("eps")
    et = cx.eps_t
    cx.S.op("dve", lambda e: e.memset(et[:], EPS), writes=[cx.t_eps])


def norm_tile(cx, xt, t_xt, g_sb, t_g, hT, t_hT, n, sqb, t_sq, ss_ps, t_ss, lnb, t_ln, rstd, t_rstd):
    S = cx.S
    S.op("act", lambda e: e.activation(out=sqb[:, :, :n], in_=xt[:, :, :n], func=AF.Square),
         reads=t_xt, writes=[t_sq])
    ob = cx.ones_bf
    for dc in range(16):
        S.op("pe", lambda e, dc=dc: e.matmul(ss_ps[:, :n], ob[:], sqb[:, dc, :n], start=(dc == 0), stop=(dc == 15)),
             reads=[t_sq, cx.t_ones_bf], writes=[t_ss])
    rstd_from_sumsq(cx, ss_ps, t_ss, n, 1.0 / D, lnb, t_ln, rstd, t_rstd)
    for dc in range(16):
        S.op("dve", lambda e, dc=dc: e.scalar_tensor_tensor(out=hT[:, dc, :n], in0=xt[:, dc, :n], scalar=g_sb[:, dc:dc + 1],
                                                             in1=rstd[:, :n], op0=ALU.mult, op1=ALU.mult),
             reads=[t_xt[dc], t_g, t_rstd], writes=[t_hT])


def build_mlp(emit_h):
    nc = bass.Bass("TRN2", target_bir_lowering=False)
    with ExitStack() as es:
        cx = Ctx(nc, es)
        S = cx.S
        TT = 512
        NT = TPC // TT
        xT = cx.dram_in("xT", [16, 128, TPC], F32)
        oT = cx.dram_in("oT", [16, 128, TPC], BF16)
        wout = cx.dram_in("wout", [4, 128, 4, 2048], F32)
        wup = cx.dram_in("wup", [16, 128, 4, 2048], F32)
        wdn = cx.dram_in("wdn", [16, 128, 4, 2048], F32)
        g_mlp = cx.dram_in("g_mlp", [128, 16], F32)
        x1T = cx.dram_out("x1T", [16, 128, TPC], F32)
        if emit_h:
            g_nxt = cx.dram_in("g_nxt", [128, 16], F32)
            h1T = cx.dram_out("h1T", [16, 128, TPC], BF16)
        cx.consts()
        make_eps(cx)
        g_sb, t_g = load_small(cx, g_mlp, [128, 16], F32, "g_mlp")
        if emit_h:
            gn_sb, t_gn = load_small(cx, g_nxt, [128, 16], F32, "g_nxt")

        xt = cx.sb([128, 16, TT], F32, "xt")
        t_xt = S.tiles("xt", 16)
        t_xt_ld = S.tiles("xt_ld", 16)
        t_x1st = S.tiles("x1st", 16)
        ot = cx.sb([128, 16, TT], BF16, "ot")
        t_ot = S.tile("ot")
        hT = cx.sb([128, 16, TT], BF16, "hT")
        t_hT = S.tiles("hT", 16)
        t_h1st = S.tile("h1st")
        aT = cx.sb([128, 64, TT], BF16, "aT")
        t_aT = S.tiles("aT", 64)
        sqb = cx.sb([128, 16, TT], BF16, "sqb")
        t_sq = S.tiles("sq", 16)
        lnb = cx.sb([128, TT], F32, "lnb")
        t_ln = S.tile("ln")
        rstd = cx.sb([128, TT], F32, "rstd")
        t_rstd = S.tile("rstd")
        r32 = [cx.sb([128, TT], F32, "r32") for _ in range(2)]
        t_r32 = S.tiles("r32", 2)
        NW = 3
        wslot = [cx.sb([128, 4, 2048], BF16, "wslot") for _ in range(NW)]
        t_w = S.tiles("w", NW)
        pbank = [cx.ps([128, 512], F32, "pb") for _ in range(5)]
        t_pb = S.tiles("pb", 5)
        ss_ps = cx.ps([128, 512], F32, "ss")
        t_ss = S.tile("ss")
        wctr = [0]
        pctr = [0]
        obf = cx.ones_bf

        def wload(src):
            i = wctr[0] % NW
            wctr[0] += 1
            S.op("pool", lambda e: e.dma_start(out=wslot[i][:], in_=src), writes=[t_w[i]], dma=True)
            return wslot[i], t_w[i]

        def nextbank():
            i = pctr[0] % 5
            pctr[0] += 1
            return pbank[i], t_pb[i]

        def load_o(tt):
            t0 = tt * TT
            S.op("sp", lambda e: e.dma_start(out=ot[:], in_=oT[:, :, t0:t0 + TT].rearrange("c p t -> p c t")),
                 writes=[t_ot], dma=True)

        def load_x(tt):
            t0 = tt * TT
            for dc in range(16):
                S.op("sp", lambda e, dc=dc: e.dma_start(out=xt[:, dc, :], in_=xT[dc, :, t0:t0 + TT]),
                     writes=[t_xt[dc]], dma=True, dma_tile=t_xt_ld[dc])

        def square(dc):
            S.op("act", lambda e: e.activation(out=sqb[:, dc, :], in_=xt[:, dc, :], func=AF.Square),
                 reads=[t_xt[dc]], writes=[t_sq[dc]])

        def sumsq(dc):
            S.op("pe", lambda e: e.matmul(ss_ps[:], obf[:], sqb[:, dc, :], start=(dc == 0), stop=(dc == 15)),
                 reads=[t_sq[dc], cx.t_ones_bf], writes=[t_ss])

        def norm_finish(gs, t_gs):
            rstd_from_sumsq(cx, ss_ps, t_ss, TT, 1.0 / D, lnb, t_ln, rstd, t_rstd)
            for dc in range(16):
                S.op("dve", lambda e, dc=dc: e.scalar_tensor_tensor(out=hT[:, dc, :], in0=xt[:, dc, :], scalar=gs[:, dc:dc + 1],
                                                                     in1=rstd[:], op0=ALU.mult, op1=ALU.mult),
                     reads=[t_xt[dc], t_gs, t_rstd], writes=[t_hT[dc]])

        LAG = 2
        load_o(0)
        load_x(0)
        for tt in range(NT):
            t0 = tt * TT
            for grp in range(4):
                ws, tw = wload(wout[grp])
                for dc4 in range(4):
                    dc = grp * 4 + dc4
                    pb, tpb = nextbank()
                    for kc in range(16):
                        S.op("pe", lambda e, ws=ws, pb=pb, dc4=dc4, kc=kc: e.matmul(
                            pb[:], ws[:, dc4, kc * 128:(kc + 1) * 128], ot[:, kc, :], start=(kc == 0), stop=(kc == 15)),
                            reads=[tw, t_ot], writes=[tpb])
                    S.op("dve", lambda e, pb=pb, dc=dc: e.tensor_tensor(out=xt[:, dc, :], in0=pb[:], in1=xt[:, dc, :], op=ALU.add),
                         reads=[tpb, t_xt[dc]], writes=[t_xt[dc]])
                    square(dc)
                    if dc >= LAG:
                        sumsq(dc - LAG)
            for dc in range(16 - LAG, 16):
                sumsq(dc)
            if tt + 1 < NT:
                load_o(tt + 1)
            norm_finish(g_sb, t_g)
            for grp in range(16):
                ws, tw = wload(wup[grp])
                for fc4 in range(4):
                    fc = grp * 4 + fc4
                    pb, tpb = nextbank()
                    for dc in range(16):
                        S.op("pe", lambda e, ws=ws, pb=pb, fc4=fc4, dc=dc: e.matmul(
                            pb[:], ws[:, fc4, dc * 128:(dc + 1) * 128], hT[:, dc, :], start=(dc == 0), stop=(dc == 15)),
                            reads=[tw, t_hT[dc]], writes=[tpb])
                    rb = r32[fc % 2]
                    trb = t_r32[fc % 2]
                    S.op("act", lambda e, pb=pb, rb=rb: e.activation(out=rb[:], in_=pb[:], func=AF.Relu),
                         reads=[tpb], writes=[trb])
                    S.op("dve", lambda e, pb=pb, rb=rb, fc=fc: e.scalar_tensor_tensor(
                        out=aT[:, fc, :], in0=pb[:], scalar=0.0, in1=rb[:], op0=ALU.max, op1=ALU.mult),
                        reads=[tpb, trb], writes=[t_aT[fc]])
            for dc in range(16):
                ws, tw = wload(wdn[dc])
                pb, tpb = nextbank()
                for fc in range(64):
                    S.op("pe", lambda e, ws=ws, pb=pb, fc=fc: e.matmul(
                        pb[:], ws[:, fc // 16, (fc % 16) * 128:(fc % 16 + 1) * 128], aT[:, fc, :], start=(fc == 0), stop=(fc == 63)),
                        reads=[tw, t_aT[fc]], writes=[tpb])
                S.op("dve", lambda e, pb=pb, dc=dc: e.tensor_tensor(out=xt[:, dc, :], in0=pb[:], in1=xt[:, dc, :], op=ALU.add),
                     reads=[tpb, t_xt[dc]], writes=[t_xt[dc]])
                S.op("sp", lambda e, dc=dc, t0=t0: e.dma_start(out=x1T[dc, :, t0:t0 + TT], in_=xt[:, dc, :]),
                     reads=[t_xt[dc]], dma=True, dma_tile=t_x1st[dc], final=True)
                if emit_h:
                    square(dc)
                    if dc >= 1:
                        sumsq(dc - 1)
            if emit_h:
                sumsq(15)
                norm_finish(gn_sb, t_gn)
                S.op("sp", lambda e, t0=t0: e.dma_start(out=h1T[:, :, t0:t0 + TT].rearrange("c p t -> p c t"), in_=hT[:]),
                     reads=t_hT, dma=True, dma_tile=t_h1st, final=True)
            if tt + 1 < NT:
                load_x(tt + 1)
        S.finalize()
        with nc.Block() as block:
            S.emit(block)
    return nc


def qk_proj(cx, W, t_W, col0, hT, t_hT, n, pb, tpb, ss2, t_ss2, q32, t_q32, sq32, t_sq32, lnb, t_ln, r2, t_r2,
            gain, t_gain, dst_ap, t_dst):
    S = cx.S
    for dc in range(16):
        S.op("pe", lambda e, dc=dc: e.matmul(pb[:, :n], W[:, dc, col0:col0 + 128], hT[:, dc, :n], start=(dc == 0), stop=(dc == 15)),
             reads=[t_W, t_hT], writes=[tpb])
    S.op("act", lambda e: e.activation(out=q32[:, :n], in_=pb[:, :n], func=AF.Copy), reads=[tpb], writes=[t_q32])
    S.op("dve", lambda e: e.tensor_tensor(out=sq32[:, :n], in0=q32[:, :n], in1=q32[:, :n], op=ALU.mult),
         reads=[t_q32], writes=[t_sq32])
    ofb = cx.ones_bf
    S.op("pe", lambda e: e.matmul(ss2[:, :n], ofb[:], sq32[:, :n], start=True, stop=True),
         reads=[t_sq32, cx.t_ones_bf], writes=[t_ss2])
    rstd_from_sumsq(cx, ss2, t_ss2, n, 1.0 / 128, lnb, t_ln, r2, t_r2)
    S.op("dve", lambda e: e.scalar_tensor_tensor(out=dst_ap, in0=q32[:, :n], scalar=gain, in1=r2[:, :n],
                                                 op0=ALU.mult, op1=ALU.mult),
         reads=[t_q32, t_r2, t_gain], writes=[t_dst])


def build_fox(SL=S_LEN, NB=2, stop=9):
    nc = bass.Bass("TRN2", target_bir_lowering=False)
    with ExitStack() as es:
        cx = Ctx(nc, es)
        S = cx.S
        TA = 256
        NTK = SL * NB
        NJ = SL // 128
        NQB = SL // 512
        xT = cx.dram_in("xT", [NTK // TA, 128, 16 * TA], F32)
        wq_d = cx.dram_in("wq", [128, 2, 2048], F32)
        wk_d = cx.dram_in("wk", [128, 2, 2048], F32)
        wv_d = cx.dram_in("wv", [128, 16, 256], F32)
        wf_d = cx.dram_in("wf", [128, 16, 4], F32)
        gmix_d = cx.dram_in("gmix", [128, 16], F32)
        qg_d = cx.dram_in("qg", [128, 1], F32)
        kg_d = cx.dram_in("kg", [128, 1], F32)
        bf_d = cx.dram_in("bfb", [128, 4], F32)
        oT = cx.dram_out("oT", [2, 128, NTK], BF16)
        cx.consts()
        make_eps(cx)
        g_sb, t_g = load_small(cx, gmix_d, [128, 16], F32, "gmix")
        qg, t_qg0 = load_small(cx, qg_d, [128, 1], F32, "qg")
        kg, t_kg = load_small(cx, kg_d, [128, 1], F32, "kg")
        bfb, t_bfb = load_small(cx, bf_d, [128, 4], F32, "bfb")
        qgs = cx.sb([128, 1], F32, "qgs")
        t_qg = S.tile("qgs")
        S.op("dve", lambda e: e.tensor_scalar(out=qgs[:], in0=qg[:], scalar1=float(128 ** -0.5), scalar2=None, op0=ALU.mult),
             reads=[t_qg0], writes=[t_qg])
        wq_f, t_wq = load_cast_weight(cx, wq_d, [128, 2, 2048], "wq")
        wk_f, t_wk = load_cast_weight(cx, wk_d, [128, 2, 2048], "wk")
        wvf_f = cx.sb([128, 16, 260], BF16, "wvf")
        t_wvf = S.tile("wvf")
        wf32 = cx.sb([128, 16, 4], F32, "wf32")
        t_wf32 = S.tile("wf32")
        S.op("pool", lambda e: e.dma_start(out=wvf_f[:, :, 0:256], in_=wv_d), writes=[t_wvf], dma=True)
        S.op("sp", lambda e: e.dma_start(out=wf32[:], in_=wf_d), writes=[t_wf32], dma=True)
        S.op("dve", lambda e: e.tensor_copy(out=wvf_f[:, :, 256:260], in_=wf32[:]), reads=[t_wf32, t_wvf], writes=[t_wvf])
        wq = wq_f[:].rearrange("p a (c n) -> p (a c) n", n=256)
        wk = wk_f[:].rearrange("p a (c n) -> p (a c) n", n=256)
        wvf = wvf_f[:]
        tri = cx.sb([128, 128], BF16, "tri")
        t_tri = S.tile("tri")
        of = cx.ones_f
        obf_ = cx.ones_bf
        S.op("pool", lambda e: e.affine_select(out=tri[:], in_=obf_[:], pattern=[[1, 128]], compare_op=ALU.is_ge, fill=0.0,
                                               base=0, channel_multiplier=-1),
             reads=[cx.t_ones_bf], writes=[t_tri])

        if stop == 0:
            S.finalize()
            with nc.Block() as block:
                S.emit(block)
            return nc
        xt = [cx.sb([128, 16, TA], F32, "xt") for _ in range(2)]
        t_xt = [S.tiles("xt", 1) * 16 for _ in range(2)]
        hT = [cx.sb([128, 16, TA], BF16, "hT") for _ in range(2)]
        t_hT = [S.tiles("hT", 16) for _ in range(2)]
        sqb = cx.sb([128, 16, TA], BF16, "sqb")
        t_sq = S.tile("sq")
        lnb = [cx.sb([128, TA], F32, "lnb") for _ in range(2)]
        t_ln = S.tiles("ln", 2)
        rstd = cx.sb([128, TA], F32, "rstd")
        t_rstd = S.tile("rstd")
        NQ = 4
        q32 = [cx.sb([128, TA], F32, "q32") for _ in range(NQ)]
        t_q32 = S.tiles("q32", NQ)
        sq32 = [cx.sb([128, TA], BF16, "sq32") for _ in range(NQ)]
        t_sq32 = S.tiles("sq32", NQ)
        r2 = [cx.sb([128, TA], F32, "r2") for _ in range(NQ)]
        t_r2 = S.tiles("r2", NQ)
        QT = [cx.sb([128, SL], BF16, "QT") for _ in range(2)]
        KT = [cx.sb([128, SL], BF16, "KT") for _ in range(2)]
        V = cx.sb([128, 2, NJ, 128], BF16, "V")
        t_QT = S.tiles("QT", 2)
        t_KT = S.tiles("KT", 2)
        t_V = S.tile("V")
        Z = cx.sb([128, NJ, 4], F32, "Z")
        t_Z = S.tile("Z")
        E = cx.sb([128, NJ, 2], F32, "E")
        t_E = S.tile("E")
        SP = cx.sb([128, 2, NJ], F32, "SP")
        t_SP = S.tile("SP")
        SPp = [cx.sb([128, 2, NJ], BF16, "SPp") for _ in range(3)]
        t_SPp = S.tile("SPp")
        SPr = cx.sb([128, 2, NJ], F32, "SPr")
        t_SPr = S.tile("SPr")
        tot = cx.sb([128, NJ], F32, "tot")
        t_tot = S.tile("tot")
        cum = cx.sb([128, NJ], F32, "cum")
        t_cum = S.tile("cum")
        excl = cx.sb([128, NJ], F32, "excl")
        t_excl = S.tile("excl")
        negc = cx.sb([128, NJ], F32, "negc")
        t_negc = S.tile("negc")
        bias = cx.sb([128, NQB, NJ], F32, "bias")
        t_bias = S.tile("bias")
        NP = 4
        pT = [cx.sb([128, 512], BF16, "pT") for _ in range(NP)]
        t_pT = S.tiles("pT", NP)
        rden = cx.sb([128, 512], F32, "rden")
        t_rden = S.tile("rden")
        ob = [cx.sb([128, 512], BF16, "ob") for _ in range(2)]
        t_ob = S.tiles("ob", 2)
        banks = [cx.ps([128, 512], F32, "bk") for _ in range(8)]
        t_bk = S.tiles("bk", 8)
        obf = cx.ones_bf
        NTI = SL // TA

        for b in range(NB):
            pa = [0]
            qc = [0]

            def load_x(ti):
                tok0 = b * SL + ti * TA
                xtb = xt[ti % 2]
                gt = tok0 // TA
                S.op("sp", lambda e, xtb=xtb, gt=gt: e.dma_start(out=xtb[:].rearrange("p c t -> p (c t)"), in_=xT[gt]),
                     writes=[t_xt[ti % 2][0]], dma=True)

            def norm_p1(ti):
                xtb = xt[ti % 2]
                S.op("act", lambda e: e.activation(out=sqb[:], in_=xtb[:], func=AF.Square), reads=t_xt[ti % 2][:1], writes=[t_sq])

            def norm_p2(ti):
                for dc in range(16):
                    S.op("pe", lambda e, dc=dc: e.matmul(banks[7][:, :TA], obf[:], sqb[:, dc, :], start=(dc == 0), stop=(dc == 15)),
                         reads=[t_sq, cx.t_ones_bf], writes=[t_bk[7]])
                rstd_from_sumsq(cx, banks[7], t_bk[7], TA, 1.0 / D, lnb[0], t_ln[0], rstd, t_rstd)

            def norm_p3(ti, lo_=0, hi_=16):
                xtb, hTb = xt[ti % 2], hT[ti % 2]
                for dc in range(lo_, hi_):
                    S.op("dve", lambda e, dc=dc: e.scalar_tensor_tensor(out=hTb[:, dc, :], in0=xtb[:, dc, :], scalar=g_sb[:, dc:dc + 1],
                                                                         in1=rstd[:], op0=ALU.mult, op1=ALU.mult),
                         reads=[t_xt[ti % 2][0], t_g, t_rstd], writes=[t_hT[ti % 2][dc]])

            combos = [(wq, t_wq, QT, t_QT, qgs, t_qg, 0), (wq, t_wq, QT, t_QT, qgs, t_qg, 1),
                      (wk, t_wk, KT, t_KT, kg, t_kg, 0), (wk, t_wk, KT, t_KT, kg, t_kg, 1)]

            def part1(ti, k):
                W, t_W, dstl, t_dstl, gain, t_gain, hd = combos[k]
                hTb = hT[ti % 2]
                pbi = pa[0] % 5
                pa[0] += 1
                qi_ = qc[0] % NQ
                qc[0] += 1
                pb, tpb = banks[pbi], t_bk[pbi]
                for dc in range(16):
                    S.op("pe", lambda e, dc=dc: e.matmul(pb[:, :TA], W[:, dc, hd * 128:hd * 128 + 128], hTb[:, dc, :], start=(dc == 0), stop=(dc == 15)),
                         reads=[t_W, t_hT[ti % 2][dc]], writes=[tpb])
                S.op("act", lambda e: e.activation(out=q32[qi_][:], in_=pb[:, :TA], func=AF.Copy), reads=[tpb], writes=[t_q32[qi_]])
                S.op("pool", lambda e: e.tensor_tensor(out=sq32[qi_][:], in0=q32[qi_][:], in1=q32[qi_][:], op=ALU.mult),
                     reads=[t_q32[qi_]], writes=[t_sq32[qi_]])
                return qi_

            def part2(ti, k, qi_):
                W, t_W, dstl, t_dstl, gain, t_gain, hd = combos[k]
                sb_ = 5 + (qi_ % 2)
                S.op("pe", lambda e: e.matmul(banks[sb_][:, :TA], obf[:], sq32[qi_][:], start=True, stop=True),
                     reads=[t_sq32[qi_], cx.t_ones_bf], writes=[t_bk[sb_]])
                rstd_from_sumsq(cx, banks[sb_], t_bk[sb_], TA, 1.0 / 128, lnb[1], t_ln[1], r2[qi_], t_r2[qi_])
                S.op("dve", lambda e: e.scalar_tensor_tensor(out=dstl[hd][:, ti * TA:(ti + 1) * TA], in0=q32[qi_][:], scalar=gain[:, 0:1],
                                                             in1=r2[qi_][:], op0=ALU.mult, op1=ALU.mult),
                     reads=[t_q32[qi_], t_r2[qi_], t_gain], writes=[t_dstl[hd]])

            def vproj(ti, sub):
                hTb = hT[ti % 2]
                j = ti * (TA // 128) + sub
                pbi = pa[0] % 5
                pa[0] += 1
                pb, tpb = banks[pbi], t_bk[pbi]
                for dc in range(16):
                    S.op("pe", lambda e, dc=dc: e.matmul(pb[:, :260], hTb[:, dc, sub * 128:(sub + 1) * 128], wvf[:, dc, :], start=(dc == 0), stop=(dc == 15)),
                         reads=[t_hT[ti % 2][dc], t_wvf], writes=[tpb])
                S.op("dve", lambda e: e.tensor_tensor(out=Z[:, j, :], in0=pb[:, 256:260], in1=bfb[:], op=ALU.add),
                     reads=[tpb, t_bfb], writes=[t_Z])
                for hd_ in range(2):
                    S.op("act", lambda e, hd_=hd_: e.activation(out=V[:, hd_, j, :], in_=pb[:, hd_ * 128:(hd_ + 1) * 128], func=AF.Copy),
                         reads=[tpb, t_Z], writes=[t_V])

            load_x(0)
            if NTI > 1:
                load_x(1)
            norm_p1(0)
            norm_p2(0)
            norm_p3(0)
            if NTI > 1:
                norm_p1(1)
            for ti in range(NTI):
                nxt = ti + 1 < NTI
                if ti + 2 < NTI:
                    load_x(ti + 2)
                a0 = part1(ti, 0)
                a1 = part1(ti, 1)
                if nxt:
                    norm_p2(ti + 1)
                    norm_p3(ti + 1)
                part2(ti, 0, a0)
                a2 = part1(ti, 2)
                part2(ti, 1, a1)
                a3 = part1(ti, 3)
                part2(ti, 2, a2)
                vproj(ti, 0)
                if ti + 2 < NTI:
                    norm_p1(ti + 2)
                part2(ti, 3, a3)
                vproj(ti, 1)
            if stop == 1:
                break
            S.op("act", lambda e: e.activation(out=E[:], in_=Z[:, :, 0:2], func=AF.Exp, scale=-1.0), reads=[t_Z], writes=[t_E])
            S.op("act", lambda e: e.activation(out=SP[:].rearrange("p h j -> p j h"), in_=E[:], func=AF.Ln, bias=1.0, scale=1.0),
                 reads=[t_E], writes=[t_SP])
            S.op("dve", lambda e: e.tensor_copy(out=SPp[0][:], in_=SP[:]), reads=[t_SP], writes=[t_SPp])
            S.op("dve", lambda e: e.tensor_tensor(out=SPr[:], in0=SP[:], in1=SPp[0][:], op=ALU.subtract), reads=[t_SP, t_SPp], writes=[t_SPr])
            S.op("dve", lambda e: e.tensor_copy(out=SPp[1][:], in_=SPr[:]), reads=[t_SPr], writes=[t_SPp])
            S.op("dve", lambda e: e.tensor_tensor(out=SPr[:], in0=SPr[:], in1=SPp[1][:], op=ALU.subtract), reads=[t_SPr, t_SPp], writes=[t_SPr])
            S.op("dve", lambda e: e.tensor_copy(out=SPp[2][:], in_=SPr[:]), reads=[t_SPr], writes=[t_SPp])
            pc = [0]
            sc = [0]
            for hd in range(2):
                for pc_ in range(3):
                    S.op("pe", lambda e, hd=hd, pc_=pc_: e.matmul(banks[0][:, :NJ], tri[:], SPp[pc_][:, hd, :], start=(pc_ == 0), stop=(pc_ == 2)),
                         reads=[t_tri, t_SPp], writes=[t_bk[0]])
                for pc_ in range(3):
                    S.op("pe", lambda e, hd=hd, pc_=pc_: e.matmul(banks[1][:, :NJ], obf_[:], SPp[pc_][:, hd, :], start=(pc_ == 0), stop=(pc_ == 2)),
                         reads=[cx.t_ones_bf, t_SPp], writes=[t_bk[1]])
                S.op("dve", lambda e: e.tensor_copy(out=tot[:], in_=banks[1][:, :NJ]), reads=[t_bk[1]], writes=[t_tot])
                S.op("dve", lambda e: e.tensor_tensor_scan(out=cum[:], data0=of[:, 0:NJ], data1=tot[:], initial=0.0,
                                                           op0=ALU.mult, op1=ALU.add),
                     reads=[t_tot, cx.t_ones_f], writes=[t_cum])
                S.op("dve", lambda e: e.tensor_tensor(out=excl[:], in0=cum[:], in1=tot[:], op=ALU.subtract),
                     reads=[t_cum, t_tot], writes=[t_excl])
                S.op("dve", lambda e: e.tensor_tensor(out=negc[:], in0=banks[0][:, :NJ], in1=excl[:], op=ALU.add),
                     reads=[t_bk[0], t_excl], writes=[t_negc])
                for qb in range(NQB):
                    nj = 4 * qb + 4
                    S.op("dve", lambda e, qb=qb, nj=nj: e.tensor_scalar(
                        out=bias[:, qb, 0:nj], in0=negc[:, 0:nj], scalar1=excl[:, 4 * qb:4 * qb + 1], scalar2=None, op0=ALU.subtract),
                        reads=[t_negc, t_excl], writes=[t_bias])
                if stop == 2:
                    continue
                pairs = []
                for qb in range(NQB):
                    for j in range(4 * qb + 4):
                        pairs.append((qb, j))
                LA = 2
                slots = {}

                def qk(idx):
                    qb, j = pairs[idx]
                    d = max(0, j - 4 * qb)
                    c0 = 128 * d
                    n = 512 - c0
                    si = sc[0] % 3
                    sc[0] += 1
                    slots[idx] = si
                    q0 = qb * 512
                    S.op("pe", lambda e, hd=hd: e.matmul(banks[si][:, :n], KT[hd][:, j * 128:(j + 1) * 128], QT[hd][:, q0 + c0:q0 + 512], start=True, stop=True),
                         reads=[t_KT[hd], t_QT[hd]], writes=[t_bk[si]])

                for idx in range(min(LA, len(pairs))):
                    qk(idx)
                for idx, (qb, j) in enumerate(pairs):
                    q0 = qb * 512
                    oi = qb % 2
                    ops_, t_ops = banks[3 + oi], t_bk[3 + oi]
                    dps, t_dps = banks[5 + oi], t_bk[5 + oi]
                    nj = 4 * qb + 4
                    d = max(0, j - 4 * qb)
                    c0 = 128 * d
                    n = 512 - c0
                    si = slots.pop(idx)
                    sps, t_sps = banks[si], t_bk[si]
                    pi = pc[0] % NP
                    pc[0] += 1
                    p_, t_p = pT[pi], t_pT[pi]
                    S.op("act", lambda e, sps=sps, p_=p_, qb=qb, j=j, n=n: e.activation(
                        out=p_[:, :n], in_=sps[:, :n], func=AF.Exp, bias=bias[:, qb, j:j + 1], scale=1.0),
                        reads=[t_sps, t_bias], writes=[t_p])
                    if j >= 4 * qb:
                        S.op("pool", lambda e, p_=p_: e.affine_select(
                            out=p_[:, 0:128], in_=p_[:, 0:128], pattern=[[1, 128]], compare_op=ALU.is_ge, fill=0.0,
                            base=0, channel_multiplier=-1), reads=[t_p], writes=[t_p])
                    if idx + LA < len(pairs):
                        qk(idx + LA)
                    S.op("pe", lambda e, ops_=ops_, p_=p_, hd=hd, j=j, c0=c0, n=n, nj=nj: e.matmul(
                        ops_[:, c0:512], V[:, hd, j, :], p_[:, :n], start=(j == 0), stop=(j == nj - 1)),
                        reads=[t_V, t_p], writes=[t_ops])
                    S.op("pe", lambda e, dps=dps, p_=p_, j=j, c0=c0, n=n, nj=nj: e.matmul(
                        dps[:, c0:512], obf[:], p_[:, :n], start=(j == 0), stop=(j == nj - 1)),
                        reads=[cx.t_ones_bf, t_p], writes=[t_dps])
                    if j == nj - 1:
                        S.op("dve", lambda e, dps=dps: e.reciprocal(out=rden[:], in_=dps[:]), reads=[t_dps], writes=[t_rden])
                        obb, t_obb = ob[oi], t_ob[oi]
                        S.op("dve", lambda e, ops_=ops_, obb=obb: e.tensor_tensor(out=obb[:], in0=ops_[:], in1=rden[:], op=ALU.mult),
                             reads=[t_ops, t_rden], writes=[t_obb])
                        tk = b * SL + q0
                        S.op("sp", lambda e, obb=obb, hd=hd, tk=tk: e.dma_start(out=oT[hd, :, tk:tk + 512], in_=obb[:]),
                             reads=[t_obb], dma=True, dma_tile=t_obb, final=True)
        S.finalize()
        with nc.Block() as block:
            S.emit(block)
    return nc


DIL_R = (1, 4, 16)


def build_dil(SL=S_LEN, NB=2):
    nc = bass.Bass("TRN2", target_bir_lowering=False)
    with ExitStack() as es:
        cx = Ctx(nc, es)
        S = cx.S
        TA = 256
        SB = 2048
        NTK = SL * NB
        hT_d = cx.dram_in("hT", [NTK // TA, 128, 16 * TA], BF16)
        wqk_d = cx.dram_in("wqk", [128, 16, 768], F32)
        wv_d = cx.dram_in("wv", [128, 2, 2048], F32)
        qg_d = cx.dram_in("qg", [128, 3], F32)
        kg_d = cx.dram_in("kg", [128, 3], F32)
        bm_d = cx.dram_in("bmat", [128, 3, 256], F32)
        vd = cx.dram_out("vscratch", [NTK, 256], BF16)
        oT = cx.dram_out("oT", [2, 128, NTK], BF16)
        cx.consts()
        make_eps(cx)
        qg, t_qg0 = load_small(cx, qg_d, [128, 3], F32, "qg")
        kg, t_kg = load_small(cx, kg_d, [128, 3], F32, "kg")
        bm, t_bm = load_small(cx, bm_d, [128, 3, 256], F32, "bm")
        qgs = cx.sb([128, 3], F32, "qgs")
        t_qg = S.tile("qgs")
        S.op("dve", lambda e: e.tensor_scalar(out=qgs[:], in0=qg[:], scalar1=float(128 ** -0.5), scalar2=None, op0=ALU.mult),
             reads=[t_qg0], writes=[t_qg])
        wqk_f, t_wqk = load_cast_weight(cx, wqk_d, [128, 16, 768], "wqk")
        wv_f, t_wv = load_cast_weight(cx, wv_d, [128, 2, 2048], "wv")
        wqk = wqk_f[:]
        wv = wv_f[:].rearrange("p a (c n) -> p (a c) n", n=256)

        hT = [cx.sb([128, 16, TA], BF16, "hT") for _ in range(2)]
        t_hT = S.tiles("hT", 2)
        lnb = cx.sb([128, TA], F32, "lnb")
        t_ln = S.tile("ln")
        NQ = 3
        q32 = [cx.sb([128, TA], F32, "q32") for _ in range(NQ)]
        t_q32 = S.tiles("q32", NQ)
        sq32 = [cx.sb([128, TA], BF16, "sq32") for _ in range(NQ)]
        t_sq32 = S.tiles("sq32", NQ)
        r2 = [cx.sb([128, TA], F32, "r2") for _ in range(NQ)]
        t_r2 = S.tiles("r2", NQ)
        Qs = [cx.sb([128, SB], BF16, "Qs") for _ in range(3)]
        t_Qs = S.tiles("Qs", 3)
        Ks = [[cx.sb([128, SB], BF16, "Ks") for _ in range(2)] for _ in range(3)]
        t_Ks = [S.tiles("Ks", 2) for _ in range(3)]
        vst = [cx.sb([128, 256], BF16, "vst") for _ in range(4)]
        t_vst = S.tiles("vst", 4)
        vbuf = [cx.sb([128, 8192], BF16, "vbuf") for _ in range(2)]
        t_vbuf = S.tiles("vbuf", 2)
        acc = cx.sb([128, 3, SB], F32, "acc")
        t_acc = S.tile("acc")
        rden = cx.sb([128, SB], F32, "rden")
        t_rden = S.tile("rden")
        ob = cx.sb([128, 2, SB], BF16, "ob")
        t_ob = S.tile("ob")
        NST = 4
        st = [cx.sb([128, 256], F32, "st") for _ in range(NST)]
        t_st = S.tiles("st", NST)
        pT = [cx.sb([128, 256], BF16, "pT") for _ in range(NST)]
        t_pT = S.tiles("pT", NST)
        banks = [cx.ps([128, 512], F32, "bk") for _ in range(8)]
        t_bk = S.tiles("bk", 8)
        obf = cx.ones_bf
        SPS_B = (0, 1, 2, 5)
        PO_B = (3, 4, 7, 6)

        t_vstore_pool = S.tiles("vstore", 16)
        vctr = [0]
        vbc = [0]
        pa = [0]
        qc = [0]
        NSB = SL // SB
        NTI = SB // TA
        tiles_all = [(b, sb, ti) for b in range(NB) for sb in range(NSB) for ti in range(NTI)]

        def load_h(gi):
            b_, sb_, ti_ = tiles_all[gi]
            tok0 = b_ * SL + sb_ * SB + ti_ * TA
            hTb = hT[gi % 2]
            gt = tok0 // TA
            S.op("sp", lambda e: e.dma_start(out=hTb[:].rearrange("p c t -> p (c t)"), in_=hT_d[gt]),
                 writes=[t_hT[gi % 2]], dma=True)

        load_h(0)
        gi = 0
        for b in range(NB):
            t_vstore = []
            for sb in range(NSB):
                cur = sb % 2
                prv = 1 - cur
                stores_this = []

                def part1(gi, ti, k):
                    which, g = divmod(k, 3)
                    hTb = hT[gi % 2]
                    pbi = pa[0] % 5
                    pa[0] += 1
                    qi_ = qc[0] % NQ
                    qc[0] += 1
                    pb, tpb = banks[pbi], t_bk[pbi]
                    col0 = k * 128
                    for dc in range(16):
                        S.op("pe", lambda e, dc=dc: e.matmul(pb[:, :TA], wqk[:, dc, col0:col0 + 128], hTb[:, dc, :], start=(dc == 0), stop=(dc == 15)),
                             reads=[t_wqk, t_hT[gi % 2]], writes=[tpb])
                    S.op("act", lambda e: e.activation(out=q32[qi_][:], in_=pb[:, :TA], func=AF.Copy), reads=[tpb], writes=[t_q32[qi_]])
                    S.op("pool", lambda e: e.tensor_tensor(out=sq32[qi_][:], in0=q32[qi_][:], in1=q32[qi_][:], op=ALU.mult),
                         reads=[t_q32[qi_]], writes=[t_sq32[qi_]])
                    return qi_

                def part2(gi, ti, k, qi_, cur=cur):
                    which, g = divmod(k, 3)
                    r = DIL_R[g]
                    if which == 0:
                        dst_t, t_dst, gain, t_gain = Qs[g], t_Qs[g], qgs[:, g:g + 1], t_qg
                    else:
                        dst_t, t_dst, gain, t_gain = Ks[g][cur], t_Ks[g][cur], kg[:, g:g + 1], t_kg
                    a0 = ti * TA // r
                    if r == 1:
                        dst_ap = dst_t[:, ti * TA:(ti + 1) * TA]
                        in0 = q32[qi_][:]
                        in1 = r2[qi_][:]
                    else:
                        dst_ap = dst_t[:].rearrange("p (b a) -> p b a", b=r)[:, :, a0:a0 + TA // r]
                        in0 = q32[qi_][:].rearrange("p (a b) -> p b a", b=r)
                        in1 = r2[qi_][:].rearrange("p (a b) -> p b a", b=r)
                    sb_ = 5 + (qi_ % 2)
                    S.op("pe", lambda e: e.matmul(banks[sb_][:, :TA], obf[:], sq32[qi_][:], start=True, stop=True),
                         reads=[t_sq32[qi_], cx.t_ones_bf], writes=[t_bk[sb_]])
                    rstd_from_sumsq(cx, banks[sb_], t_bk[sb_], TA, 1.0 / 128, lnb, t_ln, r2[qi_], t_r2[qi_])
                    S.op("dve", lambda e: e.scalar_tensor_tensor(out=dst_ap, in0=in0, scalar=gain, in1=in1, op0=ALU.mult, op1=ALU.mult),
                         reads=[t_q32[qi_], t_r2[qi_], t_gain], writes=[t_dst])

                def vproj(gi, ti, sub, b=b, sb=sb):
                    hTb = hT[gi % 2]
                    pbi = pa[0] % 5
                    pa[0] += 1
                    pb, tpb = banks[pbi], t_bk[pbi]
                    for dc in range(16):
                        S.op("pe", lambda e, dc=dc: e.matmul(pb[:, :256], hTb[:, dc, sub * 128:(sub + 1) * 128], wv[:, dc, :], start=(dc == 0), stop=(dc == 15)),
                             reads=[t_hT[gi % 2], t_wv], writes=[tpb])
                    vi = vctr[0] % 4
                    vctr[0] += 1
                    S.op("act", lambda e: e.activation(out=vst[vi][:], in_=pb[:, 0:256], func=AF.Copy), reads=[tpb], writes=[t_vst[vi]])
                    tk = b * SL + sb * SB + ti * TA + sub * 128
                    t_store = t_vstore_pool[(ti * (TA // 128) + sub) % 16]
                    S.op("sp", lambda e: e.dma_start(out=vd[tk:tk + 128, :], in_=vst[vi][:]),
                         reads=[t_vst[vi]], writes=[t_store], dma=True, dma_tile=t_store)
                    stores_this.append(t_store)

                for ti in range(NTI):
                    if gi + 1 < len(tiles_all):
                        load_h(gi + 1)
                    a = [None] * 6
                    a[0] = part1(gi, ti, 0)
                    a[1] = part1(gi, ti, 1)
                    for k in range(2, 6):
                        part2(gi, ti, k - 2, a[k - 2])
                        a[k] = part1(gi, ti, k)
                    part2(gi, ti, 4, a[4])
                    vproj(gi, ti, 0)
                    part2(gi, ti, 5, a[5])
                    vproj(gi, ti, 1)
                    gi += 1

                first = True
                has_prev_sb = sb > 0
                lo = -1 if has_prev_sb else 0
                for g in range(3):
                    r = DIL_R[g]
                    nrow = 16 // r
                    Lsb = SB // r
                    vb_i = vbc[0] % 2
                    vbc[0] += 1
                    vb, t_vb = vbuf[vb_i], t_vbuf[vb_i]
                    ntile = nrow - lo
                    base_row = (b * SL + sb * SB) // r
                    vsrc = vd.rearrange("(n x) d -> n (x d)", x=r)
                    r0 = base_row + lo * 128
                    src = vsrc[r0:r0 + ntile * 128, :].rearrange("(n i) x -> i n x", i=128)
                    vview = vb[:, 0:ntile * r * 256].rearrange("p (n x) -> p n x", x=r * 256)
                    deps = list(stores_this) + (list(t_vstore) if has_prev_sb else [])
                    S.op("sp", lambda e, vview=vview, src=src: e.dma_start(out=vview, in_=src),
                         reads=deps, writes=[t_vb], dma=True)
                    accv = acc[:].rearrange("p h (a b) -> p h a b", b=r)
                    blocks = [(rb, n_) for rb in range(r) for n_ in range(nrow)]
                    LA = 4

                    def sps_mm(i, g=g, cur=cur, prv=prv, Lsb=Lsb):
                        rb, n_ = blocks[i]
                        has_prev = has_prev_sb or n_ > 0
                        c_cur = rb * Lsb + n_ * 128
                        qblk = Qs[g][:, c_cur:c_cur + 128]
                        kcur = Ks[g][cur][:, c_cur:c_cur + 128]
                        if n_ > 0:
                            kprev = Ks[g][cur][:, c_cur - 128:c_cur]
                            t_kprev = t_Ks[g][cur]
                        else:
                            kprev = Ks[g][prv][:, rb * Lsb + Lsb - 128:rb * Lsb + Lsb]
                            t_kprev = t_Ks[g][prv]
                        bi_ = SPS_B[i % 4]
                        sps, t_sps = banks[bi_], t_bk[bi_]
                        if has_prev:
                            S.op("pe", lambda e: e.matmul(sps[:, 0:128], kprev, qblk, start=True, stop=True),
                                 reads=[t_kprev, t_Qs[g]], writes=[t_sps])
                        S.op("pe", lambda e: e.matmul(sps[:, 128:256], kcur, qblk, start=True, stop=True),
                             reads=[t_Ks[g][cur], t_Qs[g]], writes=[t_sps])

                    def add_bias(i, g=g):
                        rb, n_ = blocks[i]
                        has_prev = has_prev_sb or n_ > 0
                        c_lo = 0 if has_prev else 128
                        bi_ = SPS_B[i % 4]
                        sps, t_sps = banks[bi_], t_bk[bi_]
                        stb, t_stb = st[i % NST], t_st[i % NST]
                        S.op("dve", lambda e: e.tensor_tensor(
                            out=stb[:, c_lo:256], in0=sps[:, c_lo:256], in1=bm[:, g, c_lo:256], op=ALU.add),
                            reads=[t_sps, t_bm], writes=[t_stb])

                    for i in range(min(LA, len(blocks))):
                        sps_mm(i)
                    add_bias(0)
                    if len(blocks) > 1:
                        add_bias(1)
                    for i, (rb, n_) in enumerate(blocks):
                        has_prev = has_prev_sb or n_ > 0
                        c_lo = 0 if has_prev else 128
                        stb, t_stb = st[i % NST], t_st[i % NST]
                        p_, t_p = pT[i % NST], t_pT[i % NST]
                        S.op("act", lambda e, stb=stb, p_=p_, c_lo=c_lo: e.activation(out=p_[:, c_lo:256], in_=stb[:, c_lo:256], func=AF.Exp),
                             reads=[t_stb], writes=[t_p])
                        if i + 2 < len(blocks):
                            add_bias(i + 2)
                        if i + LA < len(blocks):
                            sps_mm(i + LA)
                        pb_ = PO_B[i % 4]
                        po, t_po = banks[pb_], t_bk[pb_]
                        ti_v = n_ - lo
                        for h in range(3):
                            if h < 2:
                                lc = vview[:, ti_v, rb * 256 + h * 128:rb * 256 + (h + 1) * 128]
                                lp = vview[:, ti_v - 1, rb * 256 + h * 128:rb * 256 + (h + 1) * 128] if has_prev else None
                                rd = [t_vb, t_p]
                            else:
                                lc = obf[:]
                                lp = obf[:]
                                rd = [cx.t_ones_bf, t_p]
                            if has_prev:
                                S.op("pe", lambda e, po=po, lp=lp, p_=p_, h=h: e.matmul(po[:, h * 128:(h + 1) * 128], lp, p_[:, 0:128], start=True, stop=False),
                                     reads=rd, writes=[t_po])
                            S.op("pe", lambda e, po=po, lc=lc, p_=p_, h=h, has_prev=has_prev: e.matmul(
                                po[:, h * 128:(h + 1) * 128], lc, p_[:, 128:256], start=(not has_prev), stop=True),
                                reads=rd, writes=[t_po])
                        dst = accv[:, :, n_ * 128:(n_ + 1) * 128, rb]
                        src_po = po[:, 0:384].rearrange("p (h q) -> p h q", h=3)
                        if first:
                            S.op("dve", lambda e, dst=dst, src_po=src_po: e.tensor_copy(out=dst, in_=src_po),
                                 reads=[t_po], writes=[t_acc])
                        else:
                            S.op("dve", lambda e, dst=dst, src_po=src_po: e.tensor_tensor(out=dst, in0=src_po, in1=dst, op=ALU.add),
                                 reads=[t_po, t_acc], writes=[t_acc])
                    first = False
                t_vstore = stores_this
                S.op("dve", lambda e: e.reciprocal(out=rden[:], in_=acc[:, 2, :]), reads=[t_acc], writes=[t_rden])
                for h in range(2):
                    S.op("dve", lambda e, h=h: e.tensor_tensor(out=ob[:, h, :], in0=acc[:, h, :], in1=rden[:], op=ALU.mult),
                         reads=[t_acc, t_rden], writes=[t_ob])
                tk = b * SL + sb * SB
                t_ost = S.tile("ost")
                S.op("sp", lambda e, tk=tk: e.dma_start(out=oT[:, :, tk:tk + SB].rearrange("h p t -> p h t"), in_=ob[:]),
                     reads=[t_ob], writes=[t_ost], dma=True, dma_tile=t_ost, final=True)
        S.finalize()
        with nc.Block() as block:
            S.emit(block)
    return nc


def qk_proj_perm(cx, W, t_W, col0, hT, t_hT, n, pb, tpb, ss2, t_ss2, q32, t_q32, sq32, t_sq32, lnb, t_ln, r2, t_r2,
                 gain, t_gain, dst_ap, t_dst, r):
    S = cx.S
    for dc in range(16):
        S.op("pe", lambda e, dc=dc: e.matmul(pb[:, :n], W[:, dc, col0:col0 + 128], hT[:, dc, :n], start=(dc == 0), stop=(dc == 15)),
             reads=[t_W, t_hT], writes=[tpb])
    S.op("act", lambda e: e.activation(out=q32[:, :n], in_=pb[:, :n], func=AF.Copy), reads=[tpb], writes=[t_q32])
    S.op("dve", lambda e: e.tensor_tensor(out=sq32[:, :n], in0=q32[:, :n], in1=q32[:, :n], op=ALU.mult),
         reads=[t_q32], writes=[t_sq32])
    ofb = cx.ones_bf
    S.op("pe", lambda e: e.matmul(ss2[:, :n], ofb[:], sq32[:, :n], start=True, stop=True),
         reads=[t_sq32, cx.t_ones_bf], writes=[t_ss2])
    rstd_from_sumsq(cx, ss2, t_ss2, n, 1.0 / 128, lnb, t_ln, r2, t_r2)
    if r == 1:
        in0 = q32[:, :n]
        in1 = r2[:, :n]
    else:
        in0 = q32[:, :n].rearrange("p (a b) -> p b a", b=r)
        in1 = r2[:, :n].rearrange("p (a b) -> p b a", b=r)
    S.op("dve", lambda e: e.scalar_tensor_tensor(out=dst_ap, in0=in0, scalar=gain, in1=in1, op0=ALU.mult, op1=ALU.mult),
         reads=[t_q32, t_r2, t_gain], writes=[t_dst])


_CACHE = {}


def _prog(name, fn, *a):
    if name not in _CACHE:
        _CACHE[name] = fn(*a)
    return _CACHE[name]


def _fm(xT2d):
    return np.ascontiguousarray(xT2d.reshape(16, 128, xT2d.shape[1]))


def _tiles(xT2d, ta=256):
    T = xT2d.shape[1]
    a = xT2d.reshape(16, 128, T // ta, ta).transpose(2, 1, 0, 3)
    return np.ascontiguousarray(a.reshape(T // ta, 128, 16 * ta))


def _wchunks(w, ngrp, per):
    K, N = w.shape
    kc = K // 128
    cb = N // 128
    a = w.reshape(kc, 128, cb, 128)
    a = a.transpose(2, 1, 0, 3)
    a = a.reshape(ngrp, per, 128, kc, 128).transpose(0, 2, 1, 3, 4)
    return np.ascontiguousarray(a.reshape(ngrp, 128, 4, (per * kc * 128) // 4))


def _wcols(w, cols):
    a = w[:, cols].reshape(16, 128, len(cols)).transpose(1, 0, 2)
    return np.ascontiguousarray(a)


def _gvec(g):
    return np.ascontiguousarray(g.reshape(16, 128).T)


def _run(nc, in_maps):
    res = run_bass_kernel_spmd(nc, in_maps, core_ids=list(range(NCORES)))
    return res.results


def kernel(x, fox_w_in, fox_b_f, fox_q_gain, fox_k_gain, fox_w_out, dil_w_in, dil_q_gain, dil_k_gain, dil_w_out,
           mix_norm_g, mlp_norm_g, mlp_w_up, mlp_w_down):
    f32 = np.float32
    x = np.asarray(x, f32)
    xT = np.ascontiguousarray(x.reshape(NTOK, D).T)
    xT_tl = _tiles(xT)

    w_in = np.asarray(fox_w_in[0], f32)
    H = 16
    fox = _prog("fox", build_fox)
    maps = []
    for c in range(NCORES):
        hA, hB = 2 * c, 2 * c + 1
        qcols = list(range(hA * 128, hA * 128 + 128)) + list(range(hB * 128, hB * 128 + 128))
        kcols = [H * 128 + q for q in qcols]
        vcols = [2 * H * 128 + q for q in qcols] + [3 * H * 128 + hA, 3 * H * 128 + hB]
        maps.append({
            "xT": xT_tl,
            "wq": _wcols(w_in, qcols).reshape(128, 2, 2048),
            "wk": _wcols(w_in, kcols).reshape(128, 2, 2048),
            "wv": _wcols(w_in, vcols[:256]),
            "wf": np.ascontiguousarray(np.pad(_wcols(w_in, vcols[256:]), ((0, 0), (0, 0), (0, 2)))),
            "gmix": _gvec(np.asarray(mix_norm_g[0], f32)),
            "qg": np.asarray(fox_q_gain[0], f32).reshape(128, 1),
            "kg": np.asarray(fox_k_gain[0], f32).reshape(128, 1),
            "bfb": np.ascontiguousarray(np.pad(np.broadcast_to(np.asarray(fox_b_f[0], f32)[[hA, hB]][None, :], (128, 2)), ((0, 0), (0, 2)))),
        })
    r = _run(fox, maps)
    oT_all = np.concatenate([r[c]["oT"].reshape(256, NTOK) for c in range(NCORES)], axis=0)

    mlp_h = _prog("mlp_h", build_mlp, True)
    wout_l = _wchunks(np.asarray(fox_w_out[0], f32), 4, 4)
    wup_l = _wchunks(np.asarray(mlp_w_up[0], f32), 16, 4)
    wdn = np.asarray(mlp_w_down[0], f32)
    wdn_l = np.ascontiguousarray(wdn.reshape(64, 128, 16, 128).transpose(2, 1, 0, 3).reshape(16, 128, 4, 2048))
    maps = []
    for c in range(NCORES):
        sl = slice(c * TPC, (c + 1) * TPC)
        maps.append({
            "xT": _fm(xT[:, sl]), "oT": _fm(oT_all[:, sl]),
            "wout": wout_l, "wup": wup_l, "wdn": wdn_l,
            "g_mlp": _gvec(np.asarray(mlp_norm_g[0], f32)),
            "g_nxt": _gvec(np.asarray(mix_norm_g[1], f32)),
        })
    r = _run(mlp_h, maps)
    x1T = np.concatenate([r[c]["x1T"].reshape(D, TPC) for c in range(NCORES)], axis=1)
    h1T = np.concatenate([r[c]["h1T"].reshape(D, TPC) for c in range(NCORES)], axis=1)
    h1T_tl = _tiles(h1T)

    dil = _prog("dil", build_dil)
    w_in1 = np.asarray(dil_w_in[0], f32)
    G, HD = 3, 8
    maps = []
    ki = np.arange(128)[:, None]
    qi = np.arange(128)[None, :]
    for c in range(NCORES):
        cols = []
        for which in range(2):
            for g in range(G):
                base = which * G * HD * 128 + (g * HD + c) * 128
                cols += list(range(base, base + 128))
        vcols = list(range(2 * G * HD * 128 + c * 256, 2 * G * HD * 128 + (c + 1) * 256))
        bmat = np.empty((128, 3, 256), f32)
        for g in range(G):
            slope = np.float32(2.0) ** (np.float32(-8.0) * np.float32(g * HD + c + 1) / np.float32(G * HD))
            rr = DIL_R[g]
            dprev = (qi + 128 - ki).astype(f32)
            dcur = (qi - ki).astype(f32)
            bmat[:, g, 0:128] = np.where(qi <= ki, -slope * dprev * rr, NEG)
            bmat[:, g, 128:256] = np.where(qi >= ki, -slope * dcur * rr, NEG)
        maps.append({
            "hT": h1T_tl,
            "wqk": _wcols(w_in1, cols),
            "wv": _wcols(w_in1, vcols).reshape(128, 2, 2048),
            "qg": np.ascontiguousarray(np.asarray(dil_q_gain[0], f32).T),
            "kg": np.ascontiguousarray(np.asarray(dil_k_gain[0], f32).T),
            "bmat": bmat,
        })
    r = _run(dil, maps)
    o1T_all = np.concatenate([r[c]["oT"].reshape(256, NTOK) for c in range(NCORES)], axis=0)

    mlp_l = _prog("mlp_l", build_mlp, False)
    wout_l = _wchunks(np.asarray(dil_w_out[0], f32), 4, 4)
    wup_l = _wchunks(np.asarray(mlp_w_up[1], f32), 16, 4)
    wdn = np.asarray(mlp_w_down[1], f32)
    wdn_l = np.ascontiguousarray(wdn.reshape(64, 128, 16, 128).transpose(2, 1, 0, 3).reshape(16, 128, 4, 2048))
    maps = []
    for c in range(NCORES):
        sl = slice(c * TPC, (c + 1) * TPC)
        maps.append({
            "xT": _fm(x1T[:, sl]), "oT": _fm(o1T_all[:, sl]),
            "wout": wout_l, "wup": wup_l, "wdn": wdn_l,
            "g_mlp": _gvec(np.asarray(mlp_norm_g[1], f32)),
        })
    r = _run(mlp_l, maps)
    x2T = np.concatenate([r[c]["x1T"].reshape(D, TPC) for c in range(NCORES)], axis=1)
    return np.ascontiguousarray(x2T.T).reshape(2, S_LEN, D).astype(f32)
```

```python
import numpy as np
from contextlib import ExitStack
import ml_dtypes
import concourse.bass as bass
import concourse.mybir as mybir
from concourse.bass_utils import run_bass_kernel_spmd

F32 = mybir.dt.float32
BF16 = mybir.dt.bfloat16
AF = mybir.ActivationFunctionType
ALU = mybir.AluOpType

NCORES = 8
D = 2048
S_LEN = 8192
NTOK = 16384
TPC = NTOK // NCORES
EPS = 1e-6
NEG = -30000.0

ENGS = ("pe", "act", "dve", "pool", "sp")


class Tile:
    __slots__ = ("name", "last_w", "readers", "dsem", "dcnt")

    def __init__(self, name):
        self.name = name
        self.last_w = None
        self.readers = []
        self.dsem = None
        self.dcnt = 0


class Op:
    __slots__ = ("idx", "eng", "fn", "deps", "dma", "dsem", "dval", "has_dep", "inc", "waits")

    def __init__(self, idx, eng, fn, dma):
        self.idx = idx
        self.eng = eng
        self.fn = fn
        self.deps = set()
        self.dma = dma
        self.dsem = None
        self.dval = 0
        self.has_dep = False
        self.inc = 0
        self.waits = []


class Sched:
    def __init__(self, nc, es):
        self.nc = nc
        self.es = es
        self.ops = []
        self.sem = {e: es.enter_context(nc.semaphore("sem_" + e)) for e in ENGS}
        self.final_dma = []
        self.ntile = 0

    def tile(self, name="t"):
        self.ntile += 1
        return Tile(f"{name}_{self.ntile}")

    def tiles(self, name, n):
        return [self.tile(name) for _ in range(n)]

    def _dma_sem(self, t):
        if t.dsem is None:
            t.dsem = self.es.enter_context(self.nc.semaphore("d_" + t.name))
        return t.dsem

    def op(self, eng, fn, reads=(), writes=(), dma=False, dma_tile=None, final=False):
        o = Op(len(self.ops), eng, fn, dma)
        for t in reads:
            if t.last_w is not None:
                o.deps.add(t.last_w)
        for t in writes:
            if t.last_w is not None:
                o.deps.add(t.last_w)
            o.deps.update(t.readers)
        for t in reads:
            t.readers.append(o.idx)
        for t in writes:
            t.last_w = o.idx
            t.readers = []
        if dma:
            t = dma_tile if dma_tile is not None else (writes[0] if writes else reads[0])
            o.dsem = self._dma_sem(t)
            t.dcnt += 16
            o.dval = t.dcnt
            if final:
                self.final_dma.append(o)
        self.ops.append(o)
        return o

    def finalize(self):
        ops = self.ops
        for o in ops:
            best = {}
            keep = []
            for d in o.deps:
                p = ops[d]
                if p.dma:
                    keep.append(d)
                    continue
                if p.eng == "pe" and o.eng == "pe":
                    continue
                if p.eng not in best or best[p.eng] < d:
                    best[p.eng] = d
            o.deps = keep + list(best.values())
            for d in o.deps:
                ops[d].has_dep = True
        cnt = {e: 0 for e in ENGS}
        for o in ops:
            if not o.dma and o.has_dep:
                cnt[o.eng] += 1
                o.inc = cnt[o.eng]
        known = {e: {} for e in ENGS}
        for o in ops:
            w = {}
            for d in o.deps:
                p = ops[d]
                if p.dma:
                    key = ("d", id(p.dsem))
                    sem, val = p.dsem, p.dval
                else:
                    key = ("e", p.eng)
                    sem, val = self.sem[p.eng], p.inc
                if known[o.eng].get(key, 0) >= val:
                    continue
                if key not in w or w[key][1] < val:
                    w[key] = (sem, val)
            for key, (sem, val) in w.items():
                known[o.eng][key] = val
            o.waits = list(w.values())

    def emit(self, block):
        per = {e: [o for o in self.ops if o.eng == e] for e in ENGS}
        finals = self.final_dma
        sems = self.sem

        def run(h, lst):
            for o in lst:
                for sem, val in o.waits:
                    h.wait_ge(sem, val)
                ins = o.fn(h)
                if o.dma:
                    ins.then_inc(o.dsem, 16)
                elif o.inc:
                    ins.then_inc(sems[o.eng], 1)

        @block.tensor
        def _(e):
            run(e, per["pe"])

        @block.scalar
        def _(e):
            run(e, per["act"])

        @block.vector
        def _(e):
            run(e, per["dve"])

        @block.gpsimd
        def _(e):
            run(e, per["pool"])

        @block.sync
        def _(e):
            run(e, per["sp"])
            for o in finals:
                e.wait_ge(o.dsem, o.dval)


class Ctx:
    def __init__(self, nc, es):
        self.nc = nc
        self.es = es
        self.S = Sched(nc, es)
        self.nalloc = 0

    def sb(self, shape, dt, name="sb"):
        self.nalloc += 1
        return self.es.enter_context(self.nc.sbuf_tensor(f"{name}{self.nalloc}", list(shape), dt))

    def ps(self, shape, dt=F32, name="ps"):
        self.nalloc += 1
        return self.es.enter_context(self.nc.psum_tensor(f"{name}{self.nalloc}", list(shape), dt))

    def dram_in(self, name, shape, dt):
        return self.nc.dram_tensor(name, list(shape), dt, kind="ExternalInput").ap()

    def dram_out(self, name, shape, dt):
        return self.nc.dram_tensor(name, list(shape), dt, kind="ExternalOutput").ap()

    def consts(self):
        S = self.S
        self.ones_bf = self.sb([128, 128], BF16, "ones_bf")
        self.ones_f = self.sb([128, 128], F32, "ones_f")
        self.t_ones_bf = S.tile("ones_bf")
        self.t_ones_f = S.tile("ones_f")
        ob, of = self.ones_bf, self.ones_f
        S.op("dve", lambda e: e.memset(ob[:], 1.0), writes=[self.t_ones_bf])
        S.op("dve", lambda e: e.memset(of[:], 1.0), writes=[self.t_ones_f])


def load_small(cx, dram_ap, shape, dt=F32, name="c", q="sp"):
    t = cx.sb(shape, dt, name)
    tl = cx.S.tile(name)
    cx.S.op(q, lambda e: e.dma_start(out=t[:], in_=dram_ap), writes=[tl], dma=True)
    return t, tl


def load_cast_weight(cx, dram_ap, shape, name):
    t = cx.sb(shape, BF16, name)
    tl = cx.S.tile(name)
    cx.S.op("pool", lambda e: e.dma_start(out=t[:], in_=dram_ap), writes=[tl], dma=True)
    return t, tl


def rstd_from_sumsq(cx, ss_ps, t_ss, n, inv_count, lnb, t_ln, rstd, t_rstd):
    S = cx.S
    eps_t = cx.eps_t
    S.op("act", lambda e: e.activation(out=lnb[:, :n], in_=ss_ps[:, :n], func=AF.Ln, bias=eps_t[:, 0:1], scale=inv_count),
         reads=[t_ss, cx.t_eps], writes=[t_ln])
    S.op("act", lambda e: e.activation(out=rstd[:, :n], in_=lnb[:, :n], func=AF.Exp, scale=-0.5),
         reads=[t_ln], writes=[t_rstd])


def make_eps(cx):
    cx.eps_t = cx.sb([128, 1], F32, "eps")
    cx.t_eps = cx.S.tile("eps")
    et = cx.eps_t
    cx.S.op("dve", lambda e: e.memset(et[:], EPS), writes=[cx.t_eps])


def norm_tile(cx, xt, t_xt, g_sb, t_g, hT, t_hT, n, sqb, t_sq, ss_ps, t_ss, lnb, t_ln, rstd, t_rstd):
    S = cx.S
    S.op("act", lambda e: e.activation(out=sqb[:, :, :n], in_=xt[:, :, :n], func=AF.Square),
         reads=t_xt, writes=[t_sq])
    ob = cx.ones_bf
    for dc in range(16):
        S.op("pe", lambda e, dc=dc: e.matmul(ss_ps[:, :n], ob[:], sqb[:, dc, :n], start=(dc == 0), stop=(dc == 15)),
             reads=[t_sq, cx.t_ones_bf], writes=[t_ss])
    rstd_from_sumsq(cx, ss_ps, t_ss, n, 1.0 / D, lnb, t_ln, rstd, t_rstd)
    for dc in range(16):
        S.op("dve", lambda e, dc=dc: e.scalar_tensor_tensor(out=hT[:, dc, :n], in0=xt[:, dc, :n], scalar=g_sb[:, dc:dc + 1],
                                                             in1=rstd[:, :n], op0=ALU.mult, op1=ALU.mult),
             reads=[t_xt[dc], t_g, t_rstd], writes=[t_hT])


def build_mlp(emit_h):
    nc = bass.Bass("TRN2", target_bir_lowering=False)
    with ExitStack() as es:
        cx = Ctx(nc, es)
        S = cx.S
        TT = 512
        NT = TPC // TT
        xT = cx.dram_in("xT", [16, 128, TPC], F32)
        oT = cx.dram_in("oT", [16, 128, TPC], BF16)
        wout = cx.dram_in("wout", [4, 128, 4, 2048], F32)
        wup = cx.dram_in("wup", [16, 128, 4, 2048], F32)
        wdn = cx.dram_in("wdn", [16, 128, 4, 2048], F32)
        g_mlp = cx.dram_in("g_mlp", [128, 16], F32)
        x1T = cx.dram_out("x1T", [16, 128, TPC], F32)
        if emit_h:
            g_nxt = cx.dram_in("g_nxt", [128, 16], F32)
            h1T = cx.dram_out("h1T", [16, 128, TPC], BF16)
        cx.consts()
        make_eps(cx)
        g_sb, t_g = load_small(cx, g_mlp, [128, 16], F32, "g_mlp")
        if emit_h:
            gn_sb, t_gn = load_small(cx, g_nxt, [128, 16], F32, "g_nxt")

        xt = cx.sb([128, 16, TT], F32, "xt")
        t_xt = S.tiles("xt", 16)
        t_xt_ld = S.tiles("xt_ld", 16)
        t_x1st = S.tiles("x1st", 16)
        ot = cx.sb([128, 16, TT], BF16, "ot")
        t_ot = S.tile("ot")
        hT = cx.sb([128, 16, TT], BF16, "hT")
        t_hT = S.tiles("hT", 16)
        t_h1st = S.tile("h1st")
        aT = cx.sb([128, 64, TT], BF16, "aT")
        t_aT = S.tiles("aT", 64)
        sqb = cx.sb([128, 16, TT], BF16, "sqb")
        t_sq = S.tiles("sq", 16)
        lnb = cx.sb([128, TT], F32, "lnb")
        t_ln = S.tile("ln")
        rstd = cx.sb([128, TT], F32, "rstd")
        t_rstd = S.tile("rstd")
        r32 = [cx.sb([128, TT], F32, "r32") for _ in range(2)]
        t_r32 = S.tiles("r32", 2)
        NW = 3
        wslot = [cx.sb([128, 4, 2048], BF16, "wslot") for _ in range(NW)]
        t_w = S.tiles("w", NW)
        pbank = [cx.ps([128, 512], F32, "pb") for _ in range(5)]
        t_pb = S.tiles("pb", 5)
        ss_ps = cx.ps([128, 512], F32, "ss")
        t_ss = S.tile("ss")
        wctr = [0]
        pctr = [0]
        obf = cx.ones_bf

        def wload(src):
            i = wctr[0] % NW
            wctr[0] += 1
            S.op("pool", lambda e: e.dma_start(out=wslot[i][:], in_=src), writes=[t_w[i]], dma=True)
            return wslot[i], t_w[i]

        def nextbank():
            i = pctr[0] % 5
            pctr[0] += 1
            return pbank[i], t_pb[i]

        def load_o(tt):
            t0 = tt * TT
            S.op("sp", lambda e: e.dma_start(out=ot[:], in_=oT[:, :, t0:t0 + TT].rearrange("c p t -> p c t")),
                 writes=[t_ot], dma=True)

        def load_x(tt):
            t0 = tt * TT
            for dc in range(16):
                S.op("sp", lambda e, dc=dc: e.dma_start(out=xt[:, dc, :], in_=xT[dc, :, t0:t0 + TT]),
                     writes=[t_xt[dc]], dma=True, dma_tile=t_xt_ld[dc])

        def square(dc):
            S.op("act", lambda e: e.activation(out=sqb[:, dc, :], in_=xt[:, dc, :], func=AF.Square),
                 reads=[t_xt[dc]], writes=[t_sq[dc]])

        def sumsq(dc):
            S.op("pe", lambda e: e.matmul(ss_ps[:], obf[:], sqb[:, dc, :], start=(dc == 0), stop=(dc == 15)),
                 reads=[t_sq[dc], cx.t_ones_bf], writes=[t_ss])

        def norm_finish(gs, t_gs):
            rstd_from_sumsq(cx, ss_ps, t_ss, TT, 1.0 / D, lnb, t_ln, rstd, t_rstd)
            for dc in range(16):
                S.op("dve", lambda e, dc=dc: e.scalar_tensor_tensor(out=hT[:, dc, :], in0=xt[:, dc, :], scalar=gs[:, dc:dc + 1],
                                                                     in1=rstd[:], op0=ALU.mult, op1=ALU.mult),
                     reads=[t_xt[dc], t_gs, t_rstd], writes=[t_hT[dc]])

        LAG = 2
        load_o(0)
        load_x(0)
        for tt in range(NT):
            t0 = tt * TT
            for grp in range(4):
                ws, tw = wload(wout[grp])
                for dc4 in range(4):
                    dc = grp * 4 + dc4
                    pb, tpb = nextbank()
                    for kc in range(16):
                        S.op("pe", lambda e, ws=ws, pb=pb, dc4=dc4, kc=kc: e.matmul(
                            pb[:], ws[:, dc4, kc * 128:(kc + 1) * 128], ot[:, kc, :], start=(kc == 0), stop=(kc == 15)),
                            reads=[tw, t_ot], writes=[tpb])
                    S.op("dve", lambda e, pb=pb, dc=dc: e.tensor_tensor(out=xt[:, dc, :], in0=pb[:], in1=xt[:, dc, :], op=ALU.add),
                         reads=[tpb, t_xt[dc]], writes=[t_xt[dc]])
                    square(dc)
                    if dc >= LAG:
                        sumsq(dc - LAG)
            for dc in range(16 - LAG, 16):
                sumsq(dc)
            if tt + 1 < NT:
                load_o(tt + 1)
            norm_finish(g_sb, t_g)
            for grp in range(16):
                ws, tw = wload(wup[grp])
                for fc4 in range(4):
                    fc = grp * 4 + fc4
                    pb, tpb = nextbank()
                    for dc in range(16):
                        S.op("pe", lambda e, ws=ws, pb=pb, fc4=fc4, dc=dc: e.matmul(
                            pb[:], ws[:, fc4, dc * 128:(dc + 1) * 128], hT[:, dc, :], start=(dc == 0), stop=(dc == 15)),
                            reads=[tw, t_hT[dc]], writes=[tpb])
                    rb = r32[fc % 2]
                    trb = t_r32[fc % 2]
                    S.op("act", lambda e, pb=pb, rb=rb: e.activation(out=rb[:], in_=pb[:], func=AF.Relu),
                         reads=[tpb], writes=[trb])
                    S.op("dve", lambda e, pb=pb, rb=rb, fc=fc: e.scalar_tensor_tensor(
                        out=aT[:, fc, :], in0=pb[:], scalar=0.0, in1=rb[:], op0=ALU.max, op1=ALU.mult),
                        reads=[tpb, trb], writes=[t_aT[fc]])
            for dc in range(16):
                ws, tw = wload(wdn[dc])
                pb, tpb = nextbank()
                for fc in range(64):
                    S.op("pe", lambda e, ws=ws, pb=pb, fc=fc: e.matmul(
                        pb[:], ws[:, fc // 16, (fc % 16) * 128:(fc % 16 + 1) * 128], aT[:, fc, :], start=(fc == 0), stop=(fc == 63)),
                        reads=[tw, t_aT[fc]], writes=[tpb])
                S.op("dve", lambda e, pb=pb, dc=dc: e.tensor_tensor(out=xt[:, dc, :], in0=pb[:], in1=xt[:, dc, :], op=ALU.add),
                     reads=[tpb, t_xt[dc]], writes=[t_xt[dc]])
                S.op("sp", lambda e, dc=dc, t0=t0: e.dma_start(out=x1T[dc, :, t0:t0 + TT], in_=xt[:, dc, :]),
                     reads=[t_xt[dc]], dma=True, dma_tile=t_x1st[dc], final=True)
                if emit_h:
                    square(dc)
                    if dc >= 1:
                        sumsq(dc - 1)
            if emit_h:
                sumsq(15)
                norm_finish(gn_sb, t_gn)
                S.op("sp", lambda e, t0=t0: e.dma_start(out=h1T[:, :, t0:t0 + TT].rearrange("c p t -> p c t"), in_=hT[:]),
                     reads=t_hT, dma=True, dma_tile=t_h1st, final=True)
            if tt + 1 < NT:
                load_x(tt + 1)
        S.finalize()
        with nc.Block() as block:
            S.emit(block)
    return nc


def qk_proj(cx, W, t_W, col0, hT, t_hT, n, pb, tpb, ss2, t_ss2, q32, t_q32, sq32, t_sq32, lnb, t_ln, r2, t_r2,
            gain, t_gain, dst_ap, t_dst):
    S = cx.S
    for dc in range(16):
        S.op("pe", lambda e, dc=dc: e.matmul(pb[:, :n], W[:, dc, col0:col0 + 128], hT[:, dc, :n], start=(dc == 0), stop=(dc == 15)),
             reads=[t_W, t_hT], writes=[tpb])
    S.op("act", lambda e: e.activation(out=q32[:, :n], in_=pb[:, :n], func=AF.Copy), reads=[tpb], writes=[t_q32])
    S.op("dve", lambda e: e.tensor_tensor(out=sq32[:, :n], in0=q32[:, :n], in1=q32[:, :n], op=ALU.mult),
         reads=[t_q32], writes=[t_sq32])
    ofb = cx.ones_bf
    S.op("pe", lambda e: e.matmul(ss2[:, :n], ofb[:], sq32[:, :n], start=True, stop=True),
         reads=[t_sq32, cx.t_ones_bf], writes=[t_ss2])
    rstd_from_sumsq(cx, ss2, t_ss2, n, 1.0 / 128, lnb, t_ln, r2, t_r2)
    S.op("dve", lambda e: e.scalar_tensor_tensor(out=dst_ap, in0=q32[:, :n], scalar=gain, in1=r2[:, :n],
                                                 op0=ALU.mult, op1=ALU.mult),
         reads=[t_q32, t_r2, t_gain], writes=[t_dst])


def build_fox(SL=S_LEN, NB=2, stop=9):
    nc = bass.Bass("TRN2", target_bir_lowering=False)
    with ExitStack() as es:
        cx = Ctx(nc, es)
        S = cx.S
        TA = 256
        NTK = SL * NB
        NJ = SL // 128
        NQB = SL // 512
        xT = cx.dram_in("xT", [NTK // TA, 128, 16 * TA], F32)
        wq_d = cx.dram_in("wq", [128, 2, 2048], F32)
        wk_d = cx.dram_in("wk", [128, 2, 2048], F32)
        wv_d = cx.dram_in("wv", [128, 16, 256], F32)
        wf_d = cx.dram_in("wf", [128, 16, 4], F32)
        gmix_d = cx.dram_in("gmix", [128, 16], F32)
        qg_d = cx.dram_in("qg", [128, 1], F32)
        kg_d = cx.dram_in("kg", [128, 1], F32)
        bf_d = cx.dram_in("bfb", [128, 4], F32)
        oT = cx.dram_out("oT", [2, 128, NTK], BF16)
        cx.consts()
        make_eps(cx)
        g_sb, t_g = load_small(cx, gmix_d, [128, 16], F32, "gmix")
        qg, t_qg0 = load_small(cx, qg_d, [128, 1], F32, "qg")
        kg, t_kg = load_small(cx, kg_d, [128, 1], F32, "kg")
        bfb, t_bfb = load_small(cx, bf_d, [128, 4], F32, "bfb")
        qgs = cx.sb([128, 1], F32, "qgs")
        t_qg = S.tile("qgs")
        S.op("dve", lambda e: e.tensor_scalar(out=qgs[:], in0=qg[:], scalar1=float(128 ** -0.5), scalar2=None, op0=ALU.mult),
             reads=[t_qg0], writes=[t_qg])
        wq_f, t_wq = load_cast_weight(cx, wq_d, [128, 2, 2048], "wq")
        wk_f, t_wk = load_cast_weight(cx, wk_d, [128, 2, 2048], "wk")
        wvf_f = cx.sb([128, 16, 260], BF16, "wvf")
        t_wvf = S.tile("wvf")
        wf32 = cx.sb([128, 16, 4], F32, "wf32")
        t_wf32 = S.tile("wf32")
        S.op("pool", lambda e: e.dma_start(out=wvf_f[:, :, 0:256], in_=wv_d), writes=[t_wvf], dma=True)
        S.op("sp", lambda e: e.dma_start(out=wf32[:], in_=wf_d), writes=[t_wf32], dma=True)
        S.op("dve", lambda e: e.tensor_copy(out=wvf_f[:, :, 256:260], in_=wf32[:]), reads=[t_wf32, t_wvf], writes=[t_wvf])
        wq = wq_f[:].rearrange("p a (c n) -> p (a c) n", n=256)
        wk = wk_f[:].rearrange("p a (c n) -> p (a c) n", n=256)
        wvf = wvf_f[:]
        tri = cx.sb([128, 128], BF16, "tri")
        t_tri = S.tile("tri")
        of = cx.ones_f
        obf_ = cx.ones_bf
        S.op("pool", lambda e: e.affine_select(out=tri[:], in_=obf_[:], pattern=[[1, 128]], compare_op=ALU.is_ge, fill=0.0,
                                               base=0, channel_multiplier=-1),
             reads=[cx.t_ones_bf], writes=[t_tri])

        if stop == 0:
            S.finalize()
            with nc.Block() as block:
                S.emit(block)
            return nc
        xt = [cx.sb([128, 16, TA], F32, "xt") for _ in range(2)]
        t_xt = [S.tiles("xt", 1) * 16 for _ in range(2)]
        hT = [cx.sb([128, 16, TA], BF16, "hT") for _ in range(2)]
        t_hT = [S.tiles("hT", 16) for _ in range(2)]
        sqb = cx.sb([128, 16, TA], BF16, "sqb")
        t_sq = S.tile("sq")
        lnb = [cx.sb([128, TA], F32, "lnb") for _ in range(2)]
        t_ln = S.tiles("ln", 2)
        rstd = cx.sb([128, TA], F32, "rstd")
        t_rstd = S.tile("rstd")
        NQ = 4
        q32 = [cx.sb([128, TA], F32, "q32") for _ in range(NQ)]
        t_q32 = S.tiles("q32", NQ)
        sq32 = [cx.sb([128, TA], BF16, "sq32") for _ in range(NQ)]
        t_sq32 = S.tiles("sq32", NQ)
        r2 = [cx.sb([128, TA], F32, "r2") for _ in range(NQ)]
        t_r2 = S.tiles("r2", NQ)
        QT = [cx.sb([128, SL], BF16, "QT") for _ in range(2)]
        KT = [cx.sb([128, SL], BF16, "KT") for _ in range(2)]
        V = cx.sb([128, 2, NJ, 128], BF16, "V")
        t_QT = S.tiles("QT", 2)
        t_KT = S.tiles("KT", 2)
        t_V = S.tile("V")
        Z = cx.sb([128, NJ, 4], F32, "Z")
        t_Z = S.tile("Z")
        E = cx.sb([128, NJ, 2], F32, "E")
        t_E = S.tile("E")
        SP = cx.sb([128, 2, NJ], F32, "SP")
        t_SP = S.tile("SP")
        SPp = [cx.sb([128, 2, NJ], BF16, "SPp") for _ in range(3)]
        t_SPp = S.tile("SPp")
        SPr = cx.sb([128, 2, NJ], F32, "SPr")
        t_SPr = S.tile("SPr")
        tot = cx.sb([128, NJ], F32, "tot")
        t_tot = S.tile("tot")
        cum = cx.sb([128, NJ], F32, "cum")
        t_cum = S.tile("cum")
        excl = cx.sb([128, NJ], F32, "excl")
        t_excl = S.tile("excl")
        negc = cx.sb([128, NJ], F32, "negc")
        t_negc = S.tile("negc")
        bias = cx.sb([128, NQB, NJ], F32, "bias")
        t_bias = S.tile("bias")
        NP = 4
        pT = [cx.sb([128, 512], BF16, "pT") for _ in range(NP)]
        t_pT = S.tiles("pT", NP)
        rden = cx.sb([128, 512], F32, "rden")
        t_rden = S.tile("rden")
        ob = [cx.sb([128, 512], BF16, "ob") for _ in range(2)]
        t_ob = S.tiles("ob", 2)
        banks = [cx.ps([128, 512], F32, "bk") for _ in range(8)]
        t_bk = S.tiles("bk", 8)
        obf = cx.ones_bf
        NTI = SL // TA

        for b in range(NB):
            pa = [0]
            qc = [0]

            def load_x(ti):
                tok0 = b * SL + ti * TA
                xtb = xt[ti % 2]
                gt = tok0 // TA
                S.op("sp", lambda e, xtb=xtb, gt=gt: e.dma_start(out=xtb[:].rearrange("p c t -> p (c t)"), in_=xT[gt]),
                     writes=[t_xt[ti % 2][0]], dma=True)

            def norm_p1(ti):
                xtb = xt[ti % 2]
                S.op("act", lambda e: e.activation(out=sqb[:], in_=xtb[:], func=AF.Square), reads=t_xt[ti % 2][:1], writes=[t_sq])

            def norm_p2(ti):
                for dc in range(16):
                    S.op("pe", lambda e, dc=dc: e.matmul(banks[7][:, :TA], obf[:], sqb[:, dc, :], start=(dc == 0), stop=(dc == 15)),
                         reads=[t_sq, cx.t_ones_bf], writes=[t_bk[7]])
                rstd_from_sumsq(cx, banks[7], t_bk[7], TA, 1.0 / D, lnb[0], t_ln[0], rstd, t_rstd)

            def norm_p3(ti, lo_=0, hi_=16):
                xtb, hTb = xt[ti % 2], hT[ti % 2]
                for dc in range(lo_, hi_):
                    S.op("dve", lambda e, dc=dc: e.scalar_tensor_tensor(out=hTb[:, dc, :], in0=xtb[:, dc, :], scalar=g_sb[:, dc:dc + 1],
                                                                         in1=rstd[:], op0=ALU.mult, op1=ALU.mult),
                         reads=[t_xt[ti % 2][0], t_g, t_rstd], writes=[t_hT[ti % 2][dc]])

            combos = [(wq, t_wq, QT, t_QT, qgs, t_qg, 0), (wq, t_wq, QT, t_QT, qgs, t_qg, 1),
                      (wk, t_wk, KT, t_KT, kg, t_kg, 0), (wk, t_wk, KT, t_KT, kg, t_kg, 1)]

            def part1(ti, k):
                W, t_W, dstl, t_dstl, gain, t_gain, hd = combos[k]
                hTb = hT[ti % 2]
                pbi = pa[0] % 5
                pa[0] += 1
                qi_ = qc[0] % NQ
                qc[0] += 1
                pb, tpb = banks[pbi], t_bk[pbi]
                for dc in range(16):
                    S.op("pe", lambda e, dc=dc: e.matmul(pb[:, :TA], W[:, dc, hd * 128:hd * 128 + 128], hTb[:, dc, :], start=(dc == 0), stop=(dc == 15)),
                         reads=[t_W, t_hT[ti % 2][dc]], writes=[tpb])
                S.op("act", lambda e: e.activation(out=q32[qi_][:], in_=pb[:, :TA], func=AF.Copy), reads=[tpb], writes=[t_q32[qi_]])
                S.op("pool", lambda e: e.tensor_tensor(out=sq32[qi_][:], in0=q32[qi_][:], in1=q32[qi_][:], op=ALU.mult),
                     reads=[t_q32[qi_]], writes=[t_sq32[qi_]])
                return qi_

            def part2(ti, k, qi_):
                W, t_W, dstl, t_dstl, gain, t_gain, hd = combos[k]
                sb_ = 5 + (qi_ % 2)
                S.op("pe", lambda e: e.matmul(banks[sb_][:, :TA], obf[:], sq32[qi_][:], start=True, stop=True),
                     reads=[t_sq32[qi_], cx.t_ones_bf], writes=[t_bk[sb_]])
                rstd_from_sumsq(cx, banks[sb_], t_bk[sb_], TA, 1.0 / 128, lnb[1], t_ln[1], r2[qi_], t_r2[qi_])
                S.op("dve", lambda e: e.scalar_tensor_tensor(out=dstl[hd][:, ti * TA:(ti + 1) * TA], in0=q32[qi_][:], scalar=gain[:, 0:1],
                                                             in1=r2[qi_][:], op0=ALU.mult, op1=ALU.mult),
                     reads=[t_q32[qi_], t_r2[qi_], t_gain], writes=[t_dstl[hd]])

            def vproj(ti, sub):
                hTb = hT[ti % 2]
                j = ti * (TA // 128) + sub
                pbi = pa[0] % 5
                pa[0] += 1
                pb, tpb = banks[pbi], t_bk[pbi]
                for dc in range(16):
                    S.op("pe", lambda e, dc=dc: e.matmul(pb[:, :260], hTb[:, dc, sub * 128:(sub + 1) * 128], wvf[:, dc, :], start=(dc == 0), stop=(dc == 15)),
                         reads=[t_hT[ti % 2][dc], t_wvf], writes=[tpb])
                S.op("dve", lambda e: e.tensor_tensor(out=Z[:, j, :], in0=pb[:, 256:260], in1=bfb[:], op=ALU.add),
                     reads=[tpb, t_bfb], writes=[t_Z])
                for hd_ in range(2):
                    S.op("act", lambda e, hd_=hd_: e.activation(out=V[:, hd_, j, :], in_=pb[:, hd_ * 128:(hd_ + 1) * 128], func=AF.Copy),
                         reads=[tpb, t_Z], writes=[t_V])

            load_x(0)
            if NTI > 1:
                load_x(1)
            norm_p1(0)
            norm_p2(0)
            norm_p3(0)
            if NTI > 1:
                norm_p1(1)
            for ti in range(NTI):
                nxt = ti + 1 < NTI
                if ti + 2 < NTI:
                    load_x(ti + 2)
                a0 = part1(ti, 0)
                a1 = part1(ti, 1)
                if nxt:
                    norm_p2(ti + 1)
                    norm_p3(ti + 1)
                part2(ti, 0, a0)
                a2 = part1(ti, 2)
                part2(ti, 1, a1)
                a3 = part1(ti, 3)
                part2(ti, 2, a2)
                vproj(ti, 0)
                if ti + 2 < NTI:
                    norm_p1(ti + 2)
                part2(ti, 3, a3)
                vproj(ti, 1)
            if stop == 1:
                break
            S.op("act", lambda e: e.activation(out=E[:], in_=Z[:, :, 0:2], func=AF.Exp, scale=-1.0), reads=[t_Z], writes=[t_E])
            S.op("act", lambda e: e.activation(out=SP[:].rearrange("p h j -> p j h"), in_=E[:], func=AF.Ln, bias=1.0, scale=1.0),
                 reads=[t_E], writes=[t_SP])
            S.op("dve", lambda e: e.tensor_copy(out=SPp[0][:], in_=SP[:]), reads=[t_SP], writes=[t_SPp])
            S.op("dve", lambda e: e.tensor_tensor(out=SPr[:], in0=SP[:], in1=SPp[0][:], op=ALU.subtract), reads=[t_SP, t_SPp], writes=[t_SPr])
            S.op("dve", lambda e: e.tensor_copy(out=SPp[1][:], in_=SPr[:]), reads=[t_SPr], writes=[t_SPp])
            S.op("dve", lambda e: e.tensor_tensor(out=SPr[:], in0=SPr[:], in1=SPp[1][:], op=ALU.subtract), reads=[t_SPr, t_SPp], writes=[t_SPr])
            S.op("dve", lambda e: e.tensor_copy(out=SPp[2][:], in_=SPr[:]), reads=[t_SPr], writes=[t_SPp])
            pc = [0]
            sc = [0]
            for hd in range(2):
                for pc_ in range(3):
                    S.op("pe", lambda e, hd=hd, pc_=pc_: e.matmul(banks[0][:, :NJ], tri[:], SPp[pc_][:, hd, :], start=(pc_ == 0), stop=(pc_ == 2)),
                         reads=[t_tri, t_SPp], writes=[t_bk[0]])
                for pc_ in range(3):
                    S.op("pe", lambda e, hd=hd, pc_=pc_: e.matmul(banks[1][:, :NJ], obf_[:], SPp[pc_][:, hd, :], start=(pc_ == 0), stop=(pc_ == 2)),
                         reads=[cx.t_ones_bf, t_SPp], writes=[t_bk[1]])
                S.op("dve", lambda e: e.tensor_copy(out=tot[:], in_=banks[1][:, :NJ]), reads=[t_bk[1]], writes=[t_tot])
                S.op("dve", lambda e: e.tensor_tensor_scan(out=cum[:], data0=of[:, 0:NJ], data1=tot[:], initial=0.0,
                                                           op0=ALU.mult, op1=ALU.add),
                     reads=[t_tot, cx.t_ones_f], writes=[t_cum])
                S.op("dve", lambda e: e.tensor_tensor(out=excl[:], in0=cum[:], in1=tot[:], op=ALU.subtract),
                     reads=[t_cum, t_tot], writes=[t_excl])
                S.op("dve", lambda e: e.tensor_tensor(out=negc[:], in0=banks[0][:, :NJ], in1=excl[:], op=ALU.add),
                     reads=[t_bk[0], t_excl], writes=[t_negc])
                for qb in range(NQB):
                    nj = 4 * qb + 4
                    S.op("dve", lambda e, qb=qb, nj=nj: e.tensor_scalar(
                        out=bias[:, qb, 0:nj], in0=negc[:, 0:nj], scalar1=excl[:, 4 * qb:4 * qb + 1], scalar2=None, op0=ALU.subtract),
                        reads=[t_negc, t_excl], writes=[t_bias])
                if stop == 2:
                    continue
                pairs = []
                for qb in range(NQB):
                    for j in range(4 * qb + 4):
                        pairs.append((qb, j))
                LA = 3
                slots = {}

                def qk(idx):
                    qb, j = pairs[idx]
                    d = max(0, j - 4 * qb)
                    c0 = 128 * d
                    n = 512 - c0
                    si = (0, 1, 2, 7)[sc[0] % 4]
                    sc[0] += 1
                    slots[idx] = si
                    q0 = qb * 512
                    S.op("pe", lambda e, hd=hd: e.matmul(banks[si][:, :n], KT[hd][:, j * 128:(j + 1) * 128], QT[hd][:, q0 + c0:q0 + 512], start=True, stop=True),
                         reads=[t_KT[hd], t_QT[hd]], writes=[t_bk[si]])

                for idx in range(min(LA, len(pairs))):
                    qk(idx)
                for idx, (qb, j) in enumerate(pairs):
                    q0 = qb * 512
                    oi = qb % 2
                    ops_, t_ops = banks[3 + oi], t_bk[3 + oi]
                    dps, t_dps = banks[5 + oi], t_bk[5 + oi]
                    nj = 4 * qb + 4
                    d = max(0, j - 4 * qb)
                    c0 = 128 * d
                    n = 512 - c0
                    si = slots.pop(idx)
                    sps, t_sps = banks[si], t_bk[si]
                    pi = pc[0] % NP
                    pc[0] += 1
                    p_, t_p = pT[pi], t_pT[pi]
                    S.op("act", lambda e, sps=sps, p_=p_, qb=qb, j=j, n=n: e.activation(
                        out=p_[:, :n], in_=sps[:, :n], func=AF.Exp, bias=bias[:, qb, j:j + 1], scale=1.0),
                        reads=[t_sps, t_bias], writes=[t_p])
                    if j >= 4 * qb:
                        S.op("pool", lambda e, p_=p_: e.affine_select(
                            out=p_[:, 0:128], in_=p_[:, 0:128], pattern=[[1, 128]], compare_op=ALU.is_ge, fill=0.0,
                            base=0, channel_multiplier=-1), reads=[t_p], writes=[t_p])
                    if idx + LA < len(pairs):
                        qk(idx + LA)
                    S.op("pe", lambda e, ops_=ops_, p_=p_, hd=hd, j=j, c0=c0, n=n, nj=nj: e.matmul(
                        ops_[:, c0:512], V[:, hd, j, :], p_[:, :n], start=(j == 0), stop=(j == nj - 1)),
                        reads=[t_V, t_p], writes=[t_ops])
                    S.op("pe", lambda e, dps=dps, p_=p_, j=j, c0=c0, n=n, nj=nj: e.matmul(
                        dps[:, c0:512], obf[:], p_[:, :n], start=(j == 0), stop=(j == nj - 1)),
                        reads=[cx.t_ones_bf, t_p], writes=[t_dps])
                    if j == nj - 1:
                        S.op("dve", lambda e, dps=dps: e.reciprocal(out=rden[:], in_=dps[:]), reads=[t_dps], writes=[t_rden])
                        obb, t_obb = ob[oi], t_ob[oi]
                        S.op("dve", lambda e, ops_=ops_, obb=obb: e.tensor_tensor(out=obb[:], in0=ops_[:], in1=rden[:], op=ALU.mult),
                             reads=[t_ops, t_rden], writes=[t_obb])
                        tk = b * SL + q0
                        S.op("sp", lambda e, obb=obb, hd=hd, tk=tk: e.dma_start(out=oT[hd, :, tk:tk + 512], in_=obb[:]),
                             reads=[t_obb], dma=True, dma_tile=t_obb, final=True)
        S.finalize()
        with nc.Block() as block:
            S.emit(block)
    return nc


DIL_R = (1, 4, 16)


def build_dil(SL=S_LEN, NB=2):
    nc = bass.Bass("TRN2", target_bir_lowering=False)
    with ExitStack() as es:
        cx = Ctx(nc, es)
        S = cx.S
        TA = 256
        SB = 2048
        NTK = SL * NB
        hT_d = cx.dram_in("hT", [NTK // TA, 128, 16 * TA], BF16)
        wqk_d = cx.dram_in("wqk", [128, 16, 768], F32)
        wv_d = cx.dram_in("wv", [128, 2, 2048], F32)
        qg_d = cx.dram_in("qg", [128, 3], F32)
        kg_d = cx.dram_in("kg", [128, 3], F32)
        bm_d = cx.dram_in("bmat", [128, 3, 256], F32)
        vd = cx.dram_out("vscratch", [NTK, 256], BF16)
        oT = cx.dram_out("oT", [2, 128, NTK], BF16)
        cx.consts()
        make_eps(cx)
        qg, t_qg0 = load_small(cx, qg_d, [128, 3], F32, "qg")
        kg, t_kg = load_small(cx, kg_d, [128, 3], F32, "kg")
        bm, t_bm = load_small(cx, bm_d, [128, 3, 256], F32, "bm")
        qgs = cx.sb([128, 3], F32, "qgs")
        t_qg = S.tile("qgs")
        S.op("dve", lambda e: e.tensor_scalar(out=qgs[:], in0=qg[:], scalar1=float(128 ** -0.5), scalar2=None, op0=ALU.mult),
             reads=[t_qg0], writes=[t_qg])
        wqk_f, t_wqk = load_cast_weight(cx, wqk_d, [128, 16, 768], "wqk")
        wv_f, t_wv = load_cast_weight(cx, wv_d, [128, 2, 2048], "wv")
        wqk = wqk_f[:]
        wv = wv_f[:].rearrange("p a (c n) -> p (a c) n", n=256)

        hT = [cx.sb([128, 16, TA], BF16, "hT") for _ in range(2)]
        t_hT = S.tiles("hT", 2)
        lnb = cx.sb([128, TA], F32, "lnb")
        t_ln = S.tile("ln")
        NQ = 3
        q32 = [cx.sb([128, TA], F32, "q32") for _ in range(NQ)]
        t_q32 = S.tiles("q32", NQ)
        sq32 = [cx.sb([128, TA], BF16, "sq32") for _ in range(NQ)]
        t_sq32 = S.tiles("sq32", NQ)
        r2 = [cx.sb([128, TA], F32, "r2") for _ in range(NQ)]
        t_r2 = S.tiles("r2", NQ)
        Qs = [cx.sb([128, SB], BF16, "Qs") for _ in range(3)]
        t_Qs = S.tiles("Qs", 3)
        Ks = [[cx.sb([128, SB], BF16, "Ks") for _ in range(2)] for _ in range(3)]
        t_Ks = [S.tiles("Ks", 2) for _ in range(3)]
        vst = [cx.sb([128, 256], BF16, "vst") for _ in range(4)]
        t_vst = S.tiles("vst", 4)
        vbuf = [cx.sb([128, 8192], BF16, "vbuf") for _ in range(2)]
        t_vbuf = S.tiles("vbuf", 2)
        acc = cx.sb([128, 3, SB], F32, "acc")
        t_acc = S.tile("acc")
        rden = cx.sb([128, SB], F32, "rden")
        t_rden = S.tile("rden")
        ob = cx.sb([128, 2, SB], BF16, "ob")
        t_ob = S.tile("ob")
        NST = 4
        st = [cx.sb([128, 256], F32, "st") for _ in range(NST)]
        t_st = S.tiles("st", NST)
        pT = [cx.sb([128, 256], BF16, "pT") for _ in range(NST)]
        t_pT = S.tiles("pT", NST)
        banks = [cx.ps([128, 512], F32, "bk") for _ in range(8)]
        t_bk = S.tiles("bk", 8)
        obf = cx.ones_bf
        SPS_B = (0, 1, 2, 5)
        PO_B = (3, 4, 7, 6)

        t_vstore_pool = S.tiles("vstore", 16)
        vctr = [0]
        vbc = [0]
        pa = [0]
        qc = [0]
        NSB = SL // SB
        NTI = SB // TA
        tiles_all = [(b, sb, ti) for b in range(NB) for sb in range(NSB) for ti in range(NTI)]

        def load_h(gi):
            b_, sb_, ti_ = tiles_all[gi]
            tok0 = b_ * SL + sb_ * SB + ti_ * TA
            hTb = hT[gi % 2]
            gt = tok0 // TA
            S.op("sp", lambda e: e.dma_start(out=hTb[:].rearrange("p c t -> p (c t)"), in_=hT_d[gt]),
                 writes=[t_hT[gi % 2]], dma=True)

        load_h(0)
        gi = 0
        for b in range(NB):
            t_vstore = []
            for sb in range(NSB):
                cur = sb % 2
                prv = 1 - cur
                stores_this = []

                def part1(gi, ti, k):
                    which, g = divmod(k, 3)
                    hTb = hT[gi % 2]
                    pbi = pa[0] % 5
                    pa[0] += 1
                    qi_ = qc[0] % NQ
                    qc[0] += 1
                    pb, tpb = banks[pbi], t_bk[pbi]
                    col0 = k * 128
                    for dc in range(16):
                        S.op("pe", lambda e, dc=dc: e.matmul(pb[:, :TA], wqk[:, dc, col0:col0 + 128], hTb[:, dc, :], start=(dc == 0), stop=(dc == 15)),
                             reads=[t_wqk, t_hT[gi % 2]], writes=[tpb])
                    S.op("act", lambda e: e.activation(out=q32[qi_][:], in_=pb[:, :TA], func=AF.Copy), reads=[tpb], writes=[t_q32[qi_]])
                    S.op("pool", lambda e: e.tensor_tensor(out=sq32[qi_][:], in0=q32[qi_][:], in1=q32[qi_][:], op=ALU.mult),
                         reads=[t_q32[qi_]], writes=[t_sq32[qi_]])
                    return qi_

                def part2(gi, ti, k, qi_, cur=cur):
                    which, g = divmod(k, 3)
                    r = DIL_R[g]
                    if which == 0:
                        dst_t, t_dst, gain, t_gain = Qs[g], t_Qs[g], qgs[:, g:g + 1], t_qg
                    else:
                        dst_t, t_dst, gain, t_gain = Ks[g][cur], t_Ks[g][cur], kg[:, g:g + 1], t_kg
                    a0 = ti * TA // r
                    if r == 1:
                        dst_ap = dst_t[:, ti * TA:(ti + 1) * TA]
                        in0 = q32[qi_][:]
                        in1 = r2[qi_][:]
                    else:
                        dst_ap = dst_t[:].rearrange("p (b a) -> p b a", b=r)[:, :, a0:a0 + TA // r]
                        in0 = q32[qi_][:].rearrange("p (a b) -> p b a", b=r)
                        in1 = r2[qi_][:].rearrange("p (a b) -> p b a", b=r)
                    sb_ = 5 + (qi_ % 2)
                    S.op("pe", lambda e: e.matmul(banks[sb_][:, :TA], obf[:], sq32[qi_][:], start=True, stop=True),
                         reads=[t_sq32[qi_], cx.t_ones_bf], writes=[t_bk[sb_]])
                    rstd_from_sumsq(cx, banks[sb_], t_bk[sb_], TA, 1.0 / 128, lnb, t_ln, r2[qi_], t_r2[qi_])
                    S.op("dve", lambda e: e.scalar_tensor_tensor(out=dst_ap, in0=in0, scalar=gain, in1=in1, op0=ALU.mult, op1=ALU.mult),
                         reads=[t_q32[qi_], t_r2[qi_], t_gain], writes=[t_dst])

                def vproj(gi, ti, sub, b=b, sb=sb):
                    hTb = hT[gi % 2]
                    pbi = pa[0] % 5
                    pa[0] += 1
                    pb, tpb = banks[pbi], t_bk[pbi]
                    for dc in range(16):
                        S.op("pe", lambda e, dc=dc: e.matmul(pb[:, :256], hTb[:, dc, sub * 128:(sub + 1) * 128], wv[:, dc, :], start=(dc == 0), stop=(dc == 15)),
                             reads=[t_hT[gi % 2], t_wv], writes=[tpb])
                    vi = vctr[0] % 4
                    vctr[0] += 1
                    S.op("act", lambda e: e.activation(out=vst[vi][:], in_=pb[:, 0:256], func=AF.Copy), reads=[tpb], writes=[t_vst[vi]])
                    tk = b * SL + sb * SB + ti * TA + sub * 128
                    t_store = t_vstore_pool[(ti * (TA // 128) + sub) % 16]
                    S.op("sp", lambda e: e.dma_start(out=vd[tk:tk + 128, :], in_=vst[vi][:]),
                         reads=[t_vst[vi]], writes=[t_store], dma=True, dma_tile=t_store)
                    stores_this.append(t_store)

                for ti in range(NTI):
                    if gi + 1 < len(tiles_all):
                        load_h(gi + 1)
                    a = [None] * 6
                    a[0] = part1(gi, ti, 0)
                    a[1] = part1(gi, ti, 1)
                    for k in range(2, 6):
                        part2(gi, ti, k - 2, a[k - 2])
                        a[k] = part1(gi, ti, k)
                    part2(gi, ti, 4, a[4])
                    vproj(gi, ti, 0)
                    part2(gi, ti, 5, a[5])
                    vproj(gi, ti, 1)
                    gi += 1

                first = True
                has_prev_sb = sb > 0
                lo = -1 if has_prev_sb else 0
                for g in range(3):
                    r = DIL_R[g]
                    nrow = 16 // r
                    Lsb = SB // r
                    vb_i = vbc[0] % 2
                    vbc[0] += 1
                    vb, t_vb = vbuf[vb_i], t_vbuf[vb_i]
                    ntile = nrow - lo
                    base_row = (b * SL + sb * SB) // r
                    vsrc = vd.rearrange("(n x) d -> n (x d)", x=r)
                    r0 = base_row + lo * 128
                    src = vsrc[r0:r0 + ntile * 128, :].rearrange("(n i) x -> i n x", i=128)
                    vview = vb[:, 0:ntile * r * 256].rearrange("p (n x) -> p n x", x=r * 256)
                    deps = list(stores_this) + (list(t_vstore) if has_prev_sb else [])
                    S.op("sp", lambda e, vview=vview, src=src: e.dma_start(out=vview, in_=src),
                         reads=deps, writes=[t_vb], dma=True)
                    accv = acc[:].rearrange("p h (a b) -> p h a b", b=r)
                    blocks = [(rb, n_) for rb in range(r) for n_ in range(nrow)]
                    LA = 4

                    def sps_mm(i, g=g, cur=cur, prv=prv, Lsb=Lsb):
                        rb, n_ = blocks[i]
                        has_prev = has_prev_sb or n_ > 0
                        c_cur = rb * Lsb + n_ * 128
                        qblk = Qs[g][:, c_cur:c_cur + 128]
                        kcur = Ks[g][cur][:, c_cur:c_cur + 128]
                        if n_ > 0:
                            kprev = Ks[g][cur][:, c_cur - 128:c_cur]
                            t_kprev = t_Ks[g][cur]
                        else:
                            kprev = Ks[g][prv][:, rb * Lsb + Lsb - 128:rb * Lsb + Lsb]
                            t_kprev = t_Ks[g][prv]
                        bi_ = SPS_B[i % 4]
                        sps, t_sps = banks[bi_], t_bk[bi_]
                        if has_prev:
                            S.op("pe", lambda e: e.matmul(sps[:, 0:128], kprev, qblk, start=True, stop=True),
                                 reads=[t_kprev, t_Qs[g]], writes=[t_sps])
                        S.op("pe", lambda e: e.matmul(sps[:, 128:256], kcur, qblk, start=True, stop=True),
                             reads=[t_Ks[g][cur], t_Qs[g]], writes=[t_sps])

                    def add_bias(i, g=g):
                        rb, n_ = blocks[i]
                        has_prev = has_prev_sb or n_ > 0
                        c_lo = 0 if has_prev else 128
                        bi_ = SPS_B[i % 4]
                        sps, t_sps = banks[bi_], t_bk[bi_]
                        stb, t_stb = st[i % NST], t_st[i % NST]
                        S.op("dve", lambda e: e.tensor_tensor(
                            out=stb[:, c_lo:256], in0=sps[:, c_lo:256], in1=bm[:, g, c_lo:256], op=ALU.add),
                            reads=[t_sps, t_bm], writes=[t_stb])

                    for i in range(min(LA, len(blocks))):
                        sps_mm(i)
                    add_bias(0)
                    if len(blocks) > 1:
                        add_bias(1)
                    for i, (rb, n_) in enumerate(blocks):
                        has_prev = has_prev_sb or n_ > 0
                        c_lo = 0 if has_prev else 128
                        stb, t_stb = st[i % NST], t_st[i % NST]
                        p_, t_p = pT[i % NST], t_pT[i % NST]
                        S.op("act", lambda e, stb=stb, p_=p_, c_lo=c_lo: e.activation(out=p_[:, c_lo:256], in_=stb[:, c_lo:256], func=AF.Exp),
                             reads=[t_stb], writes=[t_p])
                        if i + 2 < len(blocks):
                            add_bias(i + 2)
                        if i + LA < len(blocks):
                            sps_mm(i + LA)
                        pb_ = PO_B[i % 4]
                        po, t_po = banks[pb_], t_bk[pb_]
                        ti_v = n_ - lo
                        for h in range(3):
                            if h < 2:
                                lc = vview[:, ti_v, rb * 256 + h * 128:rb * 256 + (h + 1) * 128]
                                lp = vview[:, ti_v - 1, rb * 256 + h * 128:rb * 256 + (h + 1) * 128] if has_prev else None
                                rd = [t_vb, t_p]
                            else:
                                lc = obf[:]
                                lp = obf[:]
                                rd = [cx.t_ones_bf, t_p]
                            if has_prev:
                                S.op("pe", lambda e, po=po, lp=lp, p_=p_, h=h: e.matmul(po[:, h * 128:(h + 1) * 128], lp, p_[:, 0:128], start=True, stop=False),
                                     reads=rd, writes=[t_po])
                            S.op("pe", lambda e, po=po, lc=lc, p_=p_, h=h, has_prev=has_prev: e.matmul(
                                po[:, h * 128:(h + 1) * 128], lc, p_[:, 128:256], start=(not has_prev), stop=True),
                                reads=rd, writes=[t_po])
                        dst = accv[:, :, n_ * 128:(n_ + 1) * 128, rb]
                        src_po = po[:, 0:384].rearrange("p (h q) -> p h q", h=3)
                        if first:
                            S.op("dve", lambda e, dst=dst, src_po=src_po: e.tensor_copy(out=dst, in_=src_po),
                                 reads=[t_po], writes=[t_acc])
                        else:
                            S.op("dve", lambda e, dst=dst, src_po=src_po: e.tensor_tensor(out=dst, in0=src_po, in1=dst, op=ALU.add),
                                 reads=[t_po, t_acc], writes=[t_acc])
                    first = False
                t_vstore = stores_this
                S.op("dve", lambda e: e.reciprocal(out=rden[:], in_=acc[:, 2, :]), reads=[t_acc], writes=[t_rden])
                for h in range(2):
                    S.op("dve", lambda e, h=h: e.tensor_tensor(out=ob[:, h, :], in0=acc[:, h, :], in1=rden[:], op=ALU.mult),
                         reads=[t_acc, t_rden], writes=[t_ob])
                tk = b * SL + sb * SB
                t_ost = S.tile("ost")
                S.op("sp", lambda e, tk=tk: e.dma_start(out=oT[:, :, tk:tk + SB].rearrange("h p t -> p h t"), in_=ob[:]),
                     reads=[t_ob], writes=[t_ost], dma=True, dma_tile=t_ost, final=True)
        S.finalize()
        with nc.Block() as block:
            S.emit(block)
    return nc


def qk_proj_perm(cx, W, t_W, col0, hT, t_hT, n, pb, tpb, ss2, t_ss2, q32, t_q32, sq32, t_sq32, lnb, t_ln, r2, t_r2,
                 gain, t_gain, dst_ap, t_dst, r):
    S = cx.S
    for dc in range(16):
        S.op("pe", lambda e, dc=dc: e.matmul(pb[:, :n], W[:, dc, col0:col0 + 128], hT[:, dc, :n], start=(dc == 0), stop=(dc == 15)),
             reads=[t_W, t_hT], writes=[tpb])
    S.op("act", lambda e: e.activation(out=q32[:, :n], in_=pb[:, :n], func=AF.Copy), reads=[tpb], writes=[t_q32])
    S.op("dve", lambda e: e.tensor_tensor(out=sq32[:, :n], in0=q32[:, :n], in1=q32[:, :n], op=ALU.mult),
         reads=[t_q32], writes=[t_sq32])
    ofb = cx.ones_bf
    S.op("pe", lambda e: e.matmul(ss2[:, :n], ofb[:], sq32[:, :n], start=True, stop=True),
         reads=[t_sq32, cx.t_ones_bf], writes=[t_ss2])
    rstd_from_sumsq(cx, ss2, t_ss2, n, 1.0 / 128, lnb, t_ln, r2, t_r2)
    if r == 1:
        in0 = q32[:, :n]
        in1 = r2[:, :n]
    else:
        in0 = q32[:, :n].rearrange("p (a b) -> p b a", b=r)
        in1 = r2[:, :n].rearrange("p (a b) -> p b a", b=r)
    S.op("dve", lambda e: e.scalar_tensor_tensor(out=dst_ap, in0=in0, scalar=gain, in1=in1, op0=ALU.mult, op1=ALU.mult),
         reads=[t_q32, t_r2, t_gain], writes=[t_dst])


_CACHE = {}


def _prog(name, fn, *a):
    if name not in _CACHE:
        _CACHE[name] = fn(*a)
    return _CACHE[name]


def _fm(xT2d):
    return np.ascontiguousarray(xT2d.reshape(16, 128, xT2d.shape[1]))


def _tiles(xT2d, ta=256):
    T = xT2d.shape[1]
    a = xT2d.reshape(16, 128, T // ta, ta).transpose(2, 1, 0, 3)
    return np.ascontiguousarray(a.reshape(T // ta, 128, 16 * ta))


def _wchunks(w, ngrp, per):
    K, N = w.shape
    kc = K // 128
    cb = N // 128
    a = w.reshape(kc, 128, cb, 128)
    a = a.transpose(2, 1, 0, 3)
    a = a.reshape(ngrp, per, 128, kc, 128).transpose(0, 2, 1, 3, 4)
    return np.ascontiguousarray(a.reshape(ngrp, 128, 4, (per * kc * 128) // 4))


def _wcols(w, cols):
    a = w[:, cols].reshape(16, 128, len(cols)).transpose(1, 0, 2)
    return np.ascontiguousarray(a)


def _gvec(g):
    return np.ascontiguousarray(g.reshape(16, 128).T)


def _run(nc, in_maps):
    res = run_bass_kernel_spmd(nc, in_maps, core_ids=list(range(NCORES)))
    return res.results


def kernel(x, fox_w_in, fox_b_f, fox_q_gain, fox_k_gain, fox_w_out, dil_w_in, dil_q_gain, dil_k_gain, dil_w_out,
           mix_norm_g, mlp_norm_g, mlp_w_up, mlp_w_down):
    f32 = np.float32
    x = np.asarray(x, f32)
    xT = np.ascontiguousarray(x.reshape(NTOK, D).T)
    xT_tl = _tiles(xT)

    w_in = np.asarray(fox_w_in[0], f32)
    H = 16
    fox = _prog("fox", build_fox)
    maps = []
    for c in range(NCORES):
        hA, hB = 2 * c, 2 * c + 1
        qcols = list(range(hA * 128, hA * 128 + 128)) + list(range(hB * 128, hB * 128 + 128))
        kcols = [H * 128 + q for q in qcols]
        vcols = [2 * H * 128 + q for q in qcols] + [3 * H * 128 + hA, 3 * H * 128 + hB]
        maps.append({
            "xT": xT_tl,
            "wq": _wcols(w_in, qcols).reshape(128, 2, 2048),
            "wk": _wcols(w_in, kcols).reshape(128, 2, 2048),
            "wv": _wcols(w_in, vcols[:256]),
            "wf": np.ascontiguousarray(np.pad(_wcols(w_in, vcols[256:]), ((0, 0), (0, 0), (0, 2)))),
            "gmix": _gvec(np.asarray(mix_norm_g[0], f32)),
            "qg": np.asarray(fox_q_gain[0], f32).reshape(128, 1),
            "kg": np.asarray(fox_k_gain[0], f32).reshape(128, 1),
            "bfb": np.ascontiguousarray(np.pad(np.broadcast_to(np.asarray(fox_b_f[0], f32)[[hA, hB]][None, :], (128, 2)), ((0, 0), (0, 2)))),
        })
    r = _run(fox, maps)
    oT_all = np.concatenate([r[c]["oT"].reshape(256, NTOK) for c in range(NCORES)], axis=0)

    mlp_h = _prog("mlp_h", build_mlp, True)
    wout_l = _wchunks(np.asarray(fox_w_out[0], f32), 4, 4)
    wup_l = _wchunks(np.asarray(mlp_w_up[0], f32), 16, 4)
    wdn = np.asarray(mlp_w_down[0], f32)
    wdn_l = np.ascontiguousarray(wdn.reshape(64, 128, 16, 128).transpose(2, 1, 0, 3).reshape(16, 128, 4, 2048))
    maps = []
    for c in range(NCORES):
        sl = slice(c * TPC, (c + 1) * TPC)
        maps.append({
            "xT": _fm(xT[:, sl]), "oT": _fm(oT_all[:, sl]),
            "wout": wout_l, "wup": wup_l, "wdn": wdn_l,
            "g_mlp": _gvec(np.asarray(mlp_norm_g[0], f32)),
            "g_nxt": _gvec(np.asarray(mix_norm_g[1], f32)),
        })
    r = _run(mlp_h, maps)
    x1T = np.concatenate([r[c]["x1T"].reshape(D, TPC) for c in range(NCORES)], axis=1)
    h1T = np.concatenate([r[c]["h1T"].reshape(D, TPC) for c in range(NCORES)], axis=1)
    h1T_tl = _tiles(h1T)

    dil = _prog("dil", build_dil)
    w_in1 = np.asarray(dil_w_in[0], f32)
    G, HD = 3, 8
    maps = []
    ki = np.arange(128)[:, None]
    qi = np.arange(128)[None, :]
    for c in range(NCORES):
        cols = []
        for which in range(2):
            for g in range(G):
                base = which * G * HD * 128 + (g * HD + c) * 128
                cols += list(range(base, base + 128))
        vcols = list(range(2 * G * HD * 128 + c * 256, 2 * G * HD * 128 + (c + 1) * 256))
        bmat = np.empty((128, 3, 256), f32)
        for g in range(G):
            slope = np.float32(2.0) ** (np.float32(-8.0) * np.float32(g * HD + c + 1) / np.float32(G * HD))
            rr = DIL_R[g]
            dprev = (qi + 128 - ki).astype(f32)
            dcur = (qi - ki).astype(f32)
            bmat[:, g, 0:128] = np.where(qi <= ki, -slope * dprev * rr, NEG)
            bmat[:, g, 128:256] = np.where(qi >= ki, -slope * dcur * rr, NEG)
        maps.append({
            "hT": h1T_tl,
            "wqk": _wcols(w_in1, cols),
            "wv": _wcols(w_in1, vcols).reshape(128, 2, 2048),
            "qg": np.ascontiguousarray(np.asarray(dil_q_gain[0], f32).T),
            "kg": np.ascontiguousarray(np.asarray(dil_k_gain[0], f32).T),
            "bmat": bmat,
        })
    r = _run(dil, maps)
    o1T_all = np.concatenate([r[c]["oT"].reshape(256, NTOK) for c in range(NCORES)], axis=0)

    mlp_l = _prog("mlp_l", build_mlp, False)
    wout_l = _wchunks(np.asarray(dil_w_out[0], f32), 4, 4)
    wup_l = _wchunks(np.asarray(mlp_w_up[1], f32), 16, 4)
    wdn = np.asarray(mlp_w_down[1], f32)
    wdn_l = np.ascontiguousarray(wdn.reshape(64, 128, 16, 128).transpose(2, 1, 0, 3).reshape(16, 128, 4, 2048))
    maps = []
    for c in range(NCORES):
        sl = slice(c * TPC, (c + 1) * TPC)
        maps.append({
            "xT": _fm(x1T[:, sl]), "oT": _fm(o1T_all[:, sl]),
            "wout": wout_l, "wup": wup_l, "wdn": wdn_l,
            "g_mlp": _gvec(np.asarray(mlp_norm_g[1], f32)),
        })
    r = _run(mlp_l, maps)
    x2T = np.concatenate([r[c]["x1T"].reshape(D, TPC) for c in range(NCORES)], axis=1)
    return np.ascontiguousarray(x2T.T).reshape(2, S_LEN, D).astype(f32)
```
